# Optimizing a Trainium2 kernel written in Bass

```python
import jax, jax.numpy as jnp
from jax import lax
import numpy as np

D_MODEL = 1024
BATCH = 16
SEQ = 256
DEPTH = 1
DEC_BATCH = 2
DEC_SEQ = 4096
PAST_LEN = 512

GRID_W = 64
ML_HEADS = 4
ML_DIM = 1024
ML_HEAD_DIM = ML_DIM // ML_HEADS
SC_DIM = 1024
CONV_W = 3
D_FF = 2816
CHUNK = 128
N_MOD = 9
N_DIR = 2
EPS = 1e-6

Q_OFF = 0
K_OFF = Q_OFF + ML_DIM
V_OFF = K_OFF + ML_DIM
O_OFF = V_OFF + ML_DIM
IG_OFF = O_OFF + ML_DIM
FG_OFF = IG_OFF + N_DIR * ML_HEADS
B_OFF = FG_OFF + N_DIR * ML_HEADS
C_OFF = B_OFF + SC_DIM
X_OFF = C_OFF + SC_DIM
GML_OFF = X_OFF + SC_DIM
GSC_OFF = GML_OFF + D_MODEL
IN_COLS = GSC_OFF + D_MODEL

kernel_name = "hybrid_mlstm_shortconv_diffusion_step"


def rmsnorm(x, g):
    x32 = x.astype(jnp.float32)
    y = x32 * lax.rsqrt(jnp.mean(x32 * x32, axis=-1, keepdims=True) + EPS)
    return (y * g.astype(jnp.float32)).astype(x.dtype)


def modulate(h, shift, scale):
    return h * (1.0 + scale[:, None, :]) + shift[:, None, :]


def swiglu(h, w1, w2):
    a, b = jnp.split(h @ w1, 2, axis=-1)
    return (jax.nn.silu(a) * b) @ w2


def conv3_centred(u, w, b):
    up = jnp.pad(u, [(0, 0)] * (u.ndim - 2) + [(1, 1), (0, 0)])
    return up[..., :-2, :] * w[0] + up[..., 1:-1, :] * w[1] + up[..., 2:, :] * w[2] + b


def mlstm_chunkwise(q, k, v, i_pre, f_pre, C0, n0, m0):
    f32 = jnp.float32
    bsz, nh, t_len, dh = q.shape
    nc = t_len // CHUNK

    def chunks(t):
        t = t.astype(f32).reshape((bsz, nh, nc, CHUNK) + t.shape[3:])
        return jnp.moveaxis(t, 2, 0)

    xs = (chunks(q), chunks(k), chunks(v), chunks(i_pre),
          chunks(jax.nn.log_sigmoid(f_pre.astype(f32))))
    causal = jnp.tril(jnp.ones((CHUNK, CHUNK), dtype=bool))

    def step(carry, inp):
        C, n, m = carry
        qc, kc, vc, ic, lfc = inp
        b = jnp.cumsum(lfc, axis=-1)
        a = b + m[..., None]
        dmat = jnp.where(causal, b[..., :, None] - b[..., None, :] + ic[..., None, :], -jnp.inf)
        m_t = jnp.maximum(a, jnp.max(dmat, axis=-1))
        w_intra = jnp.exp(dmat - m_t[..., None])
        w_inter = jnp.exp(a - m_t)
        s = jnp.einsum('bhtd,bhsd->bhts', qc, kc) * w_intra
        num = (jnp.einsum('bhts,bhse->bhte', s, vc)
               + w_inter[..., None] * jnp.einsum('bhtd,bhde->bhte', qc, C))
        den = jnp.sum(s, axis=-1) + w_inter * jnp.einsum('bhtd,bhd->bht', qc, n)
        h = num / jnp.maximum(jnp.abs(den), jnp.exp(-m_t))[..., None]
        bl = b[..., -1]
        g = bl[..., None] - b + ic
        m_new = jnp.maximum(bl + m, jnp.max(g, axis=-1))
        decay = jnp.exp(bl + m - m_new)
        wk = jnp.exp(g - m_new[..., None])
        C_new = decay[..., None, None] * C + jnp.einsum('bhs,bhsd,bhse->bhde', wk, kc, vc)
        n_new = decay[..., None] * n + jnp.einsum('bhs,bhsd->bhd', wk, kc)
        return (C_new, n_new, m_new), h

    carry0 = (C0.astype(f32), n0.astype(f32), m0.astype(f32))
    (C, n, m), h = lax.scan(step, carry0, xs)
    h = jnp.moveaxis(h, 0, 2).reshape(bsz, nh, t_len, dh)
    return h, C, n, m


def mlstm_bidir(q, k, v, ig, fg, C0, n0, m0):
    hf, Cf, nf, mf = mlstm_chunkwise(q, k, v, ig[:, 0], fg[:, 0], C0[:, 0], n0[:, 0], m0[:, 0])
    rq, rk, rv = jnp.flip(q, axis=2), jnp.flip(k, axis=2), jnp.flip(v, axis=2)
    hb, Cb, nb, mb = mlstm_chunkwise(rq, rk, rv, jnp.flip(ig[:, 1], axis=-1),
                                     jnp.flip(fg[:, 1], axis=-1), C0[:, 1], n0[:, 1], m0[:, 1])
    h = hf + jnp.flip(hb, axis=2)
    return h, jnp.stack([Cf, Cb], axis=1), jnp.stack([nf, nb], axis=1), jnp.stack([mf, mb], axis=1)


def mixer(h, C0, n0, m0, rows, w_in, b_in, conv_w, conv_b, ml_norm, w_ml_out, w_sc_out, w_o):
    bsz, t_len, _ = h.shape
    z = h @ w_in + b_in

    def heads(t):
        return t.reshape(bsz, t_len, ML_HEADS, ML_HEAD_DIM).transpose(0, 2, 1, 3)

    q = heads(z[..., Q_OFF:K_OFF])
    k = heads(z[..., K_OFF:V_OFF]) * (ML_HEAD_DIM ** -0.5)
    v = heads(z[..., V_OFF:O_OFF])
    o = jax.nn.sigmoid(z[..., O_OFF:IG_OFF])
    ig = z[..., IG_OFF:FG_OFF].reshape(bsz, t_len, N_DIR, ML_HEADS).transpose(0, 2, 3, 1)
    fg = z[..., FG_OFF:B_OFF].reshape(bsz, t_len, N_DIR, ML_HEADS).transpose(0, 2, 3, 1)
    hm, C, n, m = mlstm_bidir(q, k, v, ig, fg, C0, n0, m0)
    hm = hm * lax.rsqrt(jnp.mean(hm * hm, axis=-1, keepdims=True) + EPS)
    hm = hm.transpose(0, 2, 1, 3).reshape(bsz, t_len, ML_DIM) * ml_norm
    ml = (o * hm) @ w_ml_out

    bg = z[..., B_OFF:C_OFF]
    cg = z[..., C_OFF:X_OFF]
    xs = z[..., X_OFF:GML_OFF]
    u = cg * xs
    if rows is None:
        uc = conv3_centred(u, conv_w, conv_b)
    else:
        uc = conv3_centred(u.reshape(bsz, rows, GRID_W, SC_DIM), conv_w, conv_b)
        uc = uc.reshape(bsz, t_len, SC_DIM)
    sc = (bg * uc) @ w_sc_out

    y = jax.nn.sigmoid(z[..., GML_OFF:GSC_OFF]) * ml + jax.nn.sigmoid(z[..., GSC_OFF:IN_COLS]) * sc
    return y @ w_o, C, n, m


def trunk_layer(x, cond, C0, n0, m0, rows, w_mod, b_mod, norm_g, ffn1_w1, ffn1_w2, w_in, b_in,
                conv_w, conv_b, ml_norm, w_ml_out, w_sc_out, w_o, ffn2_w1, ffn2_w2):
    mod = jax.nn.silu(cond) @ w_mod + b_mod
    sh1, sc1, g1, shm, scm, gm, sh2, sc2, g2 = jnp.split(mod, N_MOD, axis=-1)
    h = modulate(rmsnorm(x, norm_g[0]), sh1, sc1)
    x = x + 0.5 * g1[:, None, :] * swiglu(h, ffn1_w1, ffn1_w2)
    h = modulate(rmsnorm(x, norm_g[1]), shm, scm)
    y, C, n, m = mixer(h, C0, n0, m0, rows, w_in, b_in, conv_w, conv_b, ml_norm,
                       w_ml_out, w_sc_out, w_o)
    x = x + gm[:, None, :] * y
    h = modulate(rmsnorm(x, norm_g[2]), sh2, sc2)
    x = x + 0.5 * g2[:, None, :] * swiglu(h, ffn2_w1, ffn2_w2)
    return x, C, n, m


def setup_inputs(seed: int = 0) -> dict:
    key = jax.random.key(seed)
    ks = jax.random.split(key, 24)
    f32 = jnp.float32
    D = D_MODEL

    def nrm(k, shape, s):
        return jax.random.normal(k, shape, f32) * s

    fbias = jnp.tile(jnp.linspace(3.0, 6.0, ML_HEADS, dtype=f32), N_DIR)
    return {
        'x_prompt': nrm(ks[0], (BATCH, SEQ, D), 1.0),
        'x_sample': nrm(ks[1], (DEC_BATCH, DEC_SEQ, D), 1.0),
        'state_C': nrm(ks[2], (DEC_BATCH, DEPTH, N_DIR, ML_HEADS, ML_HEAD_DIM, ML_HEAD_DIM), 0.5),
        'state_n': nrm(ks[3], (DEC_BATCH, DEPTH, N_DIR, ML_HEADS, ML_HEAD_DIM), 0.5),
        'state_m': nrm(ks[4], (DEC_BATCH, DEPTH, N_DIR, ML_HEADS), 1.0),
        'c': nrm(ks[5], (DEC_BATCH, D), 1.0),
        'c_ctx': nrm(ks[6], (D,), 1.0),
        'w_mod': nrm(ks[7], (DEPTH, D, N_MOD * D), 0.5 * D ** -0.5),
        'b_mod': nrm(ks[8], (DEPTH, N_MOD * D), 0.02),
        'norm_g': 1.0 + nrm(ks[9], (DEPTH, 3, D), 0.02),
        'ffn1_w1': nrm(ks[10], (DEPTH, D, 2 * D_FF), D ** -0.5),
        'ffn1_w2': nrm(ks[11], (DEPTH, D_FF, D), D_FF ** -0.5),
        'w_in': nrm(ks[12], (DEPTH, D, IN_COLS), D ** -0.5),
        'b_in': nrm(ks[13], (DEPTH, IN_COLS), 0.02).at[:, FG_OFF:B_OFF].add(fbias),
        'conv_w': nrm(ks[14], (DEPTH, CONV_W, SC_DIM), CONV_W ** -0.5),
        'conv_b': nrm(ks[15], (DEPTH, SC_DIM), 0.02),
        'ml_norm': 1.0 + nrm(ks[16], (DEPTH, ML_DIM), 0.02),
        'w_ml_out': nrm(ks[17], (DEPTH, ML_DIM, D), ML_DIM ** -0.5),
        'w_sc_out': nrm(ks[18], (DEPTH, SC_DIM, D), SC_DIM ** -0.5),
        'w_o': nrm(ks[19], (DEPTH, D, D), D ** -0.5),
        'ffn2_w1': nrm(ks[20], (DEPTH, D, 2 * D_FF), D ** -0.5),
        'ffn2_w2': nrm(ks[21], (DEPTH, D_FF, D), D_FF ** -0.5),
        'final_norm': 1.0 + nrm(ks[22], (D,), 0.02),
    }


def reference(x_prompt, x_sample, state_C, state_n, state_m, c, c_ctx, w_mod, b_mod, norm_g,
              ffn1_w1, ffn1_w2, w_in, b_in, conv_w, conv_b, ml_norm, w_ml_out, w_sc_out, w_o,
              ffn2_w1, ffn2_w2, final_norm):
    f32 = jnp.float32
    bp = x_prompt.shape[0]
    rows = x_sample.shape[1] // GRID_W
    zC = jnp.zeros((bp, N_DIR, ML_HEADS, ML_HEAD_DIM, ML_HEAD_DIM), f32)
    zn = jnp.zeros((bp, N_DIR, ML_HEADS, ML_HEAD_DIM), f32)
    zm = jnp.zeros((bp, N_DIR, ML_HEADS), f32)
    xp, xs = x_prompt, x_sample
    Cs, ns, ms = [], [], []
    for l in range(DEPTH):
        lw = (w_mod[l], b_mod[l], norm_g[l], ffn1_w1[l], ffn1_w2[l], w_in[l], b_in[l],
              conv_w[l], conv_b[l], ml_norm[l], w_ml_out[l], w_sc_out[l], w_o[l],
              ffn2_w1[l], ffn2_w2[l])
        xp, Cl, nl, ml_ = trunk_layer(xp, c_ctx[None, :], zC, zn, zm, None, *lw)
        Cs.append(Cl)
        ns.append(nl)
        ms.append(ml_)
        xs, _, _, _ = trunk_layer(xs, c, state_C[:, l], state_n[:, l], state_m[:, l], rows, *lw)
    y_prompt = rmsnorm(xp, final_norm)
    y_sample = rmsnorm(xs, final_norm)
    new_state_C = jnp.stack(Cs, axis=1)
    new_state_n = jnp.stack(ns, axis=1)
    new_state_m = jnp.stack(ms, axis=1)
    return (y_prompt, y_sample, new_state_C, new_state_n, new_state_m)
```

```python
import os
import numpy as np
from contextlib import ExitStack
import concourse.bass as bass
import concourse.mybir as mybir
from concourse.bass_utils import run_bass_kernel_spmd

F32 = mybir.dt.float32
BF16 = mybir.dt.bfloat16
AF = mybir.ActivationFunctionType
ALU = mybir.AluOpType
AX = mybir.AxisListType

ENGS = ['pe', 'act', 'dve', 'pool', 'sp']
SAME_ENGINE_SYNC = {'pe': False, 'act': True, 'dve': True, 'pool': True, 'sp': False}
STRICT_SAME_ENGINE = True


class Sched:
    def __init__(self, nc, stack):
        self.nc = nc
        self.stack = stack
        self.q = {e: [] for e in ENGS}
        self.cnt = {e: 0 for e in ENGS}
        self.seen = {e: {} for e in ENGS}
        self.lastw = {}
        self.readers = {}
        self.dmacnt = {}
        self.sems = {}
        self.enabled = True
        for e in ENGS:
            self._sem('e_' + e)

    def _sem(self, name):
        if name not in self.sems:
            self.sems[name] = self.stack.enter_context(self.nc.semaphore(name))
        return self.sems[name]

    def _waits(self, eng, reads, writes):
        waits = {}
        own = 'e_' + eng

        def need(tok, raw):
            s, v = tok
            if s == own and not raw and not STRICT_SAME_ENGINE:
                return
            if v > waits.get(s, 0):
                waits[s] = v
        for k in reads:
            if k in self.lastw:
                need(self.lastw[k], True)
            if isinstance(k, tuple) and k[0] == 'ps':
                for r in self.readers.get(k, ()):
                    if r[0] != own:
                        need(r, True)
        for k in writes:
            if k in self.lastw:
                need(self.lastw[k], False)
            for r in self.readers.get(k, ()):
                need(r, False)
        final = []
        for s, v in waits.items():
            if s == own and not SAME_ENGINE_SYNC[eng]:
                continue
            if self.seen[eng].get(s, 0) >= v:
                continue
            self.seen[eng][s] = v
            final.append((s, v))
        return final

    def _record(self, tok, reads, writes):
        for k in reads:
            self.readers.setdefault(k, []).append(tok)
        for k in writes:
            self.lastw[k] = tok
            self.readers[k] = []

    def op(self, eng, fn, reads=(), writes=()):
        if not self.enabled:
            return ('none', 0)
        reads = list(reads)
        writes = list(writes)
        final = self._waits(eng, reads, writes)
        self.cnt[eng] += 1
        tok = ('e_' + eng, self.cnt[eng])
        self.q[eng].append((final, fn, 'e_' + eng, 1))
        self._record(tok, reads, writes)
        return tok

    def dma(self, eng, out, in_, sem, reads=(), writes=(), **kw):
        if not self.enabled:
            return ('none', 0)
        reads = list(reads)
        writes = list(writes)
        self._sem(sem)
        final = self._waits(eng, reads, writes)
        self.dmacnt[sem] = self.dmacnt.get(sem, 0) + 16
        tok = (sem, self.dmacnt[sem])
        self.q[eng].append((final, lambda e: e.dma_start(out=out, in_=in_, **kw), sem, 16))
        self._record(tok, reads, writes)
        return tok

    def dma_fn(self, eng, fn, sem, reads=(), writes=()):
        if not self.enabled:
            return ('none', 0)
        reads = list(reads)
        writes = list(writes)
        self._sem(sem)
        final = self._waits(eng, reads, writes)
        self.dmacnt[sem] = self.dmacnt.get(sem, 0) + 16
        tok = (sem, self.dmacnt[sem])
        self.q[eng].append((final, fn, sem, 16))
        self._record(tok, reads, writes)
        return tok

    def wait_all(self, eng, toks):
        final = []
        if not self.enabled:
            return
        mx = {}
        for s, v in toks:
            if s != 'none' and v > mx.get(s, 0):
                mx[s] = v
        for s, v in mx.items():
            if self.seen[eng].get(s, 0) >= v:
                continue
            self.seen[eng][s] = v
            final.append((s, v))
        self.q[eng].append((final, None, None, 0))

    def barrier(self, engs=('pe', 'act', 'dve', 'sp')):
        if not self.enabled:
            return
        snap = [('e_' + e, self.cnt[e]) for e in ENGS if self.cnt[e] > 0] + list(self.dmacnt.items())
        for e in engs:
            self.wait_all(e, [t for t in snap if t[0] != 'e_' + e])

    def replay(self):
        nc = self.nc
        sems = self.sems

        def run(name, e):
            for waits, fn, incsem, incval in self.q[name]:
                for s, v in waits:
                    e.wait_ge(sems[s], v)
                if fn is None:
                    continue
                ins = fn(e)
                ins.then_inc(sems[incsem], incval)

        with nc.Block() as block:
            @block.tensor
            def _(e):
                run('pe', e)

            @block.scalar
            def _(e):
                run('act', e)

            @block.vector
            def _(e):
                run('dve', e)

            @block.gpsimd
            def _(e):
                run('pool', e)

            @block.sync
            def _(e):
                run('sp', e)


D = 1024
NT = 1536
NG = 3
NTILE = 12
DFF = 2816
NJ = 22
Q_OFF, K_OFF, V_OFF, O_OFF, IG_OFF = 0, 1024, 2048, 3072, 4096
B_OFF = 4112
C_OFF = B_OFF + 1024
X_OFF = C_OFF + 1024
GML_OFF = X_OFF + 1024
GSC_OFF = GML_OFF + 1024
V_BMOD, V_NG, V_BIN, V_CW, V_CB, V_MLN, V_BK16 = 0, 72, 96, 168, 192, 200, 208
NV = 216
RING = 3
RING_ELEMS = 4096
ARENA_F32 = 19600
NFG = 3
BIGNEG = -30000.0
DEN_FAST = False
NRT = 136

DEBUG = {}


def _bin_col(off):
    return V_BIN + (off // 128 if off < 4096 else 32 + (off - 4112) // 128)


class Arena:
    def __init__(self, h32):
        self.h32 = h32
        self.h16 = h32.bitcast(BF16)
        self.cap = h32.shape[1] * 4
        self.off = 0

    def reset(self):
        self.off = 0

    def alloc(self, free_shape, dt, parts=128):
        n = 1
        for v in free_shape:
            n *= v
        size = 4 if dt == F32 else 2
        nbytes = (n * size + 31) // 32 * 32
        assert self.off + nbytes <= self.cap, ("arena overflow", self.off, nbytes, self.cap)
        base = self.off // size
        h = self.h32 if dt == F32 else self.h16
        ap = h[0:parts, base:base + n]
        self.off += nbytes
        if len(free_shape) > 1:
            names = "abcd"[:len(free_shape)]
            ap = ap.rearrange("p (%s) -> p %s" % (" ".join(names), " ".join(names)),
                              **{names[i]: free_shape[i] for i in range(len(free_shape))})
        return ap


def build_program(dbg=None):
    dbg = dbg or []
    LVL = DEBUG.get('stop', float(os.environ.get('KLVL', '99')))
    nc = bass.Bass("TRN2", target_bir_lowering=False)
    din = lambda name, shape: nc.dram_tensor(name, shape, F32, kind="ExternalInput").ap()
    dout = lambda name, shape: nc.dram_tensor(name, shape, F32, kind="ExternalOutput").ap()
    xin = din("xin", [NT, D])
    xfor = din("xfor", [NFG * 1024, D])
    condT_d = din("condT", [128, 16])
    vecs_d = din("vecs", [128, NV])
    ident_d = din("ident", [128, 128])
    ntriF_d = din("ntriF", [128, 128])
    ntriB_d = din("ntriB", [128, 128])
    fnbc_d = din("fnbc", [128, D])
    gb12_d = din("gb12", [128, 192])
    rowtab_d = din("rowtab", [1, NRT])
    bkrow_d = din("bkrow", [128, 1024])
    bvrow_d = din("bvrow", [128, 1024])
    c0ext_d = din("c0ext", [8, 2, 128, 257])
    w_mod = din("w_mod", [D, 9 * D])
    f1w1 = din("ffn1_w1", [D, 2 * DFF])
    f1w2 = din("ffn1_w2", [DFF, D])
    f2w1 = din("ffn2_w1", [D, 2 * DFF])
    f2w2 = din("ffn2_w2", [DFF, D])
    w_in = din("w_in", [D, 9232])
    w_ml = din("w_ml_out", [D, D])
    w_sc = din("w_sc_out", [D, D])
    w_o = din("w_o", [D, D])
    yout = dout("yout", [NT, D])
    outC = dout("outC", [2, 2, 4, 256, 256])
    outN = dout("outN", [2, 2, 4, 256])
    outM = dout("outM", [1, 16])
    aggC = nc.dram_tensor("aggC", [8, NFG, 2, 128, 257], F32).ap()
    cinS = nc.dram_tensor("cinS", [8, 2, 128, 257], F32).ap()
    dbg_out = {}
    for nm in dbg:
        if nm in ('x1', 'hmix', 'x2', 'x3', 'ohm', 'ymix'):
            dbg_out[nm] = dout("dbg_" + nm, [128, 8 * NT])
        elif nm == 'mod':
            dbg_out[nm] = dout("dbg_mod", [128, 144])
        elif nm == 'gates':
            dbg_out[nm] = dout("dbg_gates", [128, 2 * 96 + 96 + 192])
        elif nm == 'rows':
            dbg_out[nm] = dout("dbg_rows", [1, 6 * 96 + 8 + NFG * 16])
        elif nm == 'cin':
            dbg_out[nm] = dout("dbg_cin", [8, 2, 128, 257])
        elif nm == 'hmraw':
            dbg_out[nm] = dout("dbg_hmraw", [4, 128, 12 * 256])

    with ExitStack() as st:
        S = Sched(nc, st)
        T = lambda name, shape, dt: st.enter_context(nc.sbuf_tensor("sb_" + name, shape, dt))
        PS = [st.enter_context(nc.psum_tensor("ps%d" % i, [128, 512], F32)) for i in range(8)]
        PSB = [p.bitcast(BF16) for p in PS]
        state = {'ps': 0, 'ring': 0, 'ev': 0}

        def psum():
            b = state['ps']
            state['ps'] = (b + 1) % 8
            return b

        pools = {'S': [0, 1], 'A': [2, 3, 4], 'B': [5, 6, 7]}
        pidx = {'S': 0, 'A': 0, 'B': 0}

        def pool_bank(name):
            b = pools[name][pidx[name] % len(pools[name])]
            pidx[name] += 1
            return b

        def evq():
            state['ev'] ^= 1
            return 'act' if state['ev'] else 'dve'

        x = T("x", [128, 8, NT], F32)
        hT = T("hT", [128, 8, NT], BF16)
        arena_t = T("arena", [128, ARENA_F32], F32)
        AR = Arena(arena_t)
        wr = [T("wr%d" % i, [128, RING_ELEMS], BF16) for i in range(RING)]
        vecs = T("vecs", [128, NV], F32)
        ident = T("ident", [128, 128], F32)
        identb = T("identb", [128, 128], BF16)
        onesb = T("onesb", [128, 128], BF16)
        onesf = T("onesf", [1, 128], F32)
        ntriF = T("ntriF", [128, 128], F32)
        ntriB = T("ntriB", [128, 128], F32)
        maskF = T("maskF", [128, 128], BF16)
        maskB = T("maskB", [128, 128], BF16)
        gb12 = T("gb12", [128, 192], F32)
        condT = T("condT", [128, 16], F32)
        scond = T("scond", [128, 16], BF16)
        modT = T("modT", [128, 72, 2], F32)
        tabs = T("tabs", [128, 9, 8, 2], F32)
        xstage = [T("xstage%d" % i, [128, D], F32) for i in range(2)]
        sqb = T("sqb", [128, 2, 512], BF16)
        rstd = T("rstd", [128, 512], F32)
        rstd2 = T("rstd2", [128, 512], F32)
        tmpf = [T("tmpf%d" % i, [128, 512], F32) for i in range(2)]
        small = T("small", [128, 64], F32)
        wg = T("wg", [128, 8, 16], BF16)
        GT = T("GT", [128, 192], F32)
        SPt = T("SPt", [128, 96], F32)
        RBt = T("RBt", [128, 2, 96], F32)
        ROW = T("ROW", [96, 2, 128], F32)
        ROW2 = T("ROW2", [96, 2, 128], F32)
        COLS = T("COLS", [96, 4], F32)
        P0 = T("P0", [1, 2, 96], F32)
        MMr = T("MMr", [1, 96], F32)
        RRr = T("RRr", [1, 96], F32)
        DECr = T("DECr", [1, 96], F32)
        MFIN = T("MFIN", [1, 3, 8], F32)
        WT = T("WT", [128, 2, 96], F32)
        DB = T("DB", [128, 96], F32)
        AGS = T("AGS", [1, NFG, 2, 8], F32)
        rowtab = T("rowtab", [1, NRT], F32)
        CROW = T("CROW", [1, 256], F32)
        CB = T("CB", [128, 32], F32)
        MROW = T("MROW", [1, 16], F32)

        S.dma('sp', vecs[:], vecs_d, 'd_c0', writes=['vecs'])
        S.dma('sp', ident[:], ident_d, 'd_c1', writes=['ident'])
        S.dma('sp', condT[:], condT_d, 'd_c2', writes=['condT'])
        S.dma('sp', ntriF[:], ntriF_d, 'd_c4', writes=['ntriF'])
        S.dma('sp', ntriB[:], ntriB_d, 'd_c5', writes=['ntriB'])
        S.dma('sp', gb12[:], gb12_d, 'd_c6', writes=['gb12'])
        S.dma('sp', rowtab[:], rowtab_d, 'd_c7', writes=['rowtab'])
        S.dma('sp', tmpf[0][:, 0:128].rearrange("p (k n) -> p k n", k=8), w_in.rearrange("(kc p) n -> p kc n", p=128)[:, :, IG_OFF:IG_OFF + 16], 'd_c8', writes=[('tmpf', 0)])
        S.op('dve', lambda e: e.tensor_copy(out=wg[:], in_=tmpf[0][:, 0:128].rearrange("p (k n) -> p k n", k=8)), reads=[('tmpf', 0)], writes=['wg'])
        S.op('dve', lambda e: e.tensor_copy(out=identb[:], in_=ident[:]), reads=['ident'], writes=['identb'])
        S.op('dve', lambda e: e.memset(onesb[:], 1.0), writes=['onesb'])
        S.op('dve', lambda e: e.memset(onesf[:], 1.0), writes=['onesf'])
        S.op('dve', lambda e: e.tensor_scalar(out=maskF[:], in0=ntriF[:], scalar1=-1.0, scalar2=None, op0=ALU.mult), reads=['ntriF'], writes=['maskF'])
        S.op('dve', lambda e: e.tensor_scalar(out=maskB[:], in0=ntriB[:], scalar1=-1.0, scalar2=None, op0=ALU.mult), reads=['ntriB'], writes=['maskB'])
        S.op('dve', lambda e: e.tensor_scalar(out=vecs[:, V_BK16:V_BK16 + 8], in0=vecs[:, V_BIN + 8:V_BIN + 16], scalar1=1.0 / 16, scalar2=None, op0=ALU.mult),
             reads=['vecs'], writes=['vecs'])
        S.op('act', lambda e: e.activation(out=scond[:], in_=condT[:], func=AF.Silu), reads=['condT'], writes=['scond'])

        def load_panel(srcs, shape_str, **dims):
            if not isinstance(srcs, (list, tuple)):
                srcs = [srcs]
            s = state['ring']
            state['ring'] = (s + 1) % RING
            n = 1
            for d_ in srcs[0].shape[1:]:
                n *= d_
            assert n * len(srcs) <= RING_ELEMS
            inner = srcs[0].shape[-1]
            for i, src in enumerate(srcs):
                dst = wr[s][:, i * n:(i + 1) * n].rearrange("p (a n) -> p a n", n=inner)
                wk = [('wr', s, i)] if len(srcs) == 2 else [('wr', s, 0), ('wr', s, 1)]
                S.dma('pool', dst, src, 'd_w%d' % s, writes=wk)
            if len(srcs) == 2:
                S.lastw[('wr', s, 0)] = S.lastw[('wr', s, 1)]
            full = wr[s][:, 0:n * len(srcs)]
            if shape_str is not None:
                full = full.rearrange(shape_str, **dims)
            return s, full

        def WR(s):
            return [('wr', s, 0), ('wr', s, 1)]

        def keys(name, n=8, g=None):
            return [(name, kc, g) for kc in range(n)]

        def load_x(src, ntiles, xbuf, xkey):
            for t in range(ntiles):
                xs_i = t % 2
                S.dma('sp', xstage[xs_i][:], src[t * 128:(t + 1) * 128, :], 'd_x%d' % xs_i, writes=[('xstage', xs_i)])
                g = t // 4
                for half in range(2):
                    b = psum()

                    def f(e, half=half, xs_i=xs_i, b=b):
                        for i in range(4):
                            kc = half * 4 + i
                            ins = e.transpose(PS[b][:, i * 128:(i + 1) * 128], xstage[xs_i][:, kc * 128:(kc + 1) * 128], ident[:])
                        return ins
                    S.op('pe', f, reads=[('xstage', xs_i), 'ident'], writes=[('ps', b)])
                    dst = xbuf[:, half * 4:(half + 1) * 4, t * 128:(t + 1) * 128]
                    src_ps = PS[b][:].rearrange("p (a c) -> p a c", a=4)
                    wk = [(xkey, kc, g) for kc in range(half * 4, half * 4 + 4)]
                    if half == 0:
                        S.op('act', lambda e, dst=dst, src_ps=src_ps: e.activation(out=dst, in_=src_ps, func=AF.Copy), reads=[('ps', b)], writes=wk)
                    else:
                        S.op('dve', lambda e, dst=dst, src_ps=src_ps: e.tensor_copy(out=dst, in_=src_ps), reads=[('ps', b)], writes=wk)


        if LVL >= 1:
            load_x(xfor[0:1024, :], 8, x[:, :, 0:1024], 'x')
        scv = scond[:].rearrange("p (k m) -> p k m", k=8)
        wmv = w_mod.rearrange("(kc p) n -> p kc n", p=128)
        def mod_panels(pis):
          for pi in pis:
            s, wp = load_panel(wmv[:, :, pi * 512:(pi + 1) * 512], "p (k n) -> p k n", k=8)
            b = psum()

            def f(e, wp=wp, b=b):
                for jb in range(4):
                    for kc in range(8):
                        ins = e.matmul(PS[b][:, jb * 2:jb * 2 + 2], wp[:, kc, jb * 128:(jb + 1) * 128], scv[:, kc, :],
                                       start=(kc == 0), stop=(kc == 7))
                return ins
            S.op('pe', f, reads=WR(s) + ['scond'], writes=[('ps', b)])
            for mi in range(2):
                S.op('dve', lambda e, b=b, mi=mi, pi=pi: e.tensor_tensor(
                    out=modT[:, pi * 4:(pi + 1) * 4, mi], in0=PS[b][:, 0:8].rearrange("p (j m) -> p j m", m=2)[:, :, mi],
                    in1=vecs[:, V_BMOD + pi * 4:V_BMOD + (pi + 1) * 4], op=ALU.add),
                    reads=[('ps', b), 'vecs'], writes=[('modT', pi, mi)])
        MODK = [('modT', pi, mi) for pi in range(18) for mi in range(2)]
        mv = modT[:].rearrange("p (c k) m -> p c k m", c=9)

        def derive_tabs(groups, mk):
            for mi in range(2):
                for (ti, ci_scale, ci_shift, ci_gate, ngi, half) in groups:
                    S.op('dve', lambda e, mi=mi, ti=ti, ci_scale=ci_scale, ngi=ngi: e.scalar_tensor_tensor(
                        out=tabs[:, ti, :, mi], in0=mv[:, ci_scale, :, mi], scalar=1.0, in1=vecs[:, V_NG + ngi * 8:V_NG + ngi * 8 + 8],
                        op0=ALU.add, op1=ALU.mult), reads=mk + ['vecs'], writes=[('tabs', ti)])
                    S.op('dve', lambda e, mi=mi, ti=ti, ci_shift=ci_shift: e.tensor_copy(out=tabs[:, ti + 1, :, mi], in_=mv[:, ci_shift, :, mi]),
                         reads=mk, writes=[('tabs', ti + 1)])
                    S.op('dve', lambda e, mi=mi, ti=ti, ci_gate=ci_gate, half=half: e.tensor_scalar(
                        out=tabs[:, ti + 2, :, mi], in0=mv[:, ci_gate, :, mi], scalar1=half, scalar2=None, op0=ALU.mult),
                        reads=mk, writes=[('tabs', ti + 2)])

        def derive_one(ti, kind, chunk, ngi=0, half=1.0):
            mk = [('modT', pi, mi) for pi in (2 * chunk, 2 * chunk + 1) for mi in range(2)]
            for mi in range(2):
                if kind == 'A':
                    S.op('dve', lambda e, mi=mi: e.scalar_tensor_tensor(
                        out=tabs[:, ti, :, mi], in0=mv[:, chunk, :, mi], scalar=1.0, in1=vecs[:, V_NG + ngi * 8:V_NG + ngi * 8 + 8],
                        op0=ALU.add, op1=ALU.mult), reads=mk + ['vecs'], writes=[('tabs', ti)])
                elif kind == 'B':
                    S.op('dve', lambda e, mi=mi: e.tensor_copy(out=tabs[:, ti, :, mi], in_=mv[:, chunk, :, mi]), reads=mk, writes=[('tabs', ti)])
                else:
                    S.op('dve', lambda e, mi=mi: e.tensor_scalar(out=tabs[:, ti, :, mi], in0=mv[:, chunk, :, mi], scalar1=half, scalar2=None, op0=ALU.mult),
                         reads=mk, writes=[('tabs', ti)])
        mod_panels(range(4))
        derive_one(0, 'A', 1, ngi=0)
        derive_one(1, 'B', 0)
        mod_jobs = [
            lambda: mod_panels([4]),
            lambda: (mod_panels([5]), derive_one(2, 'G', 2, half=0.5)),
            lambda: mod_panels([6]),
            lambda: (mod_panels([7]), derive_one(4, 'B', 3)),
            lambda: mod_panels([8]),
            lambda: (mod_panels([9]), derive_one(3, 'A', 4, ngi=1)),
            lambda: mod_panels([10]),
            lambda: (mod_panels([11]), derive_one(5, 'G', 5, half=1.0)),
        ]
        if LVL < 1:
            while mod_jobs:
                mod_jobs.pop(0)()

        def mod_ap(ti, kc, mi):
            return tabs[:, ti, kc, mi:mi + 1]

        own_mi = lambda g: 0 if g == 0 else 1
        for_mi = lambda g: 1

        def norm_mod(xbuf, hbuf, ngroups, tiA, tiB, mi_of, xkey, hkey):
            RS = [rstd, rstd2]
            banks = []
            for g in range(ngroups):
                gs = slice(g * 512, (g + 1) * 512)
                b = psum()
                banks.append(b)
                for kc in range(8):
                    S.op('act', lambda e, kc=kc, gs=gs: e.activation(out=sqb[:, kc % 2, :], in_=xbuf[:, kc, gs], func=AF.Square),
                         reads=[(xkey, kc, g)], writes=[('sqb', kc % 2)])
                    S.op('pe', lambda e, kc=kc, b=b: e.matmul(PS[b][:], onesb[:], sqb[:, kc % 2, :], start=(kc == 0), stop=(kc == 7)),
                         reads=[('sqb', kc % 2), 'onesb'], writes=[('ps', b)])

            def root(g):
                rs = RS[g % 2]
                S.op('act', lambda e, b=banks[g], rs=rs: e.activation(out=rs[:], in_=PS[b][:], func=AF.Sqrt, bias=1e-6, scale=1.0 / D),
                     reads=[('ps', banks[g])], writes=[('rstd', g % 2)])
            for g in range(min(2, ngroups)):
                root(g)
            for g in range(ngroups):
                gs = slice(g * 512, (g + 1) * 512)
                mi = mi_of(g)
                rs = RS[g % 2]
                rk = ('rstd', g % 2)
                S.op('dve', lambda e, rs=rs: e.reciprocal(out=rs[:], in_=rs[:]), reads=[rk], writes=[rk])
                for kc in range(8):
                    ti_ = kc % 2
                    S.op('dve', lambda e, kc=kc, gs=gs, ti_=ti_, mi=mi, rs=rs: e.scalar_tensor_tensor(
                        out=tmpf[ti_][:], in0=xbuf[:, kc, gs], scalar=mod_ap(tiA, kc, mi), in1=rs[:], op0=ALU.mult, op1=ALU.mult),
                        reads=[(xkey, kc, g), ('tabs', tiA), rk], writes=[('tmpf', ti_)])
                    S.op('act', lambda e, kc=kc, gs=gs, ti_=ti_, mi=mi: e.activation(
                        out=hbuf[:, kc, gs], in_=tmpf[ti_][:], func=AF.Identity, bias=mod_ap(tiB, kc, mi), scale=1.0),
                        reads=[('tmpf', ti_), ('tabs', tiB)], writes=[(hkey, kc, g)])
                if g + 2 < ngroups:
                    root(g + 2)

        def ffn(xbuf, hbuf, hid, ngroups, w1, w2, tiHG, mi_of, xkey, hkey, hidkey, hook=None):
            w1v = w1.rearrange("(kc p) n -> p kc n", p=128)
            for jp in range(NJ // 2):
                if hook is not None:
                    hook()
                s, wp = load_panel([w1v[:, :, jp * 256:(jp + 1) * 256], w1v[:, :, DFF + jp * 256:DFF + (jp + 1) * 256]],
                                   "p (two k n) -> p two k n", two=2, k=8)
                for jj in range(2):
                    j = jp * 2 + jj
                    for g in range(ngroups):
                        gs = slice(g * 512, (g + 1) * 512)
                        ba = psum()
                        bb = psum()

                        def f(e, wp=wp, jj=jj, gs=gs, ba=ba, bb=bb):
                            for ab, bk in ((0, ba), (1, bb)):
                                for kc in range(8):
                                    ins = e.matmul(PS[bk][:], wp[:, ab, kc, jj * 128:(jj + 1) * 128], hbuf[:, kc, gs],
                                                   start=(kc == 0), stop=(kc == 7))
                            return ins
                        S.op('pe', f, reads=WR(s) + [(hkey, kc, g) for kc in range(8)], writes=[('ps', ba), ('ps', bb)])
                        ti_ = g % 2
                        S.op('act', lambda e, ba=ba, ti_=ti_: e.activation(out=tmpf[ti_][:], in_=PS[ba][:], func=AF.Silu),
                             reads=[('ps', ba)], writes=[('tmpf', ti_)])
                        S.op('dve', lambda e, bb=bb, ti_=ti_, j=j, gs=gs: e.tensor_tensor(
                            out=hid[:, j, gs], in0=tmpf[ti_][:], in1=PS[bb][:], op=ALU.mult),
                            reads=[('tmpf', ti_), ('ps', bb)], writes=[(hidkey, j, g)])
            w2v = w2.rearrange("(kc p) n -> p kc n", p=128)
            for cb in range(8):
                if hook is not None:
                    hook()
                s, wp = load_panel(w2v[:, :, cb * 128:(cb + 1) * 128], "p (k n) -> p k n", k=NJ)
                for g in range(ngroups):
                    gs = slice(g * 512, (g + 1) * 512)
                    mi = mi_of(g)
                    b = psum()

                    def f(e, wp=wp, gs=gs, b=b):
                        for j in range(NJ):
                            ins = e.matmul(PS[b][:], wp[:, j, :], hid[:, j, gs], start=(j == 0), stop=(j == NJ - 1))
                        return ins
                    S.op('pe', f, reads=WR(s) + [(hidkey, j, g) for j in range(NJ)], writes=[('ps', b)])
                    S.op('dve', lambda e, b=b, cb=cb, gs=gs, mi=mi: e.scalar_tensor_tensor(
                        out=xbuf[:, cb, gs], in0=PS[b][:], scalar=mod_ap(tiHG, cb, mi), in1=xbuf[:, cb, gs], op0=ALU.mult, op1=ALU.add),
                        reads=[('ps', b), ('tabs', tiHG), (xkey, cb, g)], writes=[(xkey, cb, g)])

        def proj_fm(s, wsl, nblk, rhs, rkey, nk, ngroups, evac):
            for blk in range(nblk):
                for g in range(ngroups):
                    gs = slice(g * 512, (g + 1) * 512)
                    b = psum()

                    def f(e, blk=blk, gs=gs, b=b):
                        for kc in range(nk):
                            ins = e.matmul(PS[b][:], wsl(kc, blk), rhs[:, kc, gs], start=(kc == 0), stop=(kc == nk - 1))
                        return ins
                    S.op('pe', f, reads=WR(s) + [(rkey, kc, g) for kc in range(nk)], writes=[('ps', b)])
                    evac(blk, g, gs, b)

        w_inv = w_in.rearrange("(kc p) n -> p kc n", p=128)

        def gates_front(hbuf, hkey, nt):
            n8 = nt * 8
            b = psum()

            def f(e, b=b):
                for t in range(nt):
                    for kc in range(8):
                        ins = e.matmul(PS[b][:, t * 16:(t + 1) * 16], hbuf[:, kc, t * 128:(t + 1) * 128], wg[:, kc, :],
                                       start=(kc == 0), stop=(kc == 7))
                return ins
            S.op('pe', f, reads=[(hkey, kc, g) for kc in range(8) for g in range((nt + 3) // 4)] + ['wg'], writes=[('ps', b)])
            S.op('dve', lambda e, b=b: e.tensor_tensor(out=GT[:, 0:nt * 16], in0=PS[b][:, 0:nt * 16], in1=gb12[:, 0:nt * 16], op=ALU.add),
                 reads=[('ps', b), 'gb12'], writes=['GT'])
            GTv = GT[:, 0:nt * 16].rearrange("p (t c) -> p t c", c=16)
            SPv = SPt[:, 0:n8].rearrange("p (t c) -> p t c", c=8)
            S.op('act', lambda e: e.activation(out=SPv, in_=GTv[:, :, 8:16], func=AF.Exp, scale=-1.0), reads=['GT'], writes=['SPt'])
            S.op('act', lambda e: e.activation(out=SPt[:, 0:n8], in_=SPt[:, 0:n8], func=AF.Ln, bias=1.0, scale=1.0), reads=['SPt'], writes=['SPt'])
            b2 = psum()

            def f2(e, b2=b2):
                for t in range(nt):
                    ins = e.matmul(PS[b2][:, t * 8:t * 8 + 4], ntriF[:], SPv[:, t, 0:4], start=True, stop=True)
                    ins = e.matmul(PS[b2][:, t * 8 + 4:t * 8 + 8], ntriB[:], SPv[:, t, 4:8], start=True, stop=True)
                return ins
            S.op('pe', f2, reads=['SPt', 'ntriF', 'ntriB'], writes=[('ps', b2)])
            S.op('dve', lambda e, b2=b2: e.tensor_tensor(out=RBt[:, 0, 0:n8].rearrange("p (t c) -> p t c", c=8), in0=GTv[:, :, 0:8],
                                                        in1=PS[b2][:, 0:n8].rearrange("p (t c) -> p t c", c=8), op=ALU.subtract),
                 reads=['GT', ('ps', b2)], writes=['RBt'])
            S.op('act', lambda e, b2=b2: e.activation(out=RBt[:, 1, 0:n8], in_=PS[b2][:, 0:n8], func=AF.Copy), reads=[('ps', b2)], writes=['RBt1'])
            b3 = psum()

            def f3(e, b3=b3):
                e.transpose(PS[b3][0:n8, 0:128], RBt[:, 0, 0:n8], ident[:])
                return e.transpose(PS[b3][0:n8, 128:256], RBt[:, 1, 0:n8], ident[:])
            S.op('pe', f3, reads=['RBt', 'RBt1', 'ident'], writes=[('ps', b3)])
            S.op('dve', lambda e, b3=b3: e.tensor_copy(out=ROW[0:n8].rearrange("p a c -> p (a c)"), in_=PS[b3][0:n8, 0:256]), reads=[('ps', b3)], writes=['ROW'])
            S.op('dve', lambda e: e.tensor_reduce(out=COLS[0:n8, 0:1], in_=ROW[0:n8, 0, :], axis=AX.X, op=ALU.max), reads=['ROW'], writes=['COLS'])
            S.op('dve', lambda e: e.tensor_reduce(out=COLS[0:n8, 1:2], in_=ROW[0:n8, 1, :], axis=AX.X, op=ALU.min), reads=['ROW'], writes=['COLS'])
            b4 = psum()

            def f4(e, b4=b4):
                e.matmul(PS[b4][0:1, 0:n8], COLS[0:n8, 0:1], ident[0:n8, 0:n8], start=True, stop=True)
                return e.matmul(PS[b4][0:1, 96:96 + n8], COLS[0:n8, 1:2], ident[0:n8, 0:n8], start=True, stop=True)
            S.op('pe', f4, reads=['COLS', 'ident'], writes=[('ps', b4)])
            S.op('dve', lambda e, b4=b4: e.tensor_copy(out=P0[:].rearrange("p a c -> p (a c)"), in_=PS[b4][0:1, 0:192]), reads=[('ps', b4)], writes=['P0'])

        RM = P0[0:1, 0, :]
        BL = P0[0:1, 1, :]

        def chain_recur(tiles, j0, init, fin_ap):
            for i, t in enumerate(tiles):
                sl = slice(t * 8 + j0, t * 8 + j0 + 4)
                if i == 0:
                    if isinstance(init, float):
                        S.op('dve', lambda e, sl=sl: e.memset(MMr[0:1, sl], init), writes=['MMr'])
                    else:
                        S.op('dve', lambda e, sl=sl: e.tensor_copy(out=MMr[0:1, sl], in_=init), reads=['MIN'], writes=['MMr'])
                S.op('dve', lambda e, sl=sl: e.tensor_tensor(out=RRr[0:1, sl], in0=MMr[0:1, sl], in1=RM[:, sl], op=ALU.max),
                     reads=['MMr', 'P0'], writes=['RRr'])
                if i + 1 < len(tiles):
                    t2 = tiles[i + 1]
                    dst = MMr[0:1, t2 * 8 + j0:t2 * 8 + j0 + 4]
                else:
                    dst = fin_ap
                S.op('dve', lambda e, sl=sl, dst=dst: e.tensor_tensor(out=dst, in0=RRr[0:1, sl], in1=BL[:, sl], op=ALU.add),
                     reads=['RRr', 'P0'], writes=['MMr', 'MFIN', 'AGS'])

        def gates_finish(nt):
            n8 = nt * 8
            S.op('dve', lambda e: e.tensor_tensor(out=DECr[0:1, 0:n8], in0=MMr[0:1, 0:n8], in1=RRr[0:1, 0:n8], op=ALU.subtract),
                 reads=['MMr', 'RRr'], writes=['DECr'])
            S.op('act', lambda e: e.activation(out=DECr[0:1, 0:n8], in_=DECr[0:1, 0:n8], func=AF.Exp), reads=['DECr'], writes=['DECr'])
            b = psum()
            S.op('pe', lambda e, b=b: e.matmul(PS[b][0:n8, 0:1], RRr[0:1, 0:n8], ident[0:1, 0:1], start=True, stop=True),
                 reads=['RRr', 'ident'], writes=[('ps', b)])
            S.op('dve', lambda e, b=b: e.tensor_scalar(out=COLS[0:n8, 2:3], in0=PS[b][0:n8, 0:1], scalar1=-1.0, scalar2=None, op0=ALU.mult),
                 reads=[('ps', b)], writes=['NEGR'])
            S.op('act', lambda e: e.activation(out=ROW2[0:n8, 0, :], in_=ROW[0:n8, 0, :], func=AF.Exp, bias=COLS[0:n8, 2:3], scale=1.0),
                 reads=['ROW', 'NEGR'], writes=['ROW2'])
            S.op('act', lambda e: e.activation(out=ROW2[0:n8, 1, :], in_=ROW[0:n8, 1, :], func=AF.Exp, bias=COLS[0:n8, 2:3], scale=-1.0),
                 reads=['ROW', 'NEGR'], writes=['ROW2b'])
            b2 = psum()

            def f(e, b2=b2):
                e.transpose(PS[b2][:, 0:n8], ROW2[0:n8, 0, :], ident[0:n8, 0:n8])
                return e.transpose(PS[b2][:, 96:96 + n8], ROW2[0:n8, 1, :], ident[0:n8, 0:n8])
            S.op('pe', f, reads=['ROW2', 'ROW2b', 'ident'], writes=[('ps', b2)])
            S.op('dve', lambda e, b2=b2: e.tensor_copy(out=WT[:].rearrange("p a c -> p (a c)"), in_=PS[b2][:, 0:192]), reads=[('ps', b2)], writes=['WT'])
            b3 = psum()
            S.op('pe', lambda e, b3=b3: e.matmul(PS[b3][:, 0:n8], onesf[0:1, :], DECr[0:1, 0:n8], start=True, stop=True),
                 reads=['onesf', 'DECr'], writes=[('ps', b3)])
            S.op('act', lambda e, b3=b3: e.activation(out=DB[:, 0:n8], in_=PS[b3][:, 0:n8], func=AF.Copy), reads=[('ps', b3)], writes=['DB'])

        def kv_head(h, hbuf, hkey, ngroups, nt, kT, vT, ktok, vext, pfx, alias=()):
            s, wp = load_panel([w_inv[:, :, K_OFF + h * 256:K_OFF + (h + 1) * 256], w_inv[:, :, V_OFF + h * 256:V_OFF + (h + 1) * 256]],
                               "p (two k n) -> p two k n", two=2, k=8)

            def ev_k(blk, g, gs, b):
                S.op('act', lambda e: e.activation(out=kT[:, blk, gs], in_=PS[b][:], func=AF.Identity, bias=vecs[:, V_BK16 + 2 * h + blk:V_BK16 + 2 * h + blk + 1], scale=1.0 / 16),
                     reads=[('ps', b), 'vecs'], writes=[(pfx + 'kT', g)])

            def ev_v(blk, g, gs, b):
                c = _bin_col(V_OFF) + 2 * h + blk
                S.op('dve', lambda e: e.tensor_scalar(out=vT[:, blk, gs], in0=PS[b][:], scalar1=vecs[:, c:c + 1], scalar2=None, op0=ALU.add),
                     reads=[('ps', b), 'vecs'], writes=[(pfx + 'vT', g)] + list(alias))
            proj_fm(s, lambda kc, blk: wp[:, 0, kc, blk * 128:(blk + 1) * 128], 2, hbuf, hkey, 8, ngroups, ev_k)
            proj_fm(s, lambda kc, blk: wp[:, 1, kc, blk * 128:(blk + 1) * 128], 2, hbuf, hkey, 8, ngroups, ev_v)
            for t2 in range(0, nt, 2):
                g = t2 // 4
                bK = psum()
                bV = psum()

                def f(e, bK=bK, bV=bV, t2=t2):
                    for tt in range(2):
                        ts_ = slice((t2 + tt) * 128, (t2 + tt + 1) * 128)
                        for dc in range(2):
                            e.matmul(PS[bK][:, tt * 256 + dc * 128:tt * 256 + (dc + 1) * 128], kT[:, dc, ts_], identb[:], start=True, stop=True)
                    for tt in range(2):
                        ts_ = slice((t2 + tt) * 128, (t2 + tt + 1) * 128)
                        for dc in range(2):
                            ins = e.matmul(PS[bV][:, tt * 256 + dc * 128:tt * 256 + (dc + 1) * 128], vT[:, dc, ts_], identb[:], start=True, stop=True)
                    return ins
                S.op('pe', f, reads=[(pfx + 'kT', g), (pfx + 'vT', g), 'identb'] + list(alias), writes=[('ps', bK), ('ps', bV)])
                S.op('act', lambda e, bK=bK, t2=t2: e.activation(out=ktok[:, t2:t2 + 2, :], in_=PS[bK][:].rearrange("p (a c) -> p a c", a=2), func=AF.Copy),
                     reads=[('ps', bK)], writes=[(pfx + 'ktok', t2), (pfx + 'ktok', t2 + 1)])
                S.op('dve', lambda e, bV=bV, t2=t2: e.tensor_copy(out=vext[:, t2:t2 + 2, 0:256], in_=PS[bV][:].rearrange("p (a c) -> p a c", a=2)),
                     reads=[('ps', bV)], writes=[(pfx + 'vext', t2), (pfx + 'vext', t2 + 1)])

        def chain_step(t, j, h, Cst, ckey, Vp, vpkey, ktok, vext, pfx, outputs=None):
            col = t * 8 + j
            S.op('act', lambda e: e.activation(out=Vp, in_=vext[:, t, :], func=AF.Identity, scale=WT[:, 0, col:col + 1]),
                 reads=[(pfx + 'vext', t), 'WT'], writes=[vpkey])
            if outputs is not None:
                qT, kT, Dbf, dkey, sTm, skey, hm, first, mask = outputs
                ts_ = slice(t * 128, (t + 1) * 128)
                g = t // 4
                S.op('act', lambda e: e.activation(out=Dbf, in_=Cst, func=AF.Identity, scale=DB[:, col:col + 1]), reads=[ckey, 'DB'], writes=[dkey])
                bs = psum()

                def f(e, bs=bs):
                    for dc in range(2):
                        ins = e.matmul(PS[bs][:, 0:128], kT[:, dc, ts_], qT[:, dc, ts_], start=(dc == 0), stop=(dc == 1))
                    return ins
                S.op('pe', f, reads=[(pfx + 'kT', g), (pfx + 'qT', g)], writes=[('ps', bs)])
                S.op('dve', lambda e, bs=bs: e.tensor_tensor(out=sTm, in0=PS[bs][:, 0:128], in1=mask[:], op=ALU.mult),
                     reads=[('ps', bs), 'maskF', 'maskB'], writes=[skey])
                bn = psum()

                def f2(e, bn=bn):
                    e.matmul(PS[bn][:, 0:257], sTm, Vp, start=True, stop=False)
                    e.matmul(PS[bn][:, 0:257], qT[:, 0, ts_], Dbf[:, 0, :], start=False, stop=False)
                    return e.matmul(PS[bn][:, 0:257], qT[:, 1, ts_], Dbf[:, 1, :], start=False, stop=True)
                S.op('pe', f2, reads=[skey, vpkey, dkey, (pfx + 'qT', g)], writes=[('ps', bn)])
                dcol = small[:, 32 + (col % 16):33 + (col % 16)]
                dk = ('den', col % 16)
                S.op('dve', lambda e, bn=bn: e.tensor_scalar(out=dcol, in0=PS[bn][:, 256:257], scalar1=-1.0, scalar2=WT[:, 1, col:col + 1],
                                                          op0=ALU.mult, op1=ALU.max), reads=[('ps', bn), 'WT'], writes=[dk])
                S.op('dve', lambda e, bn=bn: e.tensor_tensor(out=dcol, in0=dcol, in1=PS[bn][:, 256:257], op=ALU.max), reads=[('ps', bn), dk], writes=[dk])
                S.op('dve', lambda e: e.reciprocal(out=dcol, in_=dcol), reads=[dk], writes=[dk])
                if first:
                    S.op('act', lambda e, bn=bn: e.activation(out=hm[:, t, :], in_=PS[bn][:, 0:256], func=AF.Identity, scale=dcol),
                         reads=[('ps', bn), dk], writes=[(pfx + 'hm', t)])
                else:
                    S.op('dve', lambda e, bn=bn: e.scalar_tensor_tensor(out=hm[:, t, :], in0=PS[bn][:, 0:256], scalar=dcol, in1=hm[:, t, :],
                                                                     op0=ALU.mult, op1=ALU.add), reads=[('ps', bn), dk, (pfx + 'hm', t)], writes=[(pfx + 'hm', t)])
            b0 = psum()
            b1 = psum()

            def f3(e, b0=b0, b1=b1):
                e.matmul(PS[b0][:, 0:257], ktok[:, t, 0:128], Vp, start=True, stop=True)
                return e.matmul(PS[b1][:, 0:257], ktok[:, t, 128:256], Vp, start=True, stop=True)
            S.op('pe', f3, reads=[(pfx + 'ktok', t), vpkey], writes=[('ps', b0), ('ps', b1)])
            for dc, bk in ((0, b0), (1, b1)):
                S.op('dve', lambda e, dc=dc, bk=bk: e.scalar_tensor_tensor(out=Cst[:, dc, :], in0=Cst[:, dc, :], scalar=DB[:, col:col + 1], in1=PS[bk][:, 0:257],
                                                                         op0=ALU.mult, op1=ALU.add), reads=[('ps', bk), 'DB', ckey], writes=[ckey])

        S.enabled = LVL >= 1
        AR.reset()
        xf = x[:, :, 0:1024]
        hf = hT[:, :, 0:1024]
        hidf = AR.alloc([NJ, 1024], BF16)
        KTs = [AR.alloc([8, 256], BF16) for _ in range(2)]
        VEs = [AR.alloc([8, 258], BF16)[:, :, 0:257] for _ in range(2)]
        bkvh = AR.alloc([512], F32)
        Cf = [AR.alloc([2, 257], F32) for _ in range(2)]
        Vpf = [AR.alloc([258], BF16)[:, 0:257] for _ in range(16)]
        for p_ in range(2):
            S.op('dve', lambda e, p_=p_: e.memset(VEs[p_][:, :, 256:257], 1.0), writes=[('f_vext', p_, t) for t in range(8)])
        for g6 in range(NFG):
            norm_mod(xf, hf, 2, 0, 1, for_mi, 'x', 'h')
            ffn(xf, hf, hidf, 2, f1w1, f1w2, 2, for_mi, 'x', 'h', 'hidf', hook=(lambda: mod_jobs.pop(0)() if mod_jobs else None))
            norm_mod(xf, hf, 2, 3, 4, for_mi, 'x', 'h')
            def kvproj(h, g6=g6):
                par = (g6 * 4 + h) % 2
                ktokf, vextf = KTs[par], VEs[par]
                S.dma('sp', bkvh[:, 0:256], bkrow_d[:, h * 256:(h + 1) * 256], 'd_bk', writes=[('bkvh', 0)])
                S.dma('sp', bkvh[:, 256:512], bvrow_d[:, h * 256:(h + 1) * 256], 'd_bv', writes=[('bkvh', 1)])
                S.op('dve', lambda e: e.tensor_scalar(out=bkvh[:, 0:256], in0=bkvh[:, 0:256], scalar1=1.0 / 16, scalar2=None, op0=ALU.mult),
                     reads=[('bkvh', 0)], writes=[('bkvh', 0)])
                s_, wp = load_panel([w_inv[:, :, K_OFF + h * 256:K_OFF + (h + 1) * 256], w_inv[:, :, V_OFF + h * 256:V_OFF + (h + 1) * 256]],
                                    "p (two k n) -> p two k n", two=2, k=8)
                for t in range(8):
                    b = psum()

                    def fp(e, b=b, t=t, wp=wp):
                        for kc in range(8):
                            ins = e.matmul(PS[b][:].rearrange("p (a c) -> p a c", a=2), hf[:, kc, t * 128:(t + 1) * 128], wp[:, :, kc, :],
                                           start=(kc == 0), stop=(kc == 7))
                        return ins
                    S.op('pe', fp, reads=WR(s_) + [('h', kc, t // 4) for kc in range(8)], writes=[('ps', b)])
                    S.op('dve', lambda e, b=b, t=t, ktokf=ktokf: e.scalar_tensor_tensor(out=ktokf[:, t, :], in0=PS[b][:, 0:256], scalar=1.0 / 16, in1=bkvh[:, 0:256],
                                                                                   op0=ALU.mult, op1=ALU.add), reads=[('ps', b), ('bkvh', 0)], writes=[('f_ktok', par, t)])
                    S.op('dve', lambda e, b=b, t=t, vextf=vextf: e.tensor_tensor(out=vextf[:, t, 0:256], in0=PS[b][:, 256:512], in1=bkvh[:, 256:512], op=ALU.add),
                         reads=[('ps', b), ('bkvh', 1)], writes=[('f_vext', par, t)])

            def kvagg(h, g6=g6):
                par = (g6 * 4 + h) % 2
                ktokf, vextf = KTs[par], VEs[par]
                banks = [psum() for _ in range(4)]
                for t in range(8):
                    for dr in range(2):
                        col = t * 8 + dr * 4 + h
                        vi = dr * 8 + t
                        if dr == 0:
                            S.op('act', lambda e, t=t, col=col, vi=vi, vextf=vextf: e.activation(out=Vpf[vi], in_=vextf[:, t, :], func=AF.Identity, scale=WT[:, 0, col:col + 1]),
                                 reads=[('f_vext', par, t), 'WT'], writes=[('Vpf', vi)])
                        else:
                            S.op('act', lambda e, t=t, col=col, vi=vi, vextf=vextf: e.activation(out=Vpf[vi], in_=vextf[:, t, :], func=AF.Identity, scale=WT[:, 0, col:col + 1]),
                                 reads=[('f_vext', par, t), 'WT'], writes=[('Vpf', vi)])

                    def fa(e, t=t, banks=banks, ktokf=ktokf):
                        for dr in range(2):
                            for dc in range(2):
                                ins = e.matmul(PS[banks[dr * 2 + dc]][:, 0:257], ktokf[:, t, dc * 128:(dc + 1) * 128], Vpf[dr * 8 + t], start=(t == 0), stop=(t == 7))
                        return ins
                    S.op('pe', fa, reads=[('f_ktok', par, t), ('Vpf', t), ('Vpf', 8 + t)], writes=[('ps', bk) for bk in banks])
                for dr in range(2):
                    for dc in range(2):
                        bk = banks[dr * 2 + dc]
                        if dc == 0:
                            S.op('act', lambda e, dr=dr, dc=dc, bk=bk: e.activation(out=Cf[dr][:, dc, :], in_=PS[bk][:, 0:257], func=AF.Copy), reads=[('ps', bk)], writes=[('Cf', dr)])
                        else:
                            S.op('dve', lambda e, dr=dr, dc=dc, bk=bk: e.tensor_copy(out=Cf[dr][:, dc, :], in_=PS[bk][:, 0:257]), reads=[('ps', bk)], writes=[('Cf', dr)])
                for ci, j in ((0, h), (1, 4 + h)):
                    S.dma('sp', aggC[j, g6].rearrange("dc p c -> p dc c"), Cf[ci], 'd_ag%d' % ci, reads=[('Cf', ci)], writes=[('aggC', j)])

            kvproj(0)
            kvproj(1)
            gates_front(hf, 'h', 8)
            if g6 + 1 < NFG:
                load_x(xfor[(g6 + 1) * 1024:(g6 + 2) * 1024, :], 8, xf, 'x')
            else:
                load_x(xin, NTILE, x[:], 'x')
            for (j0, order) in ((0, list(range(8))), (4, list(range(7, -1, -1)))):
                last = order[-1]
                S.op('dve', lambda e, last=last, j0=j0: e.tensor_copy(out=MMr[0:1, last * 8 + j0:last * 8 + j0 + 4], in_=BL[:, last * 8 + j0:last * 8 + j0 + 4]),
                     reads=['P0'], writes=['MMr'])
                for idx in range(6, -1, -1):
                    t, tn = order[idx], order[idx + 1]
                    S.op('dve', lambda e, t=t, tn=tn, j0=j0: e.tensor_tensor(out=MMr[0:1, t * 8 + j0:t * 8 + j0 + 4], in0=BL[:, t * 8 + j0:t * 8 + j0 + 4],
                                                                          in1=MMr[0:1, tn * 8 + j0:tn * 8 + j0 + 4], op=ALU.add), reads=['P0', 'MMr'], writes=['MMr'])
                first = order[0]
                S.op('dve', lambda e, first=first, j0=j0, g6=g6: e.tensor_copy(out=AGS[0:1, g6, 1, j0:j0 + 4], in_=MMr[0:1, first * 8 + j0:first * 8 + j0 + 4]),
                     reads=['MMr'], writes=['AGS'])
            S.op('dve', lambda e: e.tensor_tensor(out=RRr[0:1, 0:64], in0=MMr[0:1, 0:64], in1=RM[:, 0:64], op=ALU.add), reads=['MMr', 'P0'], writes=['RRr'])
            S.op('dve', lambda e, g6=g6: e.tensor_reduce(out=AGS[0:1, g6, 0, :], in_=RRr[0:1, 0:64].rearrange("o (t j) -> o j t", j=8), axis=AX.X, op=ALU.max),
                 reads=['RRr'], writes=['AGS'])
            S.op('dve', lambda e, g6=g6: e.tensor_tensor(out=DECr[0:1, 0:64].rearrange("o (t j) -> o t j", j=8), in0=MMr[0:1, 0:64].rearrange("o (t j) -> o t j", j=8),
                                                      in1=AGS[0:1, g6, 0:1, :].to_broadcast([1, 8, 8]), op=ALU.subtract), reads=['MMr', 'AGS'], writes=['DECr'])
            be = psum()
            S.op('pe', lambda e, be=be: e.matmul(PS[be][0:64, 0:1], DECr[0:1, 0:64], ident[0:1, 0:1], start=True, stop=True), reads=['DECr', 'ident'], writes=[('ps', be)])
            S.op('dve', lambda e, be=be: e.tensor_copy(out=COLS[0:64, 2:3], in_=PS[be][0:64, 0:1]), reads=[('ps', be)], writes=['NEGR'])
            S.op('act', lambda e: e.activation(out=ROW2[0:64, 0, :], in_=ROW[0:64, 0, :], func=AF.Exp, bias=COLS[0:64, 2:3], scale=1.0), reads=['ROW', 'NEGR'], writes=['ROW2'])
            bt = psum()
            S.op('pe', lambda e, bt=bt: e.transpose(PS[bt][:, 0:64], ROW2[0:64, 0, :], ident[0:64, 0:64]), reads=['ROW2', 'ident'], writes=[('ps', bt)])
            S.op('dve', lambda e, bt=bt: e.tensor_copy(out=WT[:, 0, 0:64], in_=PS[bt][:, 0:64]), reads=[('ps', bt)], writes=['WT'])
            kvagg(0)
            kvproj(2)
            kvagg(1)
            kvproj(3)
            kvagg(2)
            kvagg(3)
        S.barrier()
        S.enabled = True
        if LVL < 1:
            load_x(xin, NTILE, x[:], 'x')

        TA = rowtab[0:1, 0:96].rearrange("o (p q j) -> o p q j", p=4, q=3)
        TM = rowtab[0:1, 96:128].rearrange("o (p j) -> o p j", p=4)
        M0 = rowtab[0:1, 128:136]
        Bt = AGS[0:1, :, 1, :]
        LW = CROW[0:1, 0:32].rearrange("o (p j) -> o p j", p=4)
        TMP = CROW[0:1, 64:88].rearrange("o (q j) -> o q j", q=3)
        MIN = CROW[0:1, 120:128]
        def combine_pre():
            for p in range(4):
                S.op('dve', lambda e, p=p: e.tensor_tensor(out=TMP, in0=TA[:, p], in1=Bt, op=ALU.mult), reads=['rowtab', 'AGS'], writes=['TMP'])
                S.op('dve', lambda e, p=p: e.tensor_reduce(out=LW[:, p, :], in_=TMP.rearrange("o q j -> o j q"), axis=AX.X, op=ALU.add), reads=['TMP'], writes=['LW'])
            S.op('dve', lambda e: e.tensor_tensor(out=LW[:, 0:3, :], in0=LW[:, 0:3, :], in1=AGS[0:1, :, 0, :], op=ALU.add), reads=['LW', 'AGS'], writes=['LW'])
            S.op('dve', lambda e: e.tensor_tensor(out=LW[:, 3, :], in0=LW[:, 3, :], in1=M0, op=ALU.add), reads=['LW', 'rowtab'], writes=['LW'])
            S.op('dve', lambda e: e.tensor_tensor(out=LW, in0=LW, in1=TM, op=ALU.add), reads=['LW', 'rowtab'], writes=['LW'])
            S.op('dve', lambda e: e.tensor_reduce(out=MIN, in_=LW.rearrange("o p j -> o j p"), axis=AX.X, op=ALU.max), reads=['LW'], writes=['MIN'])
            S.op('dve', lambda e: e.tensor_tensor(out=LW, in0=LW, in1=MIN.rearrange("o (a j) -> o a j", a=1).to_broadcast([1, 4, 8]), op=ALU.subtract),
                 reads=['LW', 'MIN'], writes=['LW'])
            S.op('act', lambda e: e.activation(out=CROW[0:1, 0:32], in_=CROW[0:1, 0:32], func=AF.Exp), reads=['LW'], writes=['LW'])
            bcb = psum()
            S.op('pe', lambda e: e.matmul(PS[bcb][:, 0:32], onesf[0:1, :], CROW[0:1, 0:32], start=True, stop=True), reads=['onesf', 'LW'], writes=[('ps', bcb)])
            S.op('act', lambda e: e.activation(out=CB[:], in_=PS[bcb][:, 0:32], func=AF.Copy), reads=[('ps', bcb)], writes=['CB'])

        def combine_j(j, stg, c0s, acc):
            S.dma('sp', stg, aggC[j].rearrange("g dc p c -> p (g dc) c"), 'd_st0', reads=[('aggC', j)], writes=[('stg', 0)])
            S.dma('sp', c0s, c0ext_d[j].rearrange("dc p c -> p dc c"), 'd_c0s0', writes=[('c0s', 0)])
            S.op('dve', lambda e: e.tensor_scalar(out=acc, in0=c0s, scalar1=CB[:, 24 + j:25 + j], scalar2=None, op0=ALU.mult),
                 reads=[('c0s', 0), 'CB'], writes=[('acc', 0)])
            for p in range(NFG):
                S.op('dve', lambda e, p=p: e.scalar_tensor_tensor(out=acc, in0=stg[:, 2 * p:2 * p + 2, :], scalar=CB[:, p * 8 + j:p * 8 + j + 1],
                                                              in1=acc, op0=ALU.mult, op1=ALU.add), reads=[('stg', 0), 'CB', ('acc', 0)], writes=[('acc', 0)])
            S.dma('sp', cinS[j].rearrange("dc p c -> p dc c"), acc, 'd_ci0', reads=[('acc', 0)], writes=[('cinS', j)])

        S.enabled = LVL >= 2
        S.barrier()
        AR.reset()
        hid = AR.alloc([NJ, NT], BF16)
        stg1 = AR.alloc([NFG * 2, 257], F32)
        c0s1 = AR.alloc([2, 257], F32)
        acc1 = AR.alloc([2, 257], F32)
        jobs = [(lambda pi=pi: mod_panels([pi])) for pi in range(12, 18)] + [lambda: derive_tabs(((6, 7, 6, 8, 2, 0.5),), MODK)]
        jobs += [combine_pre] + [(lambda j=j: combine_j(j, stg1, c0s1, acc1)) for j in range(8)]
        norm_mod(x[:], hT[:], NG, 0, 1, own_mi, 'x', 'h')
        ffn(x[:], hT[:], hid, NG, f1w1, f1w2, 2, own_mi, 'x', 'h', 'hid', hook=lambda: jobs.pop(0)() if jobs else None)
        while jobs:
            jobs.pop(0)()
        out_toks = []
        XK = [('x', kc, g) for kc in range(8) for g in range(NG)]
        HK = [('h', kc, g) for kc in range(8) for g in range(NG)]
        if 'x1' in dbg:
            out_toks.append(S.dma('sp', dbg_out['x1'], x[:].rearrange("p k t -> p (k t)"), 'd_dbg1', reads=XK))
        if 'mod' in dbg:
            out_toks.append(S.dma('sp', dbg_out['mod'], modT[:].rearrange("p j m -> p (j m)"), 'd_dbg2', reads=MODK))

        S.enabled = LVL >= 3
        if 'hmix' in dbg:
            S.barrier()
        AR.reset()
        norm_mod(x[:], hT[:], NG, 3, 4, own_mi, 'x', 'h')
        if 'hmix' in dbg:
            tmpd = AR.alloc([8, NT], F32)
            S.op('dve', lambda e: e.tensor_copy(out=tmpd, in_=hT[:]), reads=HK, writes=['tmpd'])
            out_toks.append(S.dma('sp', dbg_out['hmix'], tmpd.rearrange("p k t -> p (k t)"), 'd_dbg3', reads=['tmpd']))
            S.barrier()
            AR.reset()
        gates_front(hT[:], 'h', 12)
        if 'cin' in dbg:
            S.barrier()
            out_toks.append(S.dma('sp', dbg_out['cin'], cinS, 'd_dbg4', reads=[('cinS', j) for j in range(8)]))
        chain_recur([0, 1], 0, 0.0, MFIN[0:1, 0, 0:4])
        chain_recur([1, 0], 4, 0.0, MFIN[0:1, 0, 4:8])
        chain_recur([2, 3], 0, 0.0, MFIN[0:1, 1, 0:4])
        chain_recur([3, 2], 4, 0.0, MFIN[0:1, 1, 4:8])
        chain_recur(list(range(4, 12)), 0, MIN[:, 0:4], MFIN[0:1, 2, 0:4])
        chain_recur(list(range(11, 3, -1)), 4, MIN[:, 4:8], MFIN[0:1, 2, 4:8])
        gates_finish(12)
        out_toks.append(S.dma('sp', outM, MFIN[0:1, 0:2, :].rearrange("o a b -> o (a b)"), 'd_om', reads=['MFIN']))
        if 'gates' in dbg:
            out_toks.append(S.dma('sp', dbg_out['gates'][:, 0:192], WT[:].rearrange("p a c -> p (a c)"), 'd_dbg5', reads=['WT']))
            out_toks.append(S.dma('sp', dbg_out['gates'][:, 192:288], DB[:], 'd_dbg6', reads=['DB']))
            out_toks.append(S.dma('sp', dbg_out['gates'][:, 288:480], GT[:], 'd_dbg7', reads=['GT']))
        if 'rows' in dbg:
            for i_, (src_, k_) in enumerate(((RM, 'P0'), (BL, 'P0'), (MMr[0:1, :], 'MMr'), (RRr[0:1, :], 'RRr'), (DECr[0:1, :], 'DECr'))):
                out_toks.append(S.dma('sp', dbg_out['rows'][:, i_ * 96:(i_ + 1) * 96], src_, 'd_dbg8', reads=[k_]))
            out_toks.append(S.dma('sp', dbg_out['rows'][:, 576:584], MIN, 'd_dbg8', reads=['MIN']))
            out_toks.append(S.dma('sp', dbg_out['rows'][:, 584:584 + NFG * 16], AGS[:].rearrange("o a b c -> o (a b c)"), 'd_dbg8', reads=['AGS']))

        S.enabled = LVL >= 4
        S.barrier()
        AR.reset()
        ohmT = AR.alloc([8, NT], BF16)
        qT = AR.alloc([2, NT], BF16)
        kT = AR.alloc([2, NT], BF16)
        ktok = AR.alloc([12, 256], BF16)
        vext = AR.alloc([12, 258], BF16)[:, :, 0:257]
        hm_off = AR.off
        hm = AR.alloc([12, 256], F32)
        Cst = [AR.alloc([2, 257], F32) for _ in range(4)]
        Dbf = [AR.alloc([2, 258], BF16)[:, :, 0:257] for _ in range(4)]
        Vp = [AR.alloc([258], BF16)[:, 0:257] for _ in range(4)]
        sTm = [AR.alloc([128], BF16) for _ in range(4)]
        hmn = AR.alloc([256], F32)
        vT = AR.h16[:, hm_off // 2:hm_off // 2 + 2 * NT].rearrange("p (a t) -> p a t", a=2)
        oT = qT
        S.op('dve', lambda e: e.memset(vext[:, :, 256:257], 1.0), writes=[('m_vext', t) for t in range(12)])
        xs0b = xstage[0].bitcast(BF16)
        xs1b = xstage[1].bitcast(BF16)
        t0b = tmpf[0].bitcast(BF16)
        t1b = tmpf[1].bitcast(BF16)
        STR = [xs0b[:, i * 128:(i + 1) * 128] for i in range(16)]
        VPR = [xs1b[:, i * 272:i * 272 + 257] for i in range(7)] + [t0b[:, i * 272:i * 272 + 257] for i in range(3)] + [t1b[:, i * 272:i * 272 + 257] for i in range(3)]

        def step_pre(t, j, Vp_, vpkey, sTm_, skey, qT_, kT_, vext_, mask):
            col = t * 8 + j
            ts_ = slice(t * 128, (t + 1) * 128)
            g = t // 4
            S.op('act', lambda e: e.activation(out=Vp_, in_=vext_[:, t, :], func=AF.Identity, scale=WT[:, 0, col:col + 1]),
                 reads=[('m_vext', t), 'WT'], writes=[vpkey])
            bs = pool_bank('S')

            def f(e, bs=bs):
                for dc in range(2):
                    ins = e.matmul(PS[bs][:, 0:128], kT_[:, dc, ts_], qT_[:, dc, ts_], start=(dc == 0), stop=(dc == 1))
                return ins
            S.op('pe', f, reads=[('m_kT', g), ('m_qT', g)], writes=[('ps', bs)])
            S.op('dve', lambda e, bs=bs: e.tensor_tensor(out=sTm_, in0=PS[bs][:, 0:128], in1=mask[:], op=ALU.mult),
                 reads=[('ps', bs), 'maskF', 'maskB'], writes=[skey])

        def step_post(t, j, Cst_, ckey, Vp_, vpkey, sTm_, skey, qT_, ktok_, Dbf_, dkey, hm_, first):
            col = t * 8 + j
            ts_ = slice(t * 128, (t + 1) * 128)
            g = t // 4
            S.op('act', lambda e: e.activation(out=Dbf_, in_=Cst_, func=AF.Identity, scale=DB[:, col:col + 1]), reads=[ckey, 'DB'], writes=[dkey])
            bn = pool_bank('A')
            bkv = pool_bank('B')

            def f3(e, bn=bn, bkv=bkv):
                e.matmul(PS[bkv][:, 0:256], ktok_[:, t, 0:128], Vp_[:, 0:256], start=True, stop=True)
                e.matmul(PS[bkv][:, 256:512], ktok_[:, t, 128:256], Vp_[:, 0:256], start=True, stop=True)
                e.matmul(PS[bn][:, 260:261], ktok_[:, t, 0:128], Vp_[:, 256:257], start=True, stop=True)
                return e.matmul(PS[bn][:, 261:262], ktok_[:, t, 128:256], Vp_[:, 256:257], start=True, stop=True)
            S.op('pe', f3, reads=[('m_ktok', t), vpkey], writes=[('ps', bkv), ('ps', bn), ('psn', bn)])
            def f2(e, bn=bn):
                e.matmul(PS[bn][:, 0:257], sTm_, Vp_, start=True, stop=False)
                e.matmul(PS[bn][:, 0:257], qT_[:, 0, ts_], Dbf_[:, 0, :], start=False, stop=False)
                return e.matmul(PS[bn][:, 0:257], qT_[:, 1, ts_], Dbf_[:, 1, :], start=False, stop=True)
            S.op('pe', f2, reads=[skey, vpkey, dkey, ('m_qT', g)], writes=[('ps', bn)])

            S.op('dve', lambda e, bkv=bkv: e.scalar_tensor_tensor(out=Cst_[:, :, 0:256], in0=Cst_[:, :, 0:256], scalar=DB[:, col:col + 1],
                                                               in1=PS[bkv][:].rearrange("p (a c) -> p a c", a=2), op0=ALU.mult, op1=ALU.add),
                 reads=[('ps', bkv), 'DB', ckey, dkey], writes=[ckey])
            S.op('dve', lambda e, bn=bn: e.scalar_tensor_tensor(out=Cst_[:, :, 256], in0=Cst_[:, :, 256], scalar=DB[:, col:col + 1],
                                                             in1=PS[bn][:, 260:262], op0=ALU.mult, op1=ALU.add),
                 reads=[('psn', bn), 'DB', ckey, dkey], writes=[ckey])
            return bn

        def step_post_b(t, j, bn, hm_, first):
            col = t * 8 + j
            dcol = small[:, 32 + (col % 16):33 + (col % 16)]
            dk = ('den', col % 16)
            if DEN_FAST:
                S.op('dve', lambda e: e.tensor_tensor(out=dcol, in0=PS[bn][:, 256:257], in1=WT[:, 1, col:col + 1], op=ALU.abs_max), reads=[('ps', bn), 'WT'], writes=[dk])
                if first:
                    S.op('dve', lambda e: e.tensor_scalar(out=hm_[:, t, :], in0=PS[bn][:, 0:256], scalar1=dcol, scalar2=None, op0=ALU.divide),
                         reads=[('ps', bn), dk], writes=[('m_hm', t)])
                else:
                    S.op('dve', lambda e: e.scalar_tensor_tensor(out=hm_[:, t, :], in0=PS[bn][:, 0:256], scalar=dcol, in1=hm_[:, t, :],
                                                              op0=ALU.divide, op1=ALU.add), reads=[('ps', bn), dk, ('m_hm', t)], writes=[('m_hm', t)])
            else:
                S.op('dve', lambda e: e.tensor_scalar(out=dcol, in0=PS[bn][:, 256:257], scalar1=-1.0, scalar2=WT[:, 1, col:col + 1],
                                                   op0=ALU.mult, op1=ALU.max), reads=[('ps', bn), 'WT'], writes=[dk])
                S.op('dve', lambda e: e.tensor_tensor(out=dcol, in0=dcol, in1=PS[bn][:, 256:257], op=ALU.max), reads=[('ps', bn), dk], writes=[dk])
                S.op('dve', lambda e: e.reciprocal(out=dcol, in_=dcol), reads=[dk], writes=[dk])
                if first:
                    S.op('dve', lambda e: e.tensor_scalar(out=hm_[:, t, :], in0=PS[bn][:, 0:256], scalar1=dcol, scalar2=None, op0=ALU.mult),
                         reads=[('ps', bn), dk], writes=[('m_hm', t)])
                else:
                    S.op('dve', lambda e: e.scalar_tensor_tensor(out=hm_[:, t, :], in0=PS[bn][:, 0:256], scalar=dcol, in1=hm_[:, t, :],
                                                              op0=ALU.mult, op1=ALU.add), reads=[('ps', bn), dk, ('m_hm', t)], writes=[('m_hm', t)])
        for h in range(4):
            s, wp = load_panel(w_inv[:, :, Q_OFF + h * 256:Q_OFF + (h + 1) * 256], "p (k n) -> p k n", k=8)

            def ev_q(blk, g, gs, b, h=h):
                c = _bin_col(Q_OFF) + 2 * h + blk
                S.op(evq_act(), lambda e: e.activation(out=qT[:, blk, gs], in_=PS[b][:], func=AF.Identity, bias=vecs[:, c:c + 1], scale=1.0),
                     reads=[('ps', b), 'vecs'], writes=[('m_qT', g)])
            evq_act = lambda: 'act'
            proj_fm(s, lambda kc, blk, wp=wp: wp[:, kc, blk * 128:(blk + 1) * 128], 2, hT[:], 'h', 8, NG, ev_q)
            kv_head(h, hT[:], 'h', NG, 12, kT, vT, ktok, vext, 'm_', alias=[('m_hm', t) for t in range(12)])
            for ci in range(2):
                S.dma('sp', Cst[ci], cinS[ci * 4 + h].rearrange("dc p c -> p dc c"), 'd_cl%d' % ci, reads=[('cinS', ci * 4 + h)], writes=[('Cst', ci)])
            events = []
            for seq in range(2):
                for ci in (2, 3):
                    events.append(('ms', ci))
                t0 = seq * 2
                for i in range(2):
                    events.append(('st', t0 + i, 0, 2))
                    events.append(('st', t0 + 1 - i, 1, 3))
                    for k_ in range(2):
                        si = seq * 4 + i * 2 + k_
                        events.append(('st', 4 + si, 0, 0))
                        events.append(('st', 11 - si, 1, 1))
                for dr, ci in ((0, 2), (1, 3)):
                    events.append(('out', seq, dr, ci))
            steps = [ev for ev in events if ev[0] == 'st']
            seen_t = set()
            firsts = []
            for ev in steps:
                firsts.append(ev[1] not in seen_t)
                seen_t.add(ev[1])
            LOOK = 12

            def pre(k):
                _, t, dr, ci = steps[k]
                step_pre(t, dr * 4 + h, VPR[k % len(VPR)], ('VPR', k % len(VPR)), STR[k % len(STR)], ('STR', k % len(STR)), qT, kT, vext,
                         maskF if dr == 0 else maskB)

            pend = []

            def post(k):
                _, t, dr, ci = steps[k]
                bn = step_post(t, dr * 4 + h, Cst[ci], ('Cst', ci), VPR[k % len(VPR)], ('VPR', k % len(VPR)), STR[k % len(STR)], ('STR', k % len(STR)),
                               qT, ktok, Dbf[ci], ('Dbf', ci), hm, firsts[k])
                while pend:
                    step_post_b(*pend.pop(0))
                pend.append((t, dr * 4 + h, bn, hm, firsts[k]))
            for k in range(min(LOOK, len(steps))):
                pre(k)
            kk = 0
            for ev in events:
                if ev[0] == 'ms':
                    S.op('dve', lambda e, ci=ev[1]: e.memset(Cst[ci], 0.0), writes=[('Cst', ev[1])])
                elif ev[0] == 'out':
                    _, seq, dr, ci = ev
                    out_toks.append(S.dma('sp', outC[seq, dr, h].rearrange("(dc p) e -> p dc e", p=128), Cst[ci][:, :, 0:256], 'd_oc%d' % ci,
                                          reads=[('Cst', ci)]))
                    out_toks.append(S.dma('sp', outN[seq, dr, h].rearrange("(dc p) -> p dc", p=128), Cst[ci][:, :, 256], 'd_on%d' % ci,
                                          reads=[('Cst', ci)], allow_slow_non_contiguous=True))
                else:
                    post(kk)
                    if kk + LOOK < len(steps):
                        pre(kk + LOOK)
                    kk += 1
            while pend:
                step_post_b(*pend.pop(0))
            if 'hmraw' in dbg:
                out_toks.append(S.dma('sp', dbg_out['hmraw'][h], hm.rearrange("p t e -> p (t e)"), 'd_dbg9', reads=[('m_hm', t) for t in range(12)]))
            s, wp = load_panel(w_inv[:, :, O_OFF + h * 256:O_OFF + (h + 1) * 256], "p (k n) -> p k n", k=8)

            def ev_o(blk, g, gs, b, h=h):
                c = _bin_col(O_OFF) + 2 * h + blk
                S.op('act', lambda e: e.activation(out=oT[:, blk, gs], in_=PS[b][:], func=AF.Sigmoid, bias=vecs[:, c:c + 1], scale=1.0),
                     reads=[('ps', b), 'vecs'], writes=[('m_qT', g)])
            proj_fm(s, lambda kc, blk, wp=wp: wp[:, kc, blk * 128:(blk + 1) * 128], 2, hT[:], 'h', 8, NG, ev_o)
            HMN = [hmn, rstd[:, 0:256], rstd[:, 256:512]]
            ssq = small[:, 48:60]
            for t in range(12):
                hb = HMN[t % 3]
                S.op('act', lambda e, t=t, hb=hb: e.activation(out=hb, in_=hm[:, t, :], func=AF.Square, accum_out=small[:, 48 + t:49 + t]),
                     reads=[('m_hm', t)], writes=[('hmn', t % 3), ('hss', t)])
            S.op('act', lambda e: e.activation(out=ssq, in_=ssq, func=AF.Sqrt, bias=1e-6, scale=1.0 / 256), reads=[('hss', t) for t in range(12)], writes=['hssq'])
            S.op('dve', lambda e: e.reciprocal(out=ssq, in_=ssq), reads=['hssq'], writes=['hssq'])
            for t in range(12):
                g = t // 4
                ts_ = slice(t * 128, (t + 1) * 128)
                hb = HMN[t % 3]
                S.op('act', lambda e, t=t, hb=hb: e.activation(out=hb, in_=hm[:, t, :], func=AF.Identity, scale=small[:, 48 + t:49 + t]),
                     reads=[('m_hm', t), 'hssq'], writes=[('hmn', t % 3)])
                b = psum()

                def f(e, b=b, hb=hb):
                    e.transpose(PS[b][:, 0:128], hb[:, 0:128], ident[:])
                    return e.transpose(PS[b][:, 128:256], hb[:, 128:256], ident[:])
                S.op('pe', f, reads=[('hmn', t % 3), 'ident'], writes=[('ps', b)])
                for blk in range(2):
                    S.op('dve', lambda e, b=b, blk=blk, ts_=ts_, h=h: e.scalar_tensor_tensor(
                        out=ohmT[:, 2 * h + blk, ts_], in0=PS[b][:, blk * 128:(blk + 1) * 128], scalar=vecs[:, V_MLN + 2 * h + blk:V_MLN + 2 * h + blk + 1],
                        in1=oT[:, blk, ts_], op0=ALU.mult, op1=ALU.mult), reads=[('ps', b), 'vecs', ('m_qT', g)], writes=[('ohm', 2 * h + blk, g)])
        OHK = [('ohm', kc, g) for kc in range(8) for g in range(NG)]
        if 'ohm' in dbg:
            S.barrier()
            tmpd = AR.alloc([8, NT], F32) if AR.off + 8 * NT * 4 <= AR.cap else None
            if tmpd is None:
                AR.off = 8 * NT * 2
                tmpd = AR.alloc([8, NT], F32)
            S.op('dve', lambda e: e.tensor_copy(out=tmpd, in_=ohmT), reads=OHK, writes=['tmpd'])
            out_toks.append(S.dma('sp', dbg_out['ohm'], tmpd.rearrange("p k t -> p (k t)"), 'd_dbg3', reads=['tmpd']))

        S.enabled = LVL >= 5
        S.barrier()
        AR.reset()
        ohmT = AR.alloc([8, NT], BF16)
        yT = AR.alloc([8, NT], BF16)
        gtmp = [AR.alloc([512], BF16) for _ in range(2)]
        w_mlv = w_ml.rearrange("(kc p) n -> p kc n", p=128)
        for jp in range(4):
            s, wp = load_panel([w_inv[:, :, GML_OFF + jp * 256:GML_OFF + (jp + 1) * 256], w_mlv[:, :, jp * 256:(jp + 1) * 256]],
                               "p (two k n) -> p two k n", two=2, k=8)
            for blk in range(2):
                j = jp * 2 + blk
                for g in range(NG):
                    gs = slice(g * 512, (g + 1) * 512)
                    ba = psum()
                    bb = psum()

                    def f(e, wp=wp, blk=blk, gs=gs, ba=ba, bb=bb):
                        for kc in range(8):
                            e.matmul(PS[ba][:], wp[:, 0, kc, blk * 128:(blk + 1) * 128], hT[:, kc, gs], start=(kc == 0), stop=(kc == 7))
                        for kc in range(8):
                            ins = e.matmul(PS[bb][:], wp[:, 1, kc, blk * 128:(blk + 1) * 128], ohmT[:, kc, gs], start=(kc == 0), stop=(kc == 7))
                        return ins
                    S.op('pe', f, reads=WR(s) + [('h', kc, g) for kc in range(8)] + [('ohm', kc, g) for kc in range(8)], writes=[('ps', ba), ('ps', bb)])
                    gi = g % 2
                    c = _bin_col(GML_OFF) + j
                    S.op('act', lambda e, ba=ba, gi=gi, c=c: e.activation(out=gtmp[gi], in_=PS[ba][:], func=AF.Sigmoid, bias=vecs[:, c:c + 1], scale=1.0),
                         reads=[('ps', ba), 'vecs'], writes=[('gtmp', gi)])
                    S.op('dve', lambda e, bb=bb, gi=gi, j=j, gs=gs: e.tensor_tensor(out=yT[:, j, gs], in0=PS[bb][:], in1=gtmp[gi], op=ALU.mult),
                         reads=[('ps', bb), ('gtmp', gi)], writes=[('y', j, g)])

        S.enabled = LVL >= 6
        S.barrier()
        AR.reset()
        buT = AR.alloc([8, NT], BF16)
        yT = AR.alloc([8, NT], BF16)
        cgt = AR.alloc([NT], F32)
        ut = AR.alloc([NT], F32)
        uct = AR.alloc([NT], F32)
        gtmp = [AR.alloc([512], BF16) for _ in range(2)]
        for jp in range(4):
            s1, wp1 = load_panel([w_inv[:, :, C_OFF + jp * 256:C_OFF + (jp + 1) * 256], w_inv[:, :, X_OFF + jp * 256:X_OFF + (jp + 1) * 256]],
                                 "p (two k n) -> p two k n", two=2, k=8)
            s2, wp2 = load_panel(w_inv[:, :, B_OFF + jp * 256:B_OFF + (jp + 1) * 256], "p (k n) -> p k n", k=8)
            for blk in range(2):
                j = jp * 2 + blk

                def ev_c(blk_, g, gs, b, j=j):
                    c = _bin_col(C_OFF) + j
                    S.op('act', lambda e: e.activation(out=cgt[:, gs], in_=PS[b][:], func=AF.Identity, bias=vecs[:, c:c + 1], scale=1.0),
                         reads=[('ps', b), 'vecs'], writes=[('cgt', g)])

                def ev_x(blk_, g, gs, b, j=j):
                    c = _bin_col(X_OFF) + j
                    S.op('dve', lambda e: e.scalar_tensor_tensor(out=ut[:, gs], in0=PS[b][:], scalar=vecs[:, c:c + 1], in1=cgt[:, gs], op0=ALU.add, op1=ALU.mult),
                         reads=[('ps', b), 'vecs', ('cgt', g)], writes=[('ut', g)])
                proj_fm(s1, lambda kc, blk_, blk=blk, wp1=wp1: wp1[:, 0, kc, blk * 128:(blk + 1) * 128], 1, hT[:], 'h', 8, NG, ev_c)
                proj_fm(s1, lambda kc, blk_, blk=blk, wp1=wp1: wp1[:, 1, kc, blk * 128:(blk + 1) * 128], 1, hT[:], 'h', 8, NG, ev_x)
                UK = [('ut', g) for g in range(NG)]
                CK = [('uct', g) for g in range(NG)]
                cw = lambda i, j=j: vecs[:, V_CW + i * 8 + j:V_CW + i * 8 + j + 1]
                cw0, cw1, cw2, cbj = cw(0), cw(1), cw(2), vecs[:, V_CB + j:V_CB + j + 1]
                S.op('act', lambda e, cw1=cw1, cbj=cbj: e.activation(out=uct, in_=ut, func=AF.Identity, scale=cw1, bias=cbj),
                     reads=UK + ['vecs'], writes=CK)
                for (lo, n, L) in ((0, 2, 256), (512, 16, 64)):
                    uv = ut[:, lo:lo + n * L].rearrange("p (r w) -> p r w", w=L)
                    cv = uct[:, lo:lo + n * L].rearrange("p (r w) -> p r w", w=L)
                    S.op('dve', lambda e, uv=uv, cv=cv, L=L, cw0=cw0: e.scalar_tensor_tensor(out=cv[:, :, 1:L], in0=uv[:, :, 0:L - 1], scalar=cw0, in1=cv[:, :, 1:L],
                                                                                 op0=ALU.mult, op1=ALU.add), reads=UK + CK + ['vecs'], writes=CK)
                    S.op('dve', lambda e, uv=uv, cv=cv, L=L, cw2=cw2: e.scalar_tensor_tensor(out=cv[:, :, 0:L - 1], in0=uv[:, :, 1:L], scalar=cw2, in1=cv[:, :, 0:L - 1],
                                                                                 op0=ALU.mult, op1=ALU.add), reads=UK + CK + ['vecs'], writes=CK)

                def ev_b(blk_, g, gs, b, j=j):
                    c = _bin_col(B_OFF) + j
                    S.op('dve', lambda e: e.scalar_tensor_tensor(out=buT[:, j, gs], in0=PS[b][:], scalar=vecs[:, c:c + 1], in1=uct[:, gs], op0=ALU.add, op1=ALU.mult),
                         reads=[('ps', b), 'vecs', ('uct', g)], writes=[('bu', j, g)])
                proj_fm(s2, lambda kc, blk_, blk=blk, wp2=wp2: wp2[:, kc, blk * 128:(blk + 1) * 128], 1, hT[:], 'h', 8, NG, ev_b)
        w_scv = w_sc.rearrange("(kc p) n -> p kc n", p=128)
        for jp in range(4):
            s, wp = load_panel([w_inv[:, :, GSC_OFF + jp * 256:GSC_OFF + (jp + 1) * 256], w_scv[:, :, jp * 256:(jp + 1) * 256]],
                               "p (two k n) -> p two k n", two=2, k=8)
            for blk in range(2):
                j = jp * 2 + blk
                for g in range(NG):
                    gs = slice(g * 512, (g + 1) * 512)
                    ba = psum()
                    bb = psum()

                    def f(e, wp=wp, blk=blk, gs=gs, ba=ba, bb=bb):
                        for kc in range(8):
                            e.matmul(PS[ba][:], wp[:, 0, kc, blk * 128:(blk + 1) * 128], hT[:, kc, gs], start=(kc == 0), stop=(kc == 7))
                        for kc in range(8):
                            ins = e.matmul(PS[bb][:], wp[:, 1, kc, blk * 128:(blk + 1) * 128], buT[:, kc, gs], start=(kc == 0), stop=(kc == 7))
                        return ins
                    S.op('pe', f, reads=WR(s) + [('h', kc, g) for kc in range(8)] + [('bu', kc, g) for kc in range(8)], writes=[('ps', ba), ('ps', bb)])
                    gi = g % 2
                    c = _bin_col(GSC_OFF) + j
                    S.op('act', lambda e, ba=ba, gi=gi, c=c: e.activation(out=gtmp[gi], in_=PS[ba][:], func=AF.Sigmoid, bias=vecs[:, c:c + 1], scale=1.0),
                         reads=[('ps', ba), 'vecs'], writes=[('gtmp', gi)])
                    S.op('dve', lambda e, bb=bb, gi=gi: e.tensor_tensor(out=tmpf[gi][:], in0=PS[bb][:], in1=gtmp[gi], op=ALU.mult),
                         reads=[('ps', bb), ('gtmp', gi)], writes=[('tmpf', gi)])
                    S.op('dve', lambda e, gi=gi, j=j, gs=gs: e.tensor_tensor(out=yT[:, j, gs], in0=yT[:, j, gs], in1=tmpf[gi][:], op=ALU.add),
                         reads=[('tmpf', gi), ('y', j, g)], writes=[('y', j, g)])
        if 'ymix' in dbg:
            out_toks.append(S.dma('pool', dbg_out['ymix'], yT.rearrange("p k t -> p (k t)"), 'd_dbgy', reads=[('y', j, g) for j in range(8) for g in range(NG)]))
        w_ov = w_o.rearrange("(kc p) n -> p kc n", p=128)
        for jp in range(4):
            s, wp = load_panel(w_ov[:, :, jp * 256:(jp + 1) * 256], "p (k n) -> p k n", k=8)

            def ev_wo(blk, g, gs, b, jp=jp):
                j = jp * 2 + blk
                S.op('dve', lambda e: e.scalar_tensor_tensor(out=x[:, j, gs], in0=PS[b][:], scalar=mod_ap(5, j, own_mi(g)), in1=x[:, j, gs], op0=ALU.mult, op1=ALU.add),
                     reads=[('ps', b), ('tabs', 5), ('x', j, g)], writes=[('x', j, g)])
            proj_fm(s, lambda kc, blk, wp=wp: wp[:, kc, blk * 128:(blk + 1) * 128], 2, yT, 'y', 8, NG, ev_wo)
        if 'x2' in dbg:
            out_toks.append(S.dma('sp', dbg_out['x2'], x[:].rearrange("p k t -> p (k t)"), 'd_dbg1', reads=XK))

        S.enabled = LVL >= 7
        S.barrier()
        AR.reset()
        hid = AR.alloc([NJ, NT], BF16)
        fnbc = AR.alloc([D], F32)
        S.dma('sp', fnbc, fnbc_d, 'd_c3', writes=['fnbc'])
        norm_mod(x[:], hT[:], NG, 6, 7, own_mi, 'x', 'h')
        ffn(x[:], hT[:], hid, NG, f2w1, f2w2, 8, own_mi, 'x', 'h', 'hid')
        if 'x3' in dbg:
            out_toks.append(S.dma('sp', dbg_out['x3'], x[:].rearrange("p k t -> p (k t)"), 'd_dbg1', reads=XK))

        S.enabled = True
        if LVL < 7:
            fnbc = AR.alloc([D], F32) if AR.off + 4096 <= AR.cap else None
            if fnbc is None:
                AR.reset()
                fnbc = AR.alloc([D], F32)
            S.barrier()
            S.dma('sp', fnbc, fnbc_d, 'd_c3', writes=['fnbc'])
        for t in range(NTILE):
            g = t // 4
            oi = t % 2
            banks = []
            for half in range(2):
                b = psum()
                banks.append(b)

                def f(e, half=half, b=b, t=t):
                    for i in range(4):
                        kc = half * 4 + i
                        ins = e.transpose(PS[b][:, i * 128:(i + 1) * 128], x[:, kc, t * 128:(t + 1) * 128], ident[:])
                    return ins
                S.op('pe', f, reads=[('x', kc, g) for kc in range(half * 4, half * 4 + 4)] + ['ident'], writes=[('ps', b)])
            for half in range(2):
                S.op('act', lambda e, half=half, b=banks[half], t=t: e.activation(
                    out=tmpf[0][:], in_=PS[b][:], func=AF.Square, accum_out=small[:, 2 * (t % 8) + half:2 * (t % 8) + half + 1]),
                    reads=[('ps', banks[half])], writes=[('tmpf', 0), ('small', t % 8, half)])
            c0 = 2 * (t % 8)
            rc = small[:, 16 + (t % 8):17 + (t % 8)]
            S.op('dve', lambda e, c0=c0, rc=rc: e.tensor_tensor(out=rc, in0=small[:, c0:c0 + 1], in1=small[:, c0 + 1:c0 + 2], op=ALU.add),
                 reads=[('small', t % 8, 0), ('small', t % 8, 1)], writes=[('small2', t % 8)])
            S.op('act', lambda e, rc=rc: e.activation(out=rc, in_=rc, func=AF.Sqrt, bias=1e-6, scale=1.0 / D), reads=[('small2', t % 8)], writes=[('small2', t % 8)])
            S.op('dve', lambda e, rc=rc: e.reciprocal(out=rc, in_=rc), reads=[('small2', t % 8)], writes=[('small2', t % 8)])
            for half in range(2):
                S.op('dve', lambda e, half=half, b=banks[half], rc=rc, oi=oi: e.scalar_tensor_tensor(
                    out=xstage[oi][:, half * 512:(half + 1) * 512], in0=PS[b][:], scalar=rc,
                    in1=fnbc[:, half * 512:(half + 1) * 512], op0=ALU.mult, op1=ALU.mult),
                    reads=[('ps', banks[half]), ('small2', t % 8), 'fnbc'], writes=[('xstage', oi)])
            out_toks.append(S.dma('sp', yout[t * 128:(t + 1) * 128, :], xstage[oi][:], 'd_o%d' % oi, reads=[('xstage', oi)]))
        S.wait_all('sp', out_toks)
        S.replay()
    return nc


def _host_inputs(inputs, core, consts):
    f = np.float32
    xp = inputs['x_prompt']
    xs = inputs['x_sample']
    b = core // 4
    r = core % 4
    xin = np.concatenate([xp[2 * core], xp[2 * core + 1], xs[b, r * 1024:(r + 1) * 1024]], axis=0)
    fsegs = [s_ for s_ in range(4) if s_ != r]
    pos = fsegs
    xfor = np.concatenate([xs[b, p * 1024:(p + 1) * 1024] for p in pos], axis=0)
    cond2 = np.stack([inputs['c_ctx'], inputs['c'][b]], axis=1)
    condT = cond2.reshape(8, 128, 2).transpose(1, 0, 2).reshape(128, 16)
    TA = np.zeros((4, 3, 8), f)
    TM = np.zeros((4, 8), f)
    for j in range(8):
        fwd = j < 4
        inc = [(pos[p] < r) if fwd else (pos[p] > r) for p in range(NFG)]
        for p in range(NFG):
            TM[p, j] = 0.0 if inc[p] else BIGNEG
            for q in range(NFG):
                if inc[q] and inc[p]:
                    if (fwd and pos[p] < pos[q]) or ((not fwd) and pos[q] < pos[p]):
                        TA[p, q, j] = 1.0
        for q in range(NFG):
            if inc[q]:
                TA[3, q, j] = 1.0
    rowtab = np.concatenate([TA.reshape(-1), TM.reshape(-1), inputs['state_m'][b, 0].reshape(-1)]).astype(f)[None, :]
    sC = inputs['state_C'][b, 0].reshape(8, 2, 128, 256)
    sn = inputs['state_n'][b, 0].reshape(8, 2, 128, 1)
    c0ext = np.concatenate([sC, sn], axis=-1)
    m = dict(consts)
    m.update(xin=np.ascontiguousarray(xin, f), xfor=np.ascontiguousarray(xfor, f), condT=np.ascontiguousarray(condT, f),
             rowtab=np.ascontiguousarray(rowtab, f), c0ext=np.ascontiguousarray(c0ext, f))
    return m


def _host_consts(inputs):
    f = np.float32
    vecs = np.zeros((128, NV), f)

    def blk(v):
        return v.reshape(-1, 128).T
    vecs[:, V_BMOD:V_BMOD + 72] = blk(inputs['b_mod'][0])
    vecs[:, V_NG:V_NG + 24] = blk(inputs['norm_g'][0].reshape(-1))
    b_in = inputs['b_in'][0]
    vecs[:, V_BIN:V_BIN + 32] = blk(b_in[0:4096])
    vecs[:, V_BIN + 32:V_BIN + 72] = blk(b_in[4112:])
    vecs[:, V_CW:V_CW + 24] = blk(inputs['conv_w'][0].reshape(-1))
    vecs[:, V_CB:V_CB + 8] = blk(inputs['conv_b'][0])
    vecs[:, V_MLN:V_MLN + 8] = blk(inputs['ml_norm'][0])
    ss, tt = np.meshgrid(np.arange(128), np.arange(128), indexing='ij')
    return dict(
        vecs=vecs, ident=np.eye(128, dtype=f),
        ntriF=-(ss <= tt).astype(f), ntriB=-(ss >= tt).astype(f),
        fnbc=np.ascontiguousarray(np.broadcast_to(inputs['final_norm'], (128, D)), f),
        gb12=np.ascontiguousarray(np.broadcast_to(np.tile(b_in[4096:4112], 12), (128, 192)), f),
        bkrow=np.ascontiguousarray(np.broadcast_to(b_in[K_OFF:K_OFF + 1024], (128, 1024)), f),
        bvrow=np.ascontiguousarray(np.broadcast_to(b_in[V_OFF:V_OFF + 1024], (128, 1024)), f),
        w_mod=inputs['w_mod'][0], ffn1_w1=inputs['ffn1_w1'][0], ffn1_w2=inputs['ffn1_w2'][0],
        ffn2_w1=inputs['ffn2_w1'][0], ffn2_w2=inputs['ffn2_w2'][0], w_in=inputs['w_in'][0],
        w_ml_out=inputs['w_ml_out'][0], w_sc_out=inputs['w_sc_out'][0], w_o=inputs['w_o'][0],
    )


_NC_CACHE = {}


def kernel(**inputs):
    inputs = {k: np.asarray(v) for k, v in inputs.items()}
    dbg = tuple(DEBUG.get('dump', ())) + (DEBUG.get('stop', float(os.environ.get('KLVL', '99'))),)
    if dbg not in _NC_CACHE:
        _NC_CACHE[dbg] = build_program(list(dbg[:-1]))
    nc = _NC_CACHE[dbg]
    consts = _host_consts(inputs)
    in_maps = [_host_inputs(inputs, c, consts) for c in range(8)]
    res = run_bass_kernel_spmd(nc, in_maps, core_ids=list(range(8)))
    R = res.results
    DEBUG['results'] = R
    y_prompt = np.zeros((16, 256, D), np.float32)
    y_sample = np.zeros((2, 4096, D), np.float32)
    nC = np.zeros((16, 1, 2, 4, 256, 256), np.float32)
    nn = np.zeros((16, 1, 2, 4, 256), np.float32)
    nm = np.zeros((16, 1, 2, 4), np.float32)
    for c in range(8):
        yo = R[c]['yout']
        y_prompt[2 * c] = yo[0:256]
        y_prompt[2 * c + 1] = yo[256:512]
        y_sample[c // 4, (c % 4) * 1024:(c % 4 + 1) * 1024] = yo[512:]
        nC[2 * c:2 * c + 2, 0] = R[c]['outC']
        nn[2 * c:2 * c + 2, 0] = R[c]['outN']
        nm[2 * c:2 * c + 2, 0] = R[c]['outM'].reshape(2, 2, 4)
    return (y_prompt, y_sample, nC, nn, nm)
```

```python
import os
import numpy as np
from contextlib import ExitStack
import concourse.bass as bass
import concourse.mybir as mybir
from concourse.bass_utils import run_bass_kernel_spmd

F32 = mybir.dt.float32
BF16 = mybir.dt.bfloat16
AF = mybir.ActivationFunctionType
ALU = mybir.AluOpType
AX = mybir.AxisListType

ENGS = ['pe', 'act', 'dve', 'pool', 'sp']
SAME_ENGINE_SYNC = {'pe': False, 'act': True, 'dve': True, 'pool': True, 'sp': False}
STRICT_SAME_ENGINE = True


class Sched:
    def __init__(self, nc, stack):
        self.nc = nc
        self.stack = stack
        self.q = {e: [] for e in ENGS}
        self.cnt = {e: 0 for e in ENGS}
        self.seen = {e: {} for e in ENGS}
        self.lastw = {}
        self.readers = {}
        self.dmacnt = {}
        self.sems = {}
        self.enabled = True
        for e in ENGS:
            self._sem('e_' + e)

    def _sem(self, name):
        if name not in self.sems:
            self.sems[name] = self.stack.enter_context(self.nc.semaphore(name))
        return self.sems[name]

    def _waits(self, eng, reads, writes):
        waits = {}
        own = 'e_' + eng

        def need(tok, raw):
            s, v = tok
            if s == own and not raw and not STRICT_SAME_ENGINE:
                return
            if v > waits.get(s, 0):
                waits[s] = v
        for k in reads:
            if k in self.lastw:
                need(self.lastw[k], True)
            if isinstance(k, tuple) and k[0] == 'ps':
                for r in self.readers.get(k, ()):
                    if r[0] != own:
                        need(r, True)
        for k in writes:
            if k in self.lastw:
                need(self.lastw[k], False)
            for r in self.readers.get(k, ()):
                need(r, False)
        final = []
        for s, v in waits.items():
            if s == own and not SAME_ENGINE_SYNC[eng]:
                continue
            if self.seen[eng].get(s, 0) >= v:
                continue
            self.seen[eng][s] = v
            final.append((s, v))
        return final

    def _record(self, tok, reads, writes):
        for k in reads:
            self.readers.setdefault(k, []).append(tok)
        for k in writes:
            self.lastw[k] = tok
            self.readers[k] = []

    def op(self, eng, fn, reads=(), writes=()):
        if not self.enabled:
            return ('none', 0)
        reads = list(reads)
        writes = list(writes)
        final = self._waits(eng, reads, writes)
        self.cnt[eng] += 1
        tok = ('e_' + eng, self.cnt[eng])
        self.q[eng].append((final, fn, 'e_' + eng, 1))
        self._record(tok, reads, writes)
        return tok

    def dma(self, eng, out, in_, sem, reads=(), writes=(), **kw):
        if not self.enabled:
            return ('none', 0)
        reads = list(reads)
        writes = list(writes)
        self._sem(sem)
        final = self._waits(eng, reads, writes)
        self.dmacnt[sem] = self.dmacnt.get(sem, 0) + 16
        tok = (sem, self.dmacnt[sem])
        self.q[eng].append((final, lambda e: e.dma_start(out=out, in_=in_, **kw), sem, 16))
        self._record(tok, reads, writes)
        return tok

    def dma_fn(self, eng, fn, sem, reads=(), writes=()):
        if not self.enabled:
            return ('none', 0)
        reads = list(reads)
        writes = list(writes)
        self._sem(sem)
        final = self._waits(eng, reads, writes)
        self.dmacnt[sem] = self.dmacnt.get(sem, 0) + 16
        tok = (sem, self.dmacnt[sem])
        self.q[eng].append((final, fn, sem, 16))
        self._record(tok, reads, writes)
        return tok

    def wait_all(self, eng, toks):
        final = []
        if not self.enabled:
            return
        mx = {}
        for s, v in toks:
            if s != 'none' and v > mx.get(s, 0):
                mx[s] = v
        for s, v in mx.items():
            if self.seen[eng].get(s, 0) >= v:
                continue
            self.seen[eng][s] = v
            final.append((s, v))
        self.q[eng].append((final, None, None, 0))

    def barrier(self, engs=('pe', 'act', 'dve', 'sp')):
        if not self.enabled:
            return
        snap = [('e_' + e, self.cnt[e]) for e in ENGS if self.cnt[e] > 0] + list(self.dmacnt.items())
        for e in engs:
            self.wait_all(e, [t for t in snap if t[0] != 'e_' + e])

    def replay(self):
        nc = self.nc
        sems = self.sems

        def run(name, e):
            for waits, fn, incsem, incval in self.q[name]:
                for s, v in waits:
                    e.wait_ge(sems[s], v)
                if fn is None:
                    continue
                ins = fn(e)
                ins.then_inc(sems[incsem], incval)

        with nc.Block() as block:
            @block.tensor
            def _(e):
                run('pe', e)

            @block.scalar
            def _(e):
                run('act', e)

            @block.vector
            def _(e):
                run('dve', e)

            @block.gpsimd
            def _(e):
                run('pool', e)

            @block.sync
            def _(e):
                run('sp', e)


D = 1024
NT = 1536
NG = 3
NTILE = 12
DFF = 2816
NJ = 22
Q_OFF, K_OFF, V_OFF, O_OFF, IG_OFF = 0, 1024, 2048, 3072, 4096
B_OFF = 4112
C_OFF = B_OFF + 1024
X_OFF = C_OFF + 1024
GML_OFF = X_OFF + 1024
GSC_OFF = GML_OFF + 1024
V_BMOD, V_NG, V_BIN, V_CW, V_CB, V_MLN, V_BK16 = 0, 72, 96, 168, 192, 200, 208
NV = 216
RING = 3
RING_ELEMS = 4096
ARENA_F32 = 19600
NFG = 3
BIGNEG = -30000.0
DEN_FAST = False
NRT = 136

DEBUG = {}


def _bin_col(off):
    return V_BIN + (off // 128 if off < 4096 else 32 + (off - 4112) // 128)


class Arena:
    def __init__(self, h32):
        self.h32 = h32
        self.h16 = h32.bitcast(BF16)
        self.cap = h32.shape[1] * 4
        self.off = 0

    def reset(self):
        self.off = 0

    def alloc(self, free_shape, dt, parts=128):
        n = 1
        for v in free_shape:
            n *= v
        size = 4 if dt == F32 else 2
        nbytes = (n * size + 31) // 32 * 32
        assert self.off + nbytes <= self.cap, ("arena overflow", self.off, nbytes, self.cap)
        base = self.off // size
        h = self.h32 if dt == F32 else self.h16
        ap = h[0:parts, base:base + n]
        self.off += nbytes
        if len(free_shape) > 1:
            names = "abcd"[:len(free_shape)]
            ap = ap.rearrange("p (%s) -> p %s" % (" ".join(names), " ".join(names)),
                              **{names[i]: free_shape[i] for i in range(len(free_shape))})
        return ap


def build_program(dbg=None):
    dbg = dbg or []
    LVL = DEBUG.get('stop', float(os.environ.get('KLVL', '99')))
    nc = bass.Bass("TRN2", target_bir_lowering=False)
    din = lambda name, shape: nc.dram_tensor(name, shape, F32, kind="ExternalInput").ap()
    dout = lambda name, shape: nc.dram_tensor(name, shape, F32, kind="ExternalOutput").ap()
    xin = din("xin", [NT, D])
    xfor = din("xfor", [NFG * 1024, D])
    condT_d = din("condT", [128, 16])
    vecs_d = din("vecs", [128, NV])
    ident_d = din("ident", [128, 128])
    ntriF_d = din("ntriF", [128, 128])
    ntriB_d = din("ntriB", [128, 128])
    fnbc_d = din("fnbc", [128, D])
    gb12_d = din("gb12", [128, 192])
    rowtab_d = din("rowtab", [1, NRT])
    bkrow_d = din("bkrow", [128, 1024])
    bvrow_d = din("bvrow", [128, 1024])
    c0ext_d = din("c0ext", [8, 2, 128, 257])
    w_mod = din("w_mod", [D, 9 * D])
    f1w1 = din("ffn1_w1", [D, 2 * DFF])
    f1w2 = din("ffn1_w2", [DFF, D])
    f2w1 = din("ffn2_w1", [D, 2 * DFF])
    f2w2 = din("ffn2_w2", [DFF, D])
    w_in = din("w_in", [D, 9232])
    w_ml = din("w_ml_out", [D, D])
    w_sc = din("w_sc_out", [D, D])
    w_o = din("w_o", [D, D])
    yout = dout("yout", [NT, D])
    outC = dout("outC", [2, 2, 4, 256, 256])
    outN = dout("outN", [2, 2, 4, 256])
    outM = dout("outM", [1, 16])
    aggC = nc.dram_tensor("aggC", [8, NFG, 2, 128, 257], F32).ap()
    cinS = nc.dram_tensor("cinS", [8, 2, 128, 257], F32).ap()
    dbg_out = {}
    for nm in dbg:
        if nm in ('x1', 'hmix', 'x2', 'x3', 'ohm', 'ymix'):
            dbg_out[nm] = dout("dbg_" + nm, [128, 8 * NT])
        elif nm == 'mod':
            dbg_out[nm] = dout("dbg_mod", [128, 144])
        elif nm == 'gates':
            dbg_out[nm] = dout("dbg_gates", [128, 2 * 96 + 96 + 192])
        elif nm == 'rows':
            dbg_out[nm] = dout("dbg_rows", [1, 6 * 96 + 8 + NFG * 16])
        elif nm == 'cin':
            dbg_out[nm] = dout("dbg_cin", [8, 2, 128, 257])
        elif nm == 'hmraw':
            dbg_out[nm] = dout("dbg_hmraw", [4, 128, 12 * 256])

    with ExitStack() as st:
        S = Sched(nc, st)
        T = lambda name, shape, dt: st.enter_context(nc.sbuf_tensor("sb_" + name, shape, dt))
        PS = [st.enter_context(nc.psum_tensor("ps%d" % i, [128, 512], F32)) for i in range(8)]
        PSB = [p.bitcast(BF16) for p in PS]
        state = {'ps': 0, 'ring': 0, 'ev': 0}

        def psum():
            b = state['ps']
            state['ps'] = (b + 1) % 8
            return b

        pools = {'S': [0, 1], 'A': [2, 3, 4], 'B': [5, 6, 7]}
        pidx = {'S': 0, 'A': 0, 'B': 0}

        def pool_bank(name):
            b = pools[name][pidx[name] % len(pools[name])]
            pidx[name] += 1
            return b

        def evq():
            state['ev'] ^= 1
            return 'act' if state['ev'] else 'dve'

        x = T("x", [128, 8, NT], F32)
        hT = T("hT", [128, 8, NT], BF16)
        arena_t = T("arena", [128, ARENA_F32], F32)
        AR = Arena(arena_t)
        wr = [T("wr%d" % i, [128, RING_ELEMS], BF16) for i in range(RING)]
        vecs = T("vecs", [128, NV], F32)
        ident = T("ident", [128, 128], F32)
        identb = T("identb", [128, 128], BF16)
        onesb = T("onesb", [128, 128], BF16)
        onesf = T("onesf", [1, 128], F32)
        ntriF = T("ntriF", [128, 128], F32)
        ntriB = T("ntriB", [128, 128], F32)
        maskF = T("maskF", [128, 128], BF16)
        maskB = T("maskB", [128, 128], BF16)
        gb12 = T("gb12", [128, 192], F32)
        condT = T("condT", [128, 16], F32)
        scond = T("scond", [128, 16], BF16)
        modT = T("modT", [128, 72, 2], F32)
        tabs = T("tabs", [128, 9, 8, 2], F32)
        xstage = [T("xstage%d" % i, [128, D], F32) for i in range(2)]
        sqb = T("sqb", [128, 2, 512], BF16)
        rstd = T("rstd", [128, 512], F32)
        rstd2 = T("rstd2", [128, 512], F32)
        tmpf = [T("tmpf%d" % i, [128, 512], F32) for i in range(2)]
        small = T("small", [128, 64], F32)
        wg = T("wg", [128, 8, 16], BF16)
        GT = T("GT", [128, 192], F32)
        SPt = T("SPt", [128, 96], F32)
        RBt = T("RBt", [128, 2, 96], F32)
        ROW = T("ROW", [96, 2, 128], F32)
        ROW2 = T("ROW2", [96, 2, 128], F32)
        COLS = T("COLS", [96, 4], F32)
        P0 = T("P0", [1, 2, 96], F32)
        MMr = T("MMr", [1, 96], F32)
        RRr = T("RRr", [1, 96], F32)
        DECr = T("DECr", [1, 96], F32)
        MFIN = T("MFIN", [1, 3, 8], F32)
        WT = T("WT", [128, 2, 96], F32)
        DB = T("DB", [128, 96], F32)
        AGS = T("AGS", [1, NFG, 2, 8], F32)
        rowtab = T("rowtab", [1, NRT], F32)
        CROW = T("CROW", [1, 256], F32)
        CB = T("CB", [128, 32], F32)
        MROW = T("MROW", [1, 16], F32)

        S.dma('sp', vecs[:], vecs_d, 'd_c0', writes=['vecs'])
        S.dma('sp', ident[:], ident_d, 'd_c1', writes=['ident'])
        S.dma('sp', condT[:], condT_d, 'd_c2', writes=['condT'])
        S.dma('sp', ntriF[:], ntriF_d, 'd_c4', writes=['ntriF'])
        S.dma('sp', ntriB[:], ntriB_d, 'd_c5', writes=['ntriB'])
        S.dma('sp', gb12[:], gb12_d, 'd_c6', writes=['gb12'])
        S.dma('sp', rowtab[:], rowtab_d, 'd_c7', writes=['rowtab'])
        S.dma('sp', tmpf[0][:, 0:128].rearrange("p (k n) -> p k n", k=8), w_in.rearrange("(kc p) n -> p kc n", p=128)[:, :, IG_OFF:IG_OFF + 16], 'd_c8', writes=[('tmpf', 0)])
        S.op('dve', lambda e: e.tensor_copy(out=wg[:], in_=tmpf[0][:, 0:128].rearrange("p (k n) -> p k n", k=8)), reads=[('tmpf', 0)], writes=['wg'])
        S.op('dve', lambda e: e.tensor_copy(out=identb[:], in_=ident[:]), reads=['ident'], writes=['identb'])
        S.op('dve', lambda e: e.memset(onesb[:], 1.0), writes=['onesb'])
        S.op('dve', lambda e: e.memset(onesf[:], 1.0), writes=['onesf'])
        S.op('dve', lambda e: e.tensor_scalar(out=maskF[:], in0=ntriF[:], scalar1=-1.0, scalar2=None, op0=ALU.mult), reads=['ntriF'], writes=['maskF'])
        S.op('dve', lambda e: e.tensor_scalar(out=maskB[:], in0=ntriB[:], scalar1=-1.0, scalar2=None, op0=ALU.mult), reads=['ntriB'], writes=['maskB'])
        S.op('dve', lambda e: e.tensor_scalar(out=vecs[:, V_BK16:V_BK16 + 8], in0=vecs[:, V_BIN + 8:V_BIN + 16], scalar1=1.0 / 16, scalar2=None, op0=ALU.mult),
             reads=['vecs'], writes=['vecs'])
        S.op('act', lambda e: e.activation(out=scond[:], in_=condT[:], func=AF.Silu), reads=['condT'], writes=['scond'])

        def load_panel(srcs, shape_str, **dims):
            if not isinstance(srcs, (list, tuple)):
                srcs = [srcs]
            s = state['ring']
            state['ring'] = (s + 1) % RING
            n = 1
            for d_ in srcs[0].shape[1:]:
                n *= d_
            assert n * len(srcs) <= RING_ELEMS
            inner = srcs[0].shape[-1]
            for i, src in enumerate(srcs):
                dst = wr[s][:, i * n:(i + 1) * n].rearrange("p (a n) -> p a n", n=inner)
                wk = [('wr', s, i)] if len(srcs) == 2 else [('wr', s, 0), ('wr', s, 1)]
                S.dma('pool', dst, src, 'd_w%d' % s, writes=wk)
            if len(srcs) == 2:
                S.lastw[('wr', s, 0)] = S.lastw[('wr', s, 1)]
            full = wr[s][:, 0:n * len(srcs)]
            if shape_str is not None:
                full = full.rearrange(shape_str, **dims)
            return s, full

        def WR(s):
            return [('wr', s, 0), ('wr', s, 1)]

        def keys(name, n=8, g=None):
            return [(name, kc, g) for kc in range(n)]

        def load_x(src, ntiles, xbuf, xkey):
            for t in range(ntiles):
                xs_i = t % 2
                S.dma('sp', xstage[xs_i][:], src[t * 128:(t + 1) * 128, :], 'd_x%d' % xs_i, writes=[('xstage', xs_i)])
                g = t // 4
                for half in range(2):
                    b = psum()

                    def f(e, half=half, xs_i=xs_i, b=b):
                        for i in range(4):
                            kc = half * 4 + i
                            ins = e.transpose(PS[b][:, i * 128:(i + 1) * 128], xstage[xs_i][:, kc * 128:(kc + 1) * 128], ident[:])
                        return ins
                    S.op('pe', f, reads=[('xstage', xs_i), 'ident'], writes=[('ps', b)])
                    dst = xbuf[:, half * 4:(half + 1) * 4, t * 128:(t + 1) * 128]
                    src_ps = PS[b][:].rearrange("p (a c) -> p a c", a=4)
                    wk = [(xkey, kc, g) for kc in range(half * 4, half * 4 + 4)]
                    if half == 0:
                        S.op('act', lambda e, dst=dst, src_ps=src_ps: e.activation(out=dst, in_=src_ps, func=AF.Copy), reads=[('ps', b)], writes=wk)
                    else:
                        S.op('dve', lambda e, dst=dst, src_ps=src_ps: e.tensor_copy(out=dst, in_=src_ps), reads=[('ps', b)], writes=wk)


        if LVL >= 1:
            load_x(xfor[0:1024, :], 8, x[:, :, 0:1024], 'x')
        scv = scond[:].rearrange("p (k m) -> p k m", k=8)
        wmv = w_mod.rearrange("(kc p) n -> p kc n", p=128)
        def mod_panels(pis):
          for pi in pis:
            s, wp = load_panel(wmv[:, :, pi * 512:(pi + 1) * 512], "p (k n) -> p k n", k=8)
            b = psum()

            def f(e, wp=wp, b=b):
                for jb in range(4):
                    for kc in range(8):
                        ins = e.matmul(PS[b][:, jb * 2:jb * 2 + 2], wp[:, kc, jb * 128:(jb + 1) * 128], scv[:, kc, :],
                                       start=(kc == 0), stop=(kc == 7))
                return ins
            S.op('pe', f, reads=WR(s) + ['scond'], writes=[('ps', b)])
            for mi in range(2):
                S.op('dve', lambda e, b=b, mi=mi, pi=pi: e.tensor_tensor(
                    out=modT[:, pi * 4:(pi + 1) * 4, mi], in0=PS[b][:, 0:8].rearrange("p (j m) -> p j m", m=2)[:, :, mi],
                    in1=vecs[:, V_BMOD + pi * 4:V_BMOD + (pi + 1) * 4], op=ALU.add),
                    reads=[('ps', b), 'vecs'], writes=[('modT', pi, mi)])
        MODK = [('modT', pi, mi) for pi in range(18) for mi in range(2)]
        mv = modT[:].rearrange("p (c k) m -> p c k m", c=9)

        def derive_tabs(groups, mk):
            for mi in range(2):
                for (ti, ci_scale, ci_shift, ci_gate, ngi, half) in groups:
                    S.op('dve', lambda e, mi=mi, ti=ti, ci_scale=ci_scale, ngi=ngi: e.scalar_tensor_tensor(
                        out=tabs[:, ti, :, mi], in0=mv[:, ci_scale, :, mi], scalar=1.0, in1=vecs[:, V_NG + ngi * 8:V_NG + ngi * 8 + 8],
                        op0=ALU.add, op1=ALU.mult), reads=mk + ['vecs'], writes=[('tabs', ti)])
                    S.op('dve', lambda e, mi=mi, ti=ti, ci_shift=ci_shift: e.tensor_copy(out=tabs[:, ti + 1, :, mi], in_=mv[:, ci_shift, :, mi]),
                         reads=mk, writes=[('tabs', ti + 1)])
                    S.op('dve', lambda e, mi=mi, ti=ti, ci_gate=ci_gate, half=half: e.tensor_scalar(
                        out=tabs[:, ti + 2, :, mi], in0=mv[:, ci_gate, :, mi], scalar1=half, scalar2=None, op0=ALU.mult),
                        reads=mk, writes=[('tabs', ti + 2)])

        def derive_one(ti, kind, chunk, ngi=0, half=1.0):
            mk = [('modT', pi, mi) for pi in (2 * chunk, 2 * chunk + 1) for mi in range(2)]
            for mi in range(2):
                if kind == 'A':
                    S.op('dve', lambda e, mi=mi: e.scalar_tensor_tensor(
                        out=tabs[:, ti, :, mi], in0=mv[:, chunk, :, mi], scalar=1.0, in1=vecs[:, V_NG + ngi * 8:V_NG + ngi * 8 + 8],
                        op0=ALU.add, op1=ALU.mult), reads=mk + ['vecs'], writes=[('tabs', ti)])
                elif kind == 'B':
                    S.op('dve', lambda e, mi=mi: e.tensor_copy(out=tabs[:, ti, :, mi], in_=mv[:, chunk, :, mi]), reads=mk, writes=[('tabs', ti)])
                else:
                    S.op('dve', lambda e, mi=mi: e.tensor_scalar(out=tabs[:, ti, :, mi], in0=mv[:, chunk, :, mi], scalar1=half, scalar2=None, op0=ALU.mult),
                         reads=mk, writes=[('tabs', ti)])
        mod_panels(range(4))
        derive_one(0, 'A', 1, ngi=0)
        derive_one(1, 'B', 0)
        mod_jobs = [
            lambda: mod_panels([4]),
            lambda: (mod_panels([5]), derive_one(2, 'G', 2, half=0.5)),
            lambda: mod_panels([6]),
            lambda: (mod_panels([7]), derive_one(4, 'B', 3)),
            lambda: mod_panels([8]),
            lambda: (mod_panels([9]), derive_one(3, 'A', 4, ngi=1)),
            lambda: mod_panels([10]),
            lambda: (mod_panels([11]), derive_one(5, 'G', 5, half=1.0)),
        ]
        if LVL < 1:
            while mod_jobs:
                mod_jobs.pop(0)()

        def mod_ap(ti, kc, mi):
            return tabs[:, ti, kc, mi:mi + 1]

        own_mi = lambda g: 0 if g == 0 else 1
        for_mi = lambda g: 1

        def norm_mod(xbuf, hbuf, ngroups, tiA, tiB, mi_of, xkey, hkey):
            RS = [rstd, rstd2]
            banks = []
            for g in range(ngroups):
                gs = slice(g * 512, (g + 1) * 512)
                b = psum()
                banks.append(b)
                for kc in range(8):
                    S.op('act', lambda e, kc=kc, gs=gs: e.activation(out=sqb[:, kc % 2, :], in_=xbuf[:, kc, gs], func=AF.Square),
                         reads=[(xkey, kc, g)], writes=[('sqb', kc % 2)])
                    S.op('pe', lambda e, kc=kc, b=b: e.matmul(PS[b][:], onesb[:], sqb[:, kc % 2, :], start=(kc == 0), stop=(kc == 7)),
                         reads=[('sqb', kc % 2), 'onesb'], writes=[('ps', b)])

            def root(g):
                rs = RS[g % 2]
                S.op('act', lambda e, b=banks[g], rs=rs: e.activation(out=rs[:], in_=PS[b][:], func=AF.Sqrt, bias=1e-6, scale=1.0 / D),
                     reads=[('ps', banks[g])], writes=[('rstd', g % 2)])
            for g in range(min(2, ngroups)):
                root(g)
            for g in range(ngroups):
                gs = slice(g * 512, (g + 1) * 512)
                mi = mi_of(g)
                rs = RS[g % 2]
                rk = ('rstd', g % 2)
                S.op('dve', lambda e, rs=rs: e.reciprocal(out=rs[:], in_=rs[:]), reads=[rk], writes=[rk])
                for kc in range(8):
                    ti_ = kc % 2
                    S.op('dve', lambda e, kc=kc, gs=gs, ti_=ti_, mi=mi, rs=rs: e.scalar_tensor_tensor(
                        out=tmpf[ti_][:], in0=xbuf[:, kc, gs], scalar=mod_ap(tiA, kc, mi), in1=rs[:], op0=ALU.mult, op1=ALU.mult),
                        reads=[(xkey, kc, g), ('tabs', tiA), rk], writes=[('tmpf', ti_)])
                    S.op('act', lambda e, kc=kc, gs=gs, ti_=ti_, mi=mi: e.activation(
                        out=hbuf[:, kc, gs], in_=tmpf[ti_][:], func=AF.Identity, bias=mod_ap(tiB, kc, mi), scale=1.0),
                        reads=[('tmpf', ti_), ('tabs', tiB)], writes=[(hkey, kc, g)])
                if g + 2 < ngroups:
                    root(g + 2)

        def ffn(xbuf, hbuf, hid, ngroups, w1, w2, tiHG, mi_of, xkey, hkey, hidkey, hook=None):
            w1v = w1.rearrange("(kc p) n -> p kc n", p=128)
            for jp in range(NJ // 2):
                if hook is not None:
                    hook()
                s, wp = load_panel([w1v[:, :, jp * 256:(jp + 1) * 256], w1v[:, :, DFF + jp * 256:DFF + (jp + 1) * 256]],
                                   "p (two k n) -> p two k n", two=2, k=8)
                for jj in range(2):
                    j = jp * 2 + jj
                    for g in range(ngroups):
                        gs = slice(g * 512, (g + 1) * 512)
                        ba = psum()
                        bb = psum()

                        def f(e, wp=wp, jj=jj, gs=gs, ba=ba, bb=bb):
                            for ab, bk in ((0, ba), (1, bb)):
                                for kc in range(8):
                                    ins = e.matmul(PS[bk][:], wp[:, ab, kc, jj * 128:(jj + 1) * 128], hbuf[:, kc, gs],
                                                   start=(kc == 0), stop=(kc == 7))
                            return ins
                        S.op('pe', f, reads=WR(s) + [(hkey, kc, g) for kc in range(8)], writes=[('ps', ba), ('ps', bb)])
                        ti_ = g % 2
                        S.op('act', lambda e, ba=ba, ti_=ti_: e.activation(out=tmpf[ti_][:], in_=PS[ba][:], func=AF.Silu),
                             reads=[('ps', ba)], writes=[('tmpf', ti_)])
                        S.op('dve', lambda e, bb=bb, ti_=ti_, j=j, gs=gs: e.tensor_tensor(
                            out=hid[:, j, gs], in0=tmpf[ti_][:], in1=PS[bb][:], op=ALU.mult),
                            reads=[('tmpf', ti_), ('ps', bb)], writes=[(hidkey, j, g)])
            w2v = w2.rearrange("(kc p) n -> p kc n", p=128)
            for cb in range(8):
                if hook is not None:
                    hook()
                s, wp = load_panel(w2v[:, :, cb * 128:(cb + 1) * 128], "p (k n) -> p k n", k=NJ)
                for g in range(ngroups):
                    gs = slice(g * 512, (g + 1) * 512)
                    mi = mi_of(g)
                    b = psum()

                    def f(e, wp=wp, gs=gs, b=b):
                        for j in range(NJ):
                            ins = e.matmul(PS[b][:], wp[:, j, :], hid[:, j, gs], start=(j == 0), stop=(j == NJ - 1))
                        return ins
                    S.op('pe', f, reads=WR(s) + [(hidkey, j, g) for j in range(NJ)], writes=[('ps', b)])
                    S.op('dve', lambda e, b=b, cb=cb, gs=gs, mi=mi: e.scalar_tensor_tensor(
                        out=xbuf[:, cb, gs], in0=PS[b][:], scalar=mod_ap(tiHG, cb, mi), in1=xbuf[:, cb, gs], op0=ALU.mult, op1=ALU.add),
                        reads=[('ps', b), ('tabs', tiHG), (xkey, cb, g)], writes=[(xkey, cb, g)])

        def proj_fm(s, wsl, nblk, rhs, rkey, nk, ngroups, evac):
            for blk in range(nblk):
                for g in range(ngroups):
                    gs = slice(g * 512, (g + 1) * 512)
                    b = psum()

                    def f(e, blk=blk, gs=gs, b=b):
                        for kc in range(nk):
                            ins = e.matmul(PS[b][:], wsl(kc, blk), rhs[:, kc, gs], start=(kc == 0), stop=(kc == nk - 1))
                        return ins
                    S.op('pe', f, reads=WR(s) + [(rkey, kc, g) for kc in range(nk)], writes=[('ps', b)])
                    evac(blk, g, gs, b)

        w_inv = w_in.rearrange("(kc p) n -> p kc n", p=128)

        def gates_front(hbuf, hkey, nt):
            n8 = nt * 8
            b = psum()

            def f(e, b=b):
                for t in range(nt):
                    for kc in range(8):
                        ins = e.matmul(PS[b][:, t * 16:(t + 1) * 16], hbuf[:, kc, t * 128:(t + 1) * 128], wg[:, kc, :],
                                       start=(kc == 0), stop=(kc == 7))
                return ins
            S.op('pe', f, reads=[(hkey, kc, g) for kc in range(8) for g in range((nt + 3) // 4)] + ['wg'], writes=[('ps', b)])
            S.op('dve', lambda e, b=b: e.tensor_tensor(out=GT[:, 0:nt * 16], in0=PS[b][:, 0:nt * 16], in1=gb12[:, 0:nt * 16], op=ALU.add),
                 reads=[('ps', b), 'gb12'], writes=['GT'])
            GTv = GT[:, 0:nt * 16].rearrange("p (t c) -> p t c", c=16)
            SPv = SPt[:, 0:n8].rearrange("p (t c) -> p t c", c=8)
            S.op('act', lambda e: e.activation(out=SPv, in_=GTv[:, :, 8:16], func=AF.Exp, scale=-1.0), reads=['GT'], writes=['SPt'])
            S.op('act', lambda e: e.activation(out=SPt[:, 0:n8], in_=SPt[:, 0:n8], func=AF.Ln, bias=1.0, scale=1.0), reads=['SPt'], writes=['SPt'])
            b2 = psum()

            def f2(e, b2=b2):
                for t in range(nt):
                    ins = e.matmul(PS[b2][:, t * 8:t * 8 + 4], ntriF[:], SPv[:, t, 0:4], start=True, stop=True)
                    ins = e.matmul(PS[b2][:, t * 8 + 4:t * 8 + 8], ntriB[:], SPv[:, t, 4:8], start=True, stop=True)
                return ins
            S.op('pe', f2, reads=['SPt', 'ntriF', 'ntriB'], writes=[('ps', b2)])
            S.op('dve', lambda e, b2=b2: e.tensor_tensor(out=RBt[:, 0, 0:n8].rearrange("p (t c) -> p t c", c=8), in0=GTv[:, :, 0:8],
                                                        in1=PS[b2][:, 0:n8].rearrange("p (t c) -> p t c", c=8), op=ALU.subtract),
                 reads=['GT', ('ps', b2)], writes=['RBt'])
            S.op('act', lambda e, b2=b2: e.activation(out=RBt[:, 1, 0:n8], in_=PS[b2][:, 0:n8], func=AF.Copy), reads=[('ps', b2)], writes=['RBt1'])
            b3 = psum()

            def f3(e, b3=b3):
                e.transpose(PS[b3][0:n8, 0:128], RBt[:, 0, 0:n8], ident[:])
                return e.transpose(PS[b3][0:n8, 128:256], RBt[:, 1, 0:n8], ident[:])
            S.op('pe', f3, reads=['RBt', 'RBt1', 'ident'], writes=[('ps', b3)])
            S.op('dve', lambda e, b3=b3: e.tensor_copy(out=ROW[0:n8].rearrange("p a c -> p (a c)"), in_=PS[b3][0:n8, 0:256]), reads=[('ps', b3)], writes=['ROW'])
            S.op('dve', lambda e: e.tensor_reduce(out=COLS[0:n8, 0:1], in_=ROW[0:n8, 0, :], axis=AX.X, op=ALU.max), reads=['ROW'], writes=['COLS'])
            S.op('dve', lambda e: e.tensor_reduce(out=COLS[0:n8, 1:2], in_=ROW[0:n8, 1, :], axis=AX.X, op=ALU.min), reads=['ROW'], writes=['COLS'])
            b4 = psum()

            def f4(e, b4=b4):
                e.matmul(PS[b4][0:1, 0:n8], COLS[0:n8, 0:1], ident[0:n8, 0:n8], start=True, stop=True)
                return e.matmul(PS[b4][0:1, 96:96 + n8], COLS[0:n8, 1:2], ident[0:n8, 0:n8], start=True, stop=True)
            S.op('pe', f4, reads=['COLS', 'ident'], writes=[('ps', b4)])
            S.op('dve', lambda e, b4=b4: e.tensor_copy(out=P0[:].rearrange("p a c -> p (a c)"), in_=PS[b4][0:1, 0:192]), reads=[('ps', b4)], writes=['P0'])

        RM = P0[0:1, 0, :]
        BL = P0[0:1, 1, :]

        def chain_recur(tiles, j0, init, fin_ap):
            for i, t in enumerate(tiles):
                sl = slice(t * 8 + j0, t * 8 + j0 + 4)
                if i == 0:
                    if isinstance(init, float):
                        S.op('dve', lambda e, sl=sl: e.memset(MMr[0:1, sl], init), writes=['MMr'])
                    else:
                        S.op('dve', lambda e, sl=sl: e.tensor_copy(out=MMr[0:1, sl], in_=init), reads=['MIN'], writes=['MMr'])
                S.op('dve', lambda e, sl=sl: e.tensor_tensor(out=RRr[0:1, sl], in0=MMr[0:1, sl], in1=RM[:, sl], op=ALU.max),
                     reads=['MMr', 'P0'], writes=['RRr'])
                if i + 1 < len(tiles):
                    t2 = tiles[i + 1]
                    dst = MMr[0:1, t2 * 8 + j0:t2 * 8 + j0 + 4]
                else:
                    dst = fin_ap
                S.op('dve', lambda e, sl=sl, dst=dst: e.tensor_tensor(out=dst, in0=RRr[0:1, sl], in1=BL[:, sl], op=ALU.add),
                     reads=['RRr', 'P0'], writes=['MMr', 'MFIN', 'AGS'])

        def gates_finish(nt):
            n8 = nt * 8
            S.op('dve', lambda e: e.tensor_tensor(out=DECr[0:1, 0:n8], in0=MMr[0:1, 0:n8], in1=RRr[0:1, 0:n8], op=ALU.subtract),
                 reads=['MMr', 'RRr'], writes=['DECr'])
            S.op('act', lambda e: e.activation(out=DECr[0:1, 0:n8], in_=DECr[0:1, 0:n8], func=AF.Exp), reads=['DECr'], writes=['DECr'])
            b = psum()
            S.op('pe', lambda e, b=b: e.matmul(PS[b][0:n8, 0:1], RRr[0:1, 0:n8], ident[0:1, 0:1], start=True, stop=True),
                 reads=['RRr', 'ident'], writes=[('ps', b)])
            S.op('dve', lambda e, b=b: e.tensor_scalar(out=COLS[0:n8, 2:3], in0=PS[b][0:n8, 0:1], scalar1=-1.0, scalar2=None, op0=ALU.mult),
                 reads=[('ps', b)], writes=['NEGR'])
            S.op('act', lambda e: e.activation(out=ROW2[0:n8, 0, :], in_=ROW[0:n8, 0, :], func=AF.Exp, bias=COLS[0:n8, 2:3], scale=1.0),
                 reads=['ROW', 'NEGR'], writes=['ROW2'])
            S.op('act', lambda e: e.activation(out=ROW2[0:n8, 1, :], in_=ROW[0:n8, 1, :], func=AF.Exp, bias=COLS[0:n8, 2:3], scale=-1.0),
                 reads=['ROW', 'NEGR'], writes=['ROW2b'])
            b2 = psum()

            def f(e, b2=b2):
                e.transpose(PS[b2][:, 0:n8], ROW2[0:n8, 0, :], ident[0:n8, 0:n8])
                return e.transpose(PS[b2][:, 96:96 + n8], ROW2[0:n8, 1, :], ident[0:n8, 0:n8])
            S.op('pe', f, reads=['ROW2', 'ROW2b', 'ident'], writes=[('ps', b2)])
            S.op('dve', lambda e, b2=b2: e.tensor_copy(out=WT[:].rearrange("p a c -> p (a c)"), in_=PS[b2][:, 0:192]), reads=[('ps', b2)], writes=['WT'])
            b3 = psum()
            S.op('pe', lambda e, b3=b3: e.matmul(PS[b3][:, 0:n8], onesf[0:1, :], DECr[0:1, 0:n8], start=True, stop=True),
                 reads=['onesf', 'DECr'], writes=[('ps', b3)])
            S.op('act', lambda e, b3=b3: e.activation(out=DB[:, 0:n8], in_=PS[b3][:, 0:n8], func=AF.Copy), reads=[('ps', b3)], writes=['DB'])

        def kv_head(h, hbuf, hkey, ngroups, nt, kT, vT, ktok, vext, pfx, alias=()):
            s, wp = load_panel([w_inv[:, :, K_OFF + h * 256:K_OFF + (h + 1) * 256], w_inv[:, :, V_OFF + h * 256:V_OFF + (h + 1) * 256]],
                               "p (two k n) -> p two k n", two=2, k=8)

            def ev_k(blk, g, gs, b):
                S.op('act', lambda e: e.activation(out=kT[:, blk, gs], in_=PS[b][:], func=AF.Identity, bias=vecs[:, V_BK16 + 2 * h + blk:V_BK16 + 2 * h + blk + 1], scale=1.0 / 16),
                     reads=[('ps', b), 'vecs'], writes=[(pfx + 'kT', g)])

            def ev_v(blk, g, gs, b):
                c = _bin_col(V_OFF) + 2 * h + blk
                S.op('dve', lambda e: e.tensor_scalar(out=vT[:, blk, gs], in0=PS[b][:], scalar1=vecs[:, c:c + 1], scalar2=None, op0=ALU.add),
                     reads=[('ps', b), 'vecs'], writes=[(pfx + 'vT', g)] + list(alias))
            proj_fm(s, lambda kc, blk: wp[:, 0, kc, blk * 128:(blk + 1) * 128], 2, hbuf, hkey, 8, ngroups, ev_k)
            proj_fm(s, lambda kc, blk: wp[:, 1, kc, blk * 128:(blk + 1) * 128], 2, hbuf, hkey, 8, ngroups, ev_v)
            for t2 in range(0, nt, 2):
                g = t2 // 4
                bK = psum()
                bV = psum()

                def f(e, bK=bK, bV=bV, t2=t2):
                    for tt in range(2):
                        ts_ = slice((t2 + tt) * 128, (t2 + tt + 1) * 128)
                        for dc in range(2):
                            e.matmul(PS[bK][:, tt * 256 + dc * 128:tt * 256 + (dc + 1) * 128], kT[:, dc, ts_], identb[:], start=True, stop=True)
                    for tt in range(2):
                        ts_ = slice((t2 + tt) * 128, (t2 + tt + 1) * 128)
                        for dc in range(2):
                            ins = e.matmul(PS[bV][:, tt * 256 + dc * 128:tt * 256 + (dc + 1) * 128], vT[:, dc, ts_], identb[:], start=True, stop=True)
                    return ins
                S.op('pe', f, reads=[(pfx + 'kT', g), (pfx + 'vT', g), 'identb'] + list(alias), writes=[('ps', bK), ('ps', bV)])
                S.op('act', lambda e, bK=bK, t2=t2: e.activation(out=ktok[:, t2:t2 + 2, :], in_=PS[bK][:].rearrange("p (a c) -> p a c", a=2), func=AF.Copy),
                     reads=[('ps', bK)], writes=[(pfx + 'ktok', t2), (pfx + 'ktok', t2 + 1)])
                S.op('dve', lambda e, bV=bV, t2=t2: e.tensor_copy(out=vext[:, t2:t2 + 2, 0:256], in_=PS[bV][:].rearrange("p (a c) -> p a c", a=2)),
                     reads=[('ps', bV)], writes=[(pfx + 'vext', t2), (pfx + 'vext', t2 + 1)])

        def chain_step(t, j, h, Cst, ckey, Vp, vpkey, ktok, vext, pfx, outputs=None):
            col = t * 8 + j
            S.op('act', lambda e: e.activation(out=Vp, in_=vext[:, t, :], func=AF.Identity, scale=WT[:, 0, col:col + 1]),
                 reads=[(pfx + 'vext', t), 'WT'], writes=[vpkey])
            if outputs is not None:
                qT, kT, Dbf, dkey, sTm, skey, hm, first, mask = outputs
                ts_ = slice(t * 128, (t + 1) * 128)
                g = t // 4
                S.op('act', lambda e: e.activation(out=Dbf, in_=Cst, func=AF.Identity, scale=DB[:, col:col + 1]), reads=[ckey, 'DB'], writes=[dkey])
                bs = psum()

                def f(e, bs=bs):
                    for dc in range(2):
                        ins = e.matmul(PS[bs][:, 0:128], kT[:, dc, ts_], qT[:, dc, ts_], start=(dc == 0), stop=(dc == 1))
                    return ins
                S.op('pe', f, reads=[(pfx + 'kT', g), (pfx + 'qT', g)], writes=[('ps', bs)])
                S.op('dve', lambda e, bs=bs: e.tensor_tensor(out=sTm, in0=PS[bs][:, 0:128], in1=mask[:], op=ALU.mult),
                     reads=[('ps', bs), 'maskF', 'maskB'], writes=[skey])
                bn = psum()

                def f2(e, bn=bn):
                    e.matmul(PS[bn][:, 0:257], sTm, Vp, start=True, stop=False)
                    e.matmul(PS[bn][:, 0:257], qT[:, 0, ts_], Dbf[:, 0, :], start=False, stop=False)
                    return e.matmul(PS[bn][:, 0:257], qT[:, 1, ts_], Dbf[:, 1, :], start=False, stop=True)
                S.op('pe', f2, reads=[skey, vpkey, dkey, (pfx + 'qT', g)], writes=[('ps', bn)])
                dcol = small[:, 32 + (col % 16):33 + (col % 16)]
                dk = ('den', col % 16)
                S.op('dve', lambda e, bn=bn: e.tensor_scalar(out=dcol, in0=PS[bn][:, 256:257], scalar1=-1.0, scalar2=WT[:, 1, col:col + 1],
                                                          op0=ALU.mult, op1=ALU.max), reads=[('ps', bn), 'WT'], writes=[dk])
                S.op('dve', lambda e, bn=bn: e.tensor_tensor(out=dcol, in0=dcol, in1=PS[bn][:, 256:257], op=ALU.max), reads=[('ps', bn), dk], writes=[dk])
                S.op('dve', lambda e: e.reciprocal(out=dcol, in_=dcol), reads=[dk], writes=[dk])
                if first:
                    S.op('act', lambda e, bn=bn: e.activation(out=hm[:, t, :], in_=PS[bn][:, 0:256], func=AF.Identity, scale=dcol),
                         reads=[('ps', bn), dk], writes=[(pfx + 'hm', t)])
                else:
                    S.op('dve', lambda e, bn=bn: e.scalar_tensor_tensor(out=hm[:, t, :], in0=PS[bn][:, 0:256], scalar=dcol, in1=hm[:, t, :],
                                                                     op0=ALU.mult, op1=ALU.add), reads=[('ps', bn), dk, (pfx + 'hm', t)], writes=[(pfx + 'hm', t)])
            b0 = psum()
            b1 = psum()

            def f3(e, b0=b0, b1=b1):
                e.matmul(PS[b0][:, 0:257], ktok[:, t, 0:128], Vp, start=True, stop=True)
                return e.matmul(PS[b1][:, 0:257], ktok[:, t, 128:256], Vp, start=True, stop=True)
            S.op('pe', f3, reads=[(pfx + 'ktok', t), vpkey], writes=[('ps', b0), ('ps', b1)])
            for dc, bk in ((0, b0), (1, b1)):
                S.op('dve', lambda e, dc=dc, bk=bk: e.scalar_tensor_tensor(out=Cst[:, dc, :], in0=Cst[:, dc, :], scalar=DB[:, col:col + 1], in1=PS[bk][:, 0:257],
                                                                         op0=ALU.mult, op1=ALU.add), reads=[('ps', bk), 'DB', ckey], writes=[ckey])

        S.enabled = LVL >= 1
        AR.reset()
        xf = x[:, :, 0:1024]
        hf = hT[:, :, 0:1024]
        hidf = AR.alloc([NJ, 1024], BF16)
        KTs = [AR.alloc([8, 256], BF16) for _ in range(2)]
        VEs = [AR.alloc([8, 258], BF16)[:, :, 0:257] for _ in range(2)]
        bkvh = AR.alloc([512], F32)
        Cf = [AR.alloc([2, 257], F32) for _ in range(2)]
        Vpf = [AR.alloc([258], BF16)[:, 0:257] for _ in range(16)]
        for p_ in range(2):
            S.op('dve', lambda e, p_=p_: e.memset(VEs[p_][:, :, 256:257], 1.0), writes=[('f_vext', p_, t) for t in range(8)])
        for g6 in range(NFG):
            norm_mod(xf, hf, 2, 0, 1, for_mi, 'x', 'h')
            ffn(xf, hf, hidf, 2, f1w1, f1w2, 2, for_mi, 'x', 'h', 'hidf', hook=(lambda: mod_jobs.pop(0)() if mod_jobs else None))
            norm_mod(xf, hf, 2, 3, 4, for_mi, 'x', 'h')
            def kvproj(h, g6=g6):
                par = (g6 * 4 + h) % 2
                ktokf, vextf = KTs[par], VEs[par]
                S.dma('sp', bkvh[:, 0:256], bkrow_d[:, h * 256:(h + 1) * 256], 'd_bk', writes=[('bkvh', 0)])
                S.dma('sp', bkvh[:, 256:512], bvrow_d[:, h * 256:(h + 1) * 256], 'd_bv', writes=[('bkvh', 1)])
                S.op('dve', lambda e: e.tensor_scalar(out=bkvh[:, 0:256], in0=bkvh[:, 0:256], scalar1=1.0 / 16, scalar2=None, op0=ALU.mult),
                     reads=[('bkvh', 0)], writes=[('bkvh', 0)])
                s_, wp = load_panel([w_inv[:, :, K_OFF + h * 256:K_OFF + (h + 1) * 256], w_inv[:, :, V_OFF + h * 256:V_OFF + (h + 1) * 256]],
                                    "p (two k n) -> p two k n", two=2, k=8)
                for t in range(8):
                    b = psum()

                    def fp(e, b=b, t=t, wp=wp):
                        for kc in range(8):
                            ins = e.matmul(PS[b][:].rearrange("p (a c) -> p a c", a=2), hf[:, kc, t * 128:(t + 1) * 128], wp[:, :, kc, :],
                                           start=(kc == 0), stop=(kc == 7))
                        return ins
                    S.op('pe', fp, reads=WR(s_) + [('h', kc, t // 4) for kc in range(8)], writes=[('ps', b)])
                    S.op('dve', lambda e, b=b, t=t, ktokf=ktokf: e.scalar_tensor_tensor(out=ktokf[:, t, :], in0=PS[b][:, 0:256], scalar=1.0 / 16, in1=bkvh[:, 0:256],
                                                                                   op0=ALU.mult, op1=ALU.add), reads=[('ps', b), ('bkvh', 0)], writes=[('f_ktok', par, t)])
                    S.op('dve', lambda e, b=b, t=t, vextf=vextf: e.tensor_tensor(out=vextf[:, t, 0:256], in0=PS[b][:, 256:512], in1=bkvh[:, 256:512], op=ALU.add),
                         reads=[('ps', b), ('bkvh', 1)], writes=[('f_vext', par, t)])

            def kvagg(h, g6=g6):
                par = (g6 * 4 + h) % 2
                ktokf, vextf = KTs[par], VEs[par]
                banks = [psum() for _ in range(4)]
                for t in range(8):
                    for dr in range(2):
                        col = t * 8 + dr * 4 + h
                        vi = dr * 8 + t
                        if dr == 0:
                            S.op('act', lambda e, t=t, col=col, vi=vi, vextf=vextf: e.activation(out=Vpf[vi], in_=vextf[:, t, :], func=AF.Identity, scale=WT[:, 0, col:col + 1]),
                                 reads=[('f_vext', par, t), 'WT'], writes=[('Vpf', vi)])
                        else:
                            S.op('act', lambda e, t=t, col=col, vi=vi, vextf=vextf: e.activation(out=Vpf[vi], in_=vextf[:, t, :], func=AF.Identity, scale=WT[:, 0, col:col + 1]),
                                 reads=[('f_vext', par, t), 'WT'], writes=[('Vpf', vi)])

                    def fa(e, t=t, banks=banks, ktokf=ktokf):
                        for dr in range(2):
                            for dc in range(2):
                                ins = e.matmul(PS[banks[dr * 2 + dc]][:, 0:257], ktokf[:, t, dc * 128:(dc + 1) * 128], Vpf[dr * 8 + t], start=(t == 0), stop=(t == 7))
                        return ins
                    S.op('pe', fa, reads=[('f_ktok', par, t), ('Vpf', t), ('Vpf', 8 + t)], writes=[('ps', bk) for bk in banks])
                for dr in range(2):
                    for dc in range(2):
                        bk = banks[dr * 2 + dc]
                        if dc == 0:
                            S.op('act', lambda e, dr=dr, dc=dc, bk=bk: e.activation(out=Cf[dr][:, dc, :], in_=PS[bk][:, 0:257], func=AF.Copy), reads=[('ps', bk)], writes=[('Cf', dr)])
                        else:
                            S.op('dve', lambda e, dr=dr, dc=dc, bk=bk: e.tensor_copy(out=Cf[dr][:, dc, :], in_=PS[bk][:, 0:257]), reads=[('ps', bk)], writes=[('Cf', dr)])
                for ci, j in ((0, h), (1, 4 + h)):
                    S.dma('sp', aggC[j, g6].rearrange("dc p c -> p dc c"), Cf[ci], 'd_ag%d' % ci, reads=[('Cf', ci)], writes=[('aggC', j)])

            kvproj(0)
            kvproj(1)
            gates_front(hf, 'h', 8)
            if g6 + 1 < NFG:
                load_x(xfor[(g6 + 1) * 1024:(g6 + 2) * 1024, :], 8, xf, 'x')
            else:
                load_x(xin, NTILE, x[:], 'x')
            for (j0, order) in ((0, list(range(8))), (4, list(range(7, -1, -1)))):
                last = order[-1]
                S.op('dve', lambda e, last=last, j0=j0: e.tensor_copy(out=MMr[0:1, last * 8 + j0:last * 8 + j0 + 4], in_=BL[:, last * 8 + j0:last * 8 + j0 + 4]),
                     reads=['P0'], writes=['MMr'])
                for idx in range(6, -1, -1):
                    t, tn = order[idx], order[idx + 1]
                    S.op('dve', lambda e, t=t, tn=tn, j0=j0: e.tensor_tensor(out=MMr[0:1, t * 8 + j0:t * 8 + j0 + 4], in0=BL[:, t * 8 + j0:t * 8 + j0 + 4],
                                                                          in1=MMr[0:1, tn * 8 + j0:tn * 8 + j0 + 4], op=ALU.add), reads=['P0', 'MMr'], writes=['MMr'])
                first = order[0]
                S.op('dve', lambda e, first=first, j0=j0, g6=g6: e.tensor_copy(out=AGS[0:1, g6, 1, j0:j0 + 4], in_=MMr[0:1, first * 8 + j0:first * 8 + j0 + 4]),
                     reads=['MMr'], writes=['AGS'])
            S.op('dve', lambda e: e.tensor_tensor(out=RRr[0:1, 0:64], in0=MMr[0:1, 0:64], in1=RM[:, 0:64], op=ALU.add), reads=['MMr', 'P0'], writes=['RRr'])
            S.op('dve', lambda e, g6=g6: e.tensor_reduce(out=AGS[0:1, g6, 0, :], in_=RRr[0:1, 0:64].rearrange("o (t j) -> o j t", j=8), axis=AX.X, op=ALU.max),
                 reads=['RRr'], writes=['AGS'])
            S.op('dve', lambda e, g6=g6: e.tensor_tensor(out=DECr[0:1, 0:64].rearrange("o (t j) -> o t j", j=8), in0=MMr[0:1, 0:64].rearrange("o (t j) -> o t j", j=8),
                                                      in1=AGS[0:1, g6, 0:1, :].to_broadcast([1, 8, 8]), op=ALU.subtract), reads=['MMr', 'AGS'], writes=['DECr'])
            be = psum()
            S.op('pe', lambda e, be=be: e.matmul(PS[be][0:64, 0:1], DECr[0:1, 0:64], ident[0:1, 0:1], start=True, stop=True), reads=['DECr', 'ident'], writes=[('ps', be)])
            S.op('dve', lambda e, be=be: e.tensor_copy(out=COLS[0:64, 2:3], in_=PS[be][0:64, 0:1]), reads=[('ps', be)], writes=['NEGR'])
            S.op('act', lambda e: e.activation(out=ROW2[0:64, 0, :], in_=ROW[0:64, 0, :], func=AF.Exp, bias=COLS[0:64, 2:3], scale=1.0), reads=['ROW', 'NEGR'], writes=['ROW2'])
            bt = psum()
            S.op('pe', lambda e, bt=bt: e.transpose(PS[bt][:, 0:64], ROW2[0:64, 0, :], ident[0:64, 0:64]), reads=['ROW2', 'ident'], writes=[('ps', bt)])
            S.op('dve', lambda e, bt=bt: e.tensor_copy(out=WT[:, 0, 0:64], in_=PS[bt][:, 0:64]), reads=[('ps', bt)], writes=['WT'])
            kvagg(0)
            kvproj(2)
            kvagg(1)
            kvproj(3)
            kvagg(2)
            kvagg(3)
        S.barrier()
        S.enabled = True
        if LVL < 1:
            load_x(xin, NTILE, x[:], 'x')

        TA = rowtab[0:1, 0:96].rearrange("o (p q j) -> o p q j", p=4, q=3)
        TM = rowtab[0:1, 96:128].rearrange("o (p j) -> o p j", p=4)
        M0 = rowtab[0:1, 128:136]
        Bt = AGS[0:1, :, 1, :]
        LW = CROW[0:1, 0:32].rearrange("o (p j) -> o p j", p=4)
        TMP = CROW[0:1, 64:88].rearrange("o (q j) -> o q j", q=3)
        MIN = CROW[0:1, 120:128]
        def combine_pre():
            for p in range(4):
                S.op('dve', lambda e, p=p: e.tensor_tensor(out=TMP, in0=TA[:, p], in1=Bt, op=ALU.mult), reads=['rowtab', 'AGS'], writes=['TMP'])
                S.op('dve', lambda e, p=p: e.tensor_reduce(out=LW[:, p, :], in_=TMP.rearrange("o q j -> o j q"), axis=AX.X, op=ALU.add), reads=['TMP'], writes=['LW'])
            S.op('dve', lambda e: e.tensor_tensor(out=LW[:, 0:3, :], in0=LW[:, 0:3, :], in1=AGS[0:1, :, 0, :], op=ALU.add), reads=['LW', 'AGS'], writes=['LW'])
            S.op('dve', lambda e: e.tensor_tensor(out=LW[:, 3, :], in0=LW[:, 3, :], in1=M0, op=ALU.add), reads=['LW', 'rowtab'], writes=['LW'])
            S.op('dve', lambda e: e.tensor_tensor(out=LW, in0=LW, in1=TM, op=ALU.add), reads=['LW', 'rowtab'], writes=['LW'])
            S.op('dve', lambda e: e.tensor_reduce(out=MIN, in_=LW.rearrange("o p j -> o j p"), axis=AX.X, op=ALU.max), reads=['LW'], writes=['MIN'])
            S.op('dve', lambda e: e.tensor_tensor(out=LW, in0=LW, in1=MIN.rearrange("o (a j) -> o a j", a=1).to_broadcast([1, 4, 8]), op=ALU.subtract),
                 reads=['LW', 'MIN'], writes=['LW'])
            S.op('act', lambda e: e.activation(out=CROW[0:1, 0:32], in_=CROW[0:1, 0:32], func=AF.Exp), reads=['LW'], writes=['LW'])
            bcb = psum()
            S.op('pe', lambda e: e.matmul(PS[bcb][:, 0:32], onesf[0:1, :], CROW[0:1, 0:32], start=True, stop=True), reads=['onesf', 'LW'], writes=[('ps', bcb)])
            S.op('act', lambda e: e.activation(out=CB[:], in_=PS[bcb][:, 0:32], func=AF.Copy), reads=[('ps', bcb)], writes=['CB'])

        def combine_j(j, stg, c0s, acc):
            S.dma('sp', stg, aggC[j].rearrange("g dc p c -> p (g dc) c"), 'd_st0', reads=[('aggC', j)], writes=[('stg', 0)])
            S.dma('sp', c0s, c0ext_d[j].rearrange("dc p c -> p dc c"), 'd_c0s0', writes=[('c0s', 0)])
            S.op('dve', lambda e: e.tensor_scalar(out=acc, in0=c0s, scalar1=CB[:, 24 + j:25 + j], scalar2=None, op0=ALU.mult),
                 reads=[('c0s', 0), 'CB'], writes=[('acc', 0)])
            for p in range(NFG):
                S.op('dve', lambda e, p=p: e.scalar_tensor_tensor(out=acc, in0=stg[:, 2 * p:2 * p + 2, :], scalar=CB[:, p * 8 + j:p * 8 + j + 1],
                                                              in1=acc, op0=ALU.mult, op1=ALU.add), reads=[('stg', 0), 'CB', ('acc', 0)], writes=[('acc', 0)])
            S.dma('sp', cinS[j].rearrange("dc p c -> p dc c"), acc, 'd_ci0', reads=[('acc', 0)], writes=[('cinS', j)])

        S.enabled = LVL >= 2
        S.barrier()
        AR.reset()
        hid = AR.alloc([NJ, NT], BF16)
        stg1 = AR.alloc([NFG * 2, 257], F32)
        c0s1 = AR.alloc([2, 257], F32)
        acc1 = AR.alloc([2, 257], F32)
        jobs = [(lambda pi=pi: mod_panels([pi])) for pi in range(12, 18)] + [lambda: derive_tabs(((6, 7, 6, 8, 2, 0.5),), MODK)]
        jobs += [combine_pre] + [(lambda j=j: combine_j(j, stg1, c0s1, acc1)) for j in range(8)]
        norm_mod(x[:], hT[:], NG, 0, 1, own_mi, 'x', 'h')
        ffn(x[:], hT[:], hid, NG, f1w1, f1w2, 2, own_mi, 'x', 'h', 'hid', hook=lambda: jobs.pop(0)() if jobs else None)
        while jobs:
            jobs.pop(0)()
        out_toks = []
        XK = [('x', kc, g) for kc in range(8) for g in range(NG)]
        HK = [('h', kc, g) for kc in range(8) for g in range(NG)]
        if 'x1' in dbg:
            out_toks.append(S.dma('sp', dbg_out['x1'], x[:].rearrange("p k t -> p (k t)"), 'd_dbg1', reads=XK))
        if 'mod' in dbg:
            out_toks.append(S.dma('sp', dbg_out['mod'], modT[:].rearrange("p j m -> p (j m)"), 'd_dbg2', reads=MODK))

        S.enabled = LVL >= 3
        if 'hmix' in dbg:
            S.barrier()
        AR.reset()
        norm_mod(x[:], hT[:], NG, 3, 4, own_mi, 'x', 'h')
        if 'hmix' in dbg:
            tmpd = AR.alloc([8, NT], F32)
            S.op('dve', lambda e: e.tensor_copy(out=tmpd, in_=hT[:]), reads=HK, writes=['tmpd'])
            out_toks.append(S.dma('sp', dbg_out['hmix'], tmpd.rearrange("p k t -> p (k t)"), 'd_dbg3', reads=['tmpd']))
            S.barrier()
            AR.reset()
        gates_front(hT[:], 'h', 12)
        if 'cin' in dbg:
            S.barrier()
            out_toks.append(S.dma('sp', dbg_out['cin'], cinS, 'd_dbg4', reads=[('cinS', j) for j in range(8)]))
        chain_recur([0, 1], 0, 0.0, MFIN[0:1, 0, 0:4])
        chain_recur([1, 0], 4, 0.0, MFIN[0:1, 0, 4:8])
        chain_recur([2, 3], 0, 0.0, MFIN[0:1, 1, 0:4])
        chain_recur([3, 2], 4, 0.0, MFIN[0:1, 1, 4:8])
        chain_recur(list(range(4, 12)), 0, MIN[:, 0:4], MFIN[0:1, 2, 0:4])
        chain_recur(list(range(11, 3, -1)), 4, MIN[:, 4:8], MFIN[0:1, 2, 4:8])
        gates_finish(12)
        out_toks.append(S.dma('sp', outM, MFIN[0:1, 0:2, :].rearrange("o a b -> o (a b)"), 'd_om', reads=['MFIN']))
        if 'gates' in dbg:
            out_toks.append(S.dma('sp', dbg_out['gates'][:, 0:192], WT[:].rearrange("p a c -> p (a c)"), 'd_dbg5', reads=['WT']))
            out_toks.append(S.dma('sp', dbg_out['gates'][:, 192:288], DB[:], 'd_dbg6', reads=['DB']))
            out_toks.append(S.dma('sp', dbg_out['gates'][:, 288:480], GT[:], 'd_dbg7', reads=['GT']))
        if 'rows' in dbg:
            for i_, (src_, k_) in enumerate(((RM, 'P0'), (BL, 'P0'), (MMr[0:1, :], 'MMr'), (RRr[0:1, :], 'RRr'), (DECr[0:1, :], 'DECr'))):
                out_toks.append(S.dma('sp', dbg_out['rows'][:, i_ * 96:(i_ + 1) * 96], src_, 'd_dbg8', reads=[k_]))
            out_toks.append(S.dma('sp', dbg_out['rows'][:, 576:584], MIN, 'd_dbg8', reads=['MIN']))
            out_toks.append(S.dma('sp', dbg_out['rows'][:, 584:584 + NFG * 16], AGS[:].rearrange("o a b c -> o (a b c)"), 'd_dbg8', reads=['AGS']))

        S.enabled = LVL >= 4
        S.barrier()
        AR.reset()
        ohmT = AR.alloc([8, NT], BF16)
        qT = AR.alloc([2, NT], BF16)
        kT = AR.alloc([2, NT], BF16)
        ktok = AR.alloc([12, 256], BF16)
        vext = AR.alloc([12, 258], BF16)[:, :, 0:257]
        hm_off = AR.off
        hm = AR.alloc([12, 256], F32)
        Cst = [AR.alloc([2, 257], F32) for _ in range(4)]
        Dbf = [AR.alloc([2, 258], BF16)[:, :, 0:257] for _ in range(4)]
        Vp = [AR.alloc([258], BF16)[:, 0:257] for _ in range(4)]
        sTm = [AR.alloc([128], BF16) for _ in range(4)]
        hmn = AR.alloc([256], F32)
        vT = AR.h16[:, hm_off // 2:hm_off // 2 + 2 * NT].rearrange("p (a t) -> p a t", a=2)
        oT = qT
        S.op('dve', lambda e: e.memset(vext[:, :, 256:257], 1.0), writes=[('m_vext', t) for t in range(12)])
        xs0b = xstage[0].bitcast(BF16)
        xs1b = xstage[1].bitcast(BF16)
        t0b = tmpf[0].bitcast(BF16)
        t1b = tmpf[1].bitcast(BF16)
        STR = [xs0b[:, i * 128:(i + 1) * 128] for i in range(16)]
        VPR = [xs1b[:, i * 272:i * 272 + 257] for i in range(7)] + [t0b[:, i * 272:i * 272 + 257] for i in range(3)] + [t1b[:, i * 272:i * 272 + 257] for i in range(3)]

        def step_pre(t, j, Vp_, vpkey, sTm_, skey, qT_, kT_, vext_, mask):
            col = t * 8 + j
            ts_ = slice(t * 128, (t + 1) * 128)
            g = t // 4
            S.op('act', lambda e: e.activation(out=Vp_, in_=vext_[:, t, :], func=AF.Identity, scale=WT[:, 0, col:col + 1]),
                 reads=[('m_vext', t), 'WT'], writes=[vpkey])
            bs = pool_bank('S')

            def f(e, bs=bs):
                for dc in range(2):
                    ins = e.matmul(PS[bs][:, 0:128], kT_[:, dc, ts_], qT_[:, dc, ts_], start=(dc == 0), stop=(dc == 1))
                return ins
            S.op('pe', f, reads=[('m_kT', g), ('m_qT', g)], writes=[('ps', bs)])
            S.op('dve', lambda e, bs=bs: e.tensor_tensor(out=sTm_, in0=PS[bs][:, 0:128], in1=mask[:], op=ALU.mult),
                 reads=[('ps', bs), 'maskF', 'maskB'], writes=[skey])

        def step_post(t, j, Cst_, ckey, Vp_, vpkey, sTm_, skey, qT_, ktok_, Dbf_, dkey, hm_, first):
            col = t * 8 + j
            ts_ = slice(t * 128, (t + 1) * 128)
            g = t // 4
            S.op('act', lambda e: e.activation(out=Dbf_, in_=Cst_, func=AF.Identity, scale=DB[:, col:col + 1]), reads=[ckey, 'DB'], writes=[dkey])
            bn = pool_bank('A')
            bkv = pool_bank('B')

            def f3(e, bn=bn, bkv=bkv):
                e.matmul(PS[bkv][:, 0:256], ktok_[:, t, 0:128], Vp_[:, 0:256], start=True, stop=True)
                e.matmul(PS[bkv][:, 256:512], ktok_[:, t, 128:256], Vp_[:, 0:256], start=True, stop=True)
                e.matmul(PS[bn][:, 260:261], ktok_[:, t, 0:128], Vp_[:, 256:257], start=True, stop=True)
                return e.matmul(PS[bn][:, 261:262], ktok_[:, t, 128:256], Vp_[:, 256:257], start=True, stop=True)
            S.op('pe', f3, reads=[('m_ktok', t), vpkey], writes=[('ps', bkv), ('ps', bn), ('psn', bn)])
            def f2(e, bn=bn):
                e.matmul(PS[bn][:, 0:257], sTm_, Vp_, start=True, stop=False)
                e.matmul(PS[bn][:, 0:257], qT_[:, 0, ts_], Dbf_[:, 0, :], start=False, stop=False)
                return e.matmul(PS[bn][:, 0:257], qT_[:, 1, ts_], Dbf_[:, 1, :], start=False, stop=True)
            S.op('pe', f2, reads=[skey, vpkey, dkey, ('m_qT', g)], writes=[('ps', bn)])

            S.op('dve', lambda e, bkv=bkv: e.scalar_tensor_tensor(out=Cst_[:, :, 0:256], in0=Cst_[:, :, 0:256], scalar=DB[:, col:col + 1],
                                                               in1=PS[bkv][:].rearrange("p (a c) -> p a c", a=2), op0=ALU.mult, op1=ALU.add),
                 reads=[('ps', bkv), 'DB', ckey, dkey], writes=[ckey])
            S.op('dve', lambda e, bn=bn: e.scalar_tensor_tensor(out=Cst_[:, :, 256], in0=Cst_[:, :, 256], scalar=DB[:, col:col + 1],
                                                             in1=PS[bn][:, 260:262], op0=ALU.mult, op1=ALU.add),
                 reads=[('psn', bn), 'DB', ckey, dkey], writes=[ckey])
            return bn

        def step_post_b(t, j, bn, hm_, first):
            col = t * 8 + j
            dcol = small[:, 32 + (col % 16):33 + (col % 16)]
            dk = ('den', col % 16)
            if DEN_FAST:
                S.op('dve', lambda e: e.tensor_tensor(out=dcol, in0=PS[bn][:, 256:257], in1=WT[:, 1, col:col + 1], op=ALU.abs_max), reads=[('ps', bn), 'WT'], writes=[dk])
                if first:
                    S.op('dve', lambda e: e.tensor_scalar(out=hm_[:, t, :], in0=PS[bn][:, 0:256], scalar1=dcol, scalar2=None, op0=ALU.divide),
                         reads=[('ps', bn), dk], writes=[('m_hm', t)])
                else:
                    S.op('dve', lambda e: e.scalar_tensor_tensor(out=hm_[:, t, :], in0=PS[bn][:, 0:256], scalar=dcol, in1=hm_[:, t, :],
                                                              op0=ALU.divide, op1=ALU.add), reads=[('ps', bn), dk, ('m_hm', t)], writes=[('m_hm', t)])
            else:
                S.op('dve', lambda e: e.tensor_scalar(out=dcol, in0=PS[bn][:, 256:257], scalar1=-1.0, scalar2=WT[:, 1, col:col + 1],
                                                   op0=ALU.mult, op1=ALU.max), reads=[('ps', bn), 'WT'], writes=[dk])
                S.op('dve', lambda e: e.tensor_tensor(out=dcol, in0=dcol, in1=PS[bn][:, 256:257], op=ALU.max), reads=[('ps', bn), dk], writes=[dk])
                S.op('dve', lambda e: e.reciprocal(out=dcol, in_=dcol), reads=[dk], writes=[dk])
                if first:
                    S.op('dve', lambda e: e.tensor_scalar(out=hm_[:, t, :], in0=PS[bn][:, 0:256], scalar1=dcol, scalar2=None, op0=ALU.mult),
                         reads=[('ps', bn), dk], writes=[('m_hm', t)])
                else:
                    S.op('dve', lambda e: e.scalar_tensor_tensor(out=hm_[:, t, :], in0=PS[bn][:, 0:256], scalar=dcol, in1=hm_[:, t, :],
                                                              op0=ALU.mult, op1=ALU.add), reads=[('ps', bn), dk, ('m_hm', t)], writes=[('m_hm', t)])
        for h in range(4):
            s, wp = load_panel(w_inv[:, :, Q_OFF + h * 256:Q_OFF + (h + 1) * 256], "p (k n) -> p k n", k=8)

            def ev_q(blk, g, gs, b, h=h):
                c = _bin_col(Q_OFF) + 2 * h + blk
                S.op('dve', lambda e: e.tensor_scalar(out=qT[:, blk, gs], in0=PS[b][:], scalar1=vecs[:, c:c + 1], scalar2=None, op0=ALU.add),
                     reads=[('ps', b), 'vecs'], writes=[('m_qT', g)])
            evq_act = lambda: 'act'
            proj_fm(s, lambda kc, blk, wp=wp: wp[:, kc, blk * 128:(blk + 1) * 128], 2, hT[:], 'h', 8, NG, ev_q)
            kv_head(h, hT[:], 'h', NG, 12, kT, vT, ktok, vext, 'm_', alias=[('m_hm', t) for t in range(12)])
            for ci in range(2):
                S.dma('sp', Cst[ci], cinS[ci * 4 + h].rearrange("dc p c -> p dc c"), 'd_cl%d' % ci, reads=[('cinS', ci * 4 + h)], writes=[('Cst', ci)])
            events = []
            for seq in range(2):
                for ci in (2, 3):
                    events.append(('ms', ci))
                t0 = seq * 2
                for i in range(2):
                    events.append(('st', t0 + i, 0, 2))
                    events.append(('st', t0 + 1 - i, 1, 3))
                    for k_ in range(2):
                        si = seq * 4 + i * 2 + k_
                        events.append(('st', 4 + si, 0, 0))
                        events.append(('st', 11 - si, 1, 1))
                for dr, ci in ((0, 2), (1, 3)):
                    events.append(('out', seq, dr, ci))
            steps = [ev for ev in events if ev[0] == 'st']
            seen_t = set()
            firsts = []
            for ev in steps:
                firsts.append(ev[1] not in seen_t)
                seen_t.add(ev[1])
            LOOK = 12

            def pre(k):
                _, t, dr, ci = steps[k]
                step_pre(t, dr * 4 + h, VPR[k % len(VPR)], ('VPR', k % len(VPR)), STR[k % len(STR)], ('STR', k % len(STR)), qT, kT, vext,
                         maskF if dr == 0 else maskB)

            pend = []

            def post(k):
                _, t, dr, ci = steps[k]
                bn = step_post(t, dr * 4 + h, Cst[ci], ('Cst', ci), VPR[k % len(VPR)], ('VPR', k % len(VPR)), STR[k % len(STR)], ('STR', k % len(STR)),
                               qT, ktok, Dbf[ci], ('Dbf', ci), hm, firsts[k])
                while pend:
                    step_post_b(*pend.pop(0))
                pend.append((t, dr * 4 + h, bn, hm, firsts[k]))
            for k in range(min(LOOK, len(steps))):
                pre(k)
            kk = 0
            for ev in events:
                if ev[0] == 'ms':
                    S.op('dve', lambda e, ci=ev[1]: e.memset(Cst[ci], 0.0), writes=[('Cst', ev[1])])
                elif ev[0] == 'out':
                    _, seq, dr, ci = ev
                    out_toks.append(S.dma('sp', outC[seq, dr, h].rearrange("(dc p) e -> p dc e", p=128), Cst[ci][:, :, 0:256], 'd_oc%d' % ci,
                                          reads=[('Cst', ci)]))
                    out_toks.append(S.dma('sp', outN[seq, dr, h].rearrange("(dc p) -> p dc", p=128), Cst[ci][:, :, 256], 'd_on%d' % ci,
                                          reads=[('Cst', ci)], allow_slow_non_contiguous=True))
                else:
                    post(kk)
                    if kk + LOOK < len(steps):
                        pre(kk + LOOK)
                    kk += 1
            while pend:
                step_post_b(*pend.pop(0))
            if 'hmraw' in dbg:
                out_toks.append(S.dma('sp', dbg_out['hmraw'][h], hm.rearrange("p t e -> p (t e)"), 'd_dbg9', reads=[('m_hm', t) for t in range(12)]))
            HMN = [hmn, rstd[:, 0:256], rstd[:, 256:512]]
            ssq = small[:, 48:60]
            for t in range(12):
                hb = HMN[t % 3]
                S.op('act', lambda e, t=t, hb=hb: e.activation(out=hb, in_=hm[:, t, :], func=AF.Square, accum_out=small[:, 48 + t:49 + t]),
                     reads=[('m_hm', t)], writes=[('hmn', t % 3), ('hss', t)])
            S.op('act', lambda e: e.activation(out=ssq, in_=ssq, func=AF.Sqrt, bias=1e-6, scale=1.0 / 256), reads=[('hss', t) for t in range(12)], writes=['hssq'])
            S.op('dve', lambda e: e.reciprocal(out=ssq, in_=ssq), reads=['hssq'], writes=['hssq'])
            s, wp = load_panel(w_inv[:, :, O_OFF + h * 256:O_OFF + (h + 1) * 256], "p (k n) -> p k n", k=8)

            def ev_o(blk, g, gs, b, h=h):
                c = _bin_col(O_OFF) + 2 * h + blk
                S.op('act', lambda e: e.activation(out=oT[:, blk, gs], in_=PS[b][:], func=AF.Sigmoid, bias=vecs[:, c:c + 1], scale=1.0),
                     reads=[('ps', b), 'vecs'], writes=[('m_qT', g)])
            proj_fm(s, lambda kc, blk, wp=wp: wp[:, kc, blk * 128:(blk + 1) * 128], 2, hT[:], 'h', 8, NG, ev_o)
            for t in range(12):
                g = t // 4
                ts_ = slice(t * 128, (t + 1) * 128)
                hb = HMN[t % 3]
                S.op('act', lambda e, t=t, hb=hb: e.activation(out=hb, in_=hm[:, t, :], func=AF.Identity, scale=small[:, 48 + t:49 + t]),
                     reads=[('m_hm', t), 'hssq'], writes=[('hmn', t % 3)])
                b = psum()

                def f(e, b=b, hb=hb):
                    e.transpose(PS[b][:, 0:128], hb[:, 0:128], ident[:])
                    return e.transpose(PS[b][:, 128:256], hb[:, 128:256], ident[:])
                S.op('pe', f, reads=[('hmn', t % 3), 'ident'], writes=[('ps', b)])
                for blk in range(2):
                    S.op('dve', lambda e, b=b, blk=blk, ts_=ts_, h=h: e.scalar_tensor_tensor(
                        out=ohmT[:, 2 * h + blk, ts_], in0=PS[b][:, blk * 128:(blk + 1) * 128], scalar=vecs[:, V_MLN + 2 * h + blk:V_MLN + 2 * h + blk + 1],
                        in1=oT[:, blk, ts_], op0=ALU.mult, op1=ALU.mult), reads=[('ps', b), 'vecs', ('m_qT', g)], writes=[('ohm', 2 * h + blk, g)])
        OHK = [('ohm', kc, g) for kc in range(8) for g in range(NG)]
        if 'ohm' in dbg:
            S.barrier()
            tmpd = AR.alloc([8, NT], F32) if AR.off + 8 * NT * 4 <= AR.cap else None
            if tmpd is None:
                AR.off = 8 * NT * 2
                tmpd = AR.alloc([8, NT], F32)
            S.op('dve', lambda e: e.tensor_copy(out=tmpd, in_=ohmT), reads=OHK, writes=['tmpd'])
            out_toks.append(S.dma('sp', dbg_out['ohm'], tmpd.rearrange("p k t -> p (k t)"), 'd_dbg3', reads=['tmpd']))

        S.enabled = LVL >= 5
        S.barrier()
        AR.reset()
        ohmT = AR.alloc([8, NT], BF16)
        yT = AR.alloc([8, NT], BF16)
        gtmp = [AR.alloc([512], BF16) for _ in range(2)]
        w_mlv = w_ml.rearrange("(kc p) n -> p kc n", p=128)
        for jp in range(4):
            s, wp = load_panel([w_inv[:, :, GML_OFF + jp * 256:GML_OFF + (jp + 1) * 256], w_mlv[:, :, jp * 256:(jp + 1) * 256]],
                               "p (two k n) -> p two k n", two=2, k=8)
            for blk in range(2):
                j = jp * 2 + blk
                for g in range(NG):
                    gs = slice(g * 512, (g + 1) * 512)
                    ba = psum()
                    bb = psum()

                    def f(e, wp=wp, blk=blk, gs=gs, ba=ba, bb=bb):
                        for kc in range(8):
                            e.matmul(PS[ba][:], wp[:, 0, kc, blk * 128:(blk + 1) * 128], hT[:, kc, gs], start=(kc == 0), stop=(kc == 7))
                        for kc in range(8):
                            ins = e.matmul(PS[bb][:], wp[:, 1, kc, blk * 128:(blk + 1) * 128], ohmT[:, kc, gs], start=(kc == 0), stop=(kc == 7))
                        return ins
                    S.op('pe', f, reads=WR(s) + [('h', kc, g) for kc in range(8)] + [('ohm', kc, g) for kc in range(8)], writes=[('ps', ba), ('ps', bb)])
                    gi = g % 2
                    c = _bin_col(GML_OFF) + j
                    S.op('act', lambda e, ba=ba, gi=gi, c=c: e.activation(out=gtmp[gi], in_=PS[ba][:], func=AF.Sigmoid, bias=vecs[:, c:c + 1], scale=1.0),
                         reads=[('ps', ba), 'vecs'], writes=[('gtmp', gi)])
                    S.op('dve', lambda e, bb=bb, gi=gi, j=j, gs=gs: e.tensor_tensor(out=yT[:, j, gs], in0=PS[bb][:], in1=gtmp[gi], op=ALU.mult),
                         reads=[('ps', bb), ('gtmp', gi)], writes=[('y', j, g)])

        S.enabled = LVL >= 6
        S.barrier()
        AR.reset()
        buT = AR.alloc([8, NT], BF16)
        yT = AR.alloc([8, NT], BF16)
        cgt = AR.alloc([NT], F32)
        ut = AR.alloc([NT], F32)
        uct = AR.alloc([NT], F32)
        gtmp = [AR.alloc([512], BF16) for _ in range(2)]
        for jp in range(4):
            s1, wp1 = load_panel([w_inv[:, :, C_OFF + jp * 256:C_OFF + (jp + 1) * 256], w_inv[:, :, X_OFF + jp * 256:X_OFF + (jp + 1) * 256]],
                                 "p (two k n) -> p two k n", two=2, k=8)
            s2, wp2 = load_panel(w_inv[:, :, B_OFF + jp * 256:B_OFF + (jp + 1) * 256], "p (k n) -> p k n", k=8)
            for blk in range(2):
                j = jp * 2 + blk

                def ev_c(blk_, g, gs, b, j=j):
                    c = _bin_col(C_OFF) + j
                    S.op('act', lambda e: e.activation(out=cgt[:, gs], in_=PS[b][:], func=AF.Identity, bias=vecs[:, c:c + 1], scale=1.0),
                         reads=[('ps', b), 'vecs'], writes=[('cgt', g)])

                def ev_x(blk_, g, gs, b, j=j):
                    c = _bin_col(X_OFF) + j
                    S.op('dve', lambda e: e.scalar_tensor_tensor(out=ut[:, gs], in0=PS[b][:], scalar=vecs[:, c:c + 1], in1=cgt[:, gs], op0=ALU.add, op1=ALU.mult),
                         reads=[('ps', b), 'vecs', ('cgt', g)], writes=[('ut', g)])
                proj_fm(s1, lambda kc, blk_, blk=blk, wp1=wp1: wp1[:, 0, kc, blk * 128:(blk + 1) * 128], 1, hT[:], 'h', 8, NG, ev_c)
                proj_fm(s1, lambda kc, blk_, blk=blk, wp1=wp1: wp1[:, 1, kc, blk * 128:(blk + 1) * 128], 1, hT[:], 'h', 8, NG, ev_x)
                UK = [('ut', g) for g in range(NG)]
                CK = [('uct', g) for g in range(NG)]
                cw = lambda i, j=j: vecs[:, V_CW + i * 8 + j:V_CW + i * 8 + j + 1]
                cw0, cw1, cw2, cbj = cw(0), cw(1), cw(2), vecs[:, V_CB + j:V_CB + j + 1]
                S.op('act', lambda e, cw1=cw1, cbj=cbj: e.activation(out=uct, in_=ut, func=AF.Identity, scale=cw1, bias=cbj),
                     reads=UK + ['vecs'], writes=CK)
                for (lo, n, L) in ((0, 2, 256), (512, 16, 64)):
                    uv = ut[:, lo:lo + n * L].rearrange("p (r w) -> p r w", w=L)
                    cv = uct[:, lo:lo + n * L].rearrange("p (r w) -> p r w", w=L)
                    S.op('dve', lambda e, uv=uv, cv=cv, L=L, cw0=cw0: e.scalar_tensor_tensor(out=cv[:, :, 1:L], in0=uv[:, :, 0:L - 1], scalar=cw0, in1=cv[:, :, 1:L],
                                                                                 op0=ALU.mult, op1=ALU.add), reads=UK + CK + ['vecs'], writes=CK)
                    S.op('dve', lambda e, uv=uv, cv=cv, L=L, cw2=cw2: e.scalar_tensor_tensor(out=cv[:, :, 0:L - 1], in0=uv[:, :, 1:L], scalar=cw2, in1=cv[:, :, 0:L - 1],
                                                                                 op0=ALU.mult, op1=ALU.add), reads=UK + CK + ['vecs'], writes=CK)

                def ev_b(blk_, g, gs, b, j=j):
                    c = _bin_col(B_OFF) + j
                    S.op('dve', lambda e: e.scalar_tensor_tensor(out=buT[:, j, gs], in0=PS[b][:], scalar=vecs[:, c:c + 1], in1=uct[:, gs], op0=ALU.add, op1=ALU.mult),
                         reads=[('ps', b), 'vecs', ('uct', g)], writes=[('bu', j, g)])
                proj_fm(s2, lambda kc, blk_, blk=blk, wp2=wp2: wp2[:, kc, blk * 128:(blk + 1) * 128], 1, hT[:], 'h', 8, NG, ev_b)
        w_scv = w_sc.rearrange("(kc p) n -> p kc n", p=128)
        for jp in range(4):
            s, wp = load_panel([w_inv[:, :, GSC_OFF + jp * 256:GSC_OFF + (jp + 1) * 256], w_scv[:, :, jp * 256:(jp + 1) * 256]],
                               "p (two k n) -> p two k n", two=2, k=8)
            for blk in range(2):
                j = jp * 2 + blk
                for g in range(NG):
                    gs = slice(g * 512, (g + 1) * 512)
                    ba = psum()
                    bb = psum()

                    def f(e, wp=wp, blk=blk, gs=gs, ba=ba, bb=bb):
                        for kc in range(8):
                            e.matmul(PS[ba][:], wp[:, 0, kc, blk * 128:(blk + 1) * 128], hT[:, kc, gs], start=(kc == 0), stop=(kc == 7))
                        for kc in range(8):
                            ins = e.matmul(PS[bb][:], wp[:, 1, kc, blk * 128:(blk + 1) * 128], buT[:, kc, gs], start=(kc == 0), stop=(kc == 7))
                        return ins
                    S.op('pe', f, reads=WR(s) + [('h', kc, g) for kc in range(8)] + [('bu', kc, g) for kc in range(8)], writes=[('ps', ba), ('ps', bb)])
                    gi = g % 2
                    c = _bin_col(GSC_OFF) + j
                    S.op('act', lambda e, ba=ba, gi=gi, c=c: e.activation(out=gtmp[gi], in_=PS[ba][:], func=AF.Sigmoid, bias=vecs[:, c:c + 1], scale=1.0),
                         reads=[('ps', ba), 'vecs'], writes=[('gtmp', gi)])
                    S.op('dve', lambda e, bb=bb, gi=gi: e.tensor_tensor(out=tmpf[gi][:], in0=PS[bb][:], in1=gtmp[gi], op=ALU.mult),
                         reads=[('ps', bb), ('gtmp', gi)], writes=[('tmpf', gi)])
                    S.op('dve', lambda e, gi=gi, j=j, gs=gs: e.tensor_tensor(out=yT[:, j, gs], in0=yT[:, j, gs], in1=tmpf[gi][:], op=ALU.add),
                         reads=[('tmpf', gi), ('y', j, g)], writes=[('y', j, g)])
        if 'ymix' in dbg:
            out_toks.append(S.dma('pool', dbg_out['ymix'], yT.rearrange("p k t -> p (k t)"), 'd_dbgy', reads=[('y', j, g) for j in range(8) for g in range(NG)]))
        w_ov = w_o.rearrange("(kc p) n -> p kc n", p=128)
        for jp in range(4):
            s, wp = load_panel(w_ov[:, :, jp * 256:(jp + 1) * 256], "p (k n) -> p k n", k=8)

            def ev_wo(blk, g, gs, b, jp=jp):
                j = jp * 2 + blk
                S.op('dve', lambda e: e.scalar_tensor_tensor(out=x[:, j, gs], in0=PS[b][:], scalar=mod_ap(5, j, own_mi(g)), in1=x[:, j, gs], op0=ALU.mult, op1=ALU.add),
                     reads=[('ps', b), ('tabs', 5), ('x', j, g)], writes=[('x', j, g)])
            proj_fm(s, lambda kc, blk, wp=wp: wp[:, kc, blk * 128:(blk + 1) * 128], 2, yT, 'y', 8, NG, ev_wo)
        if 'x2' in dbg:
            out_toks.append(S.dma('sp', dbg_out['x2'], x[:].rearrange("p k t -> p (k t)"), 'd_dbg1', reads=XK))

        S.enabled = LVL >= 7
        S.barrier()
        AR.reset()
        hid = AR.alloc([NJ, NT], BF16)
        fnbc = AR.alloc([D], F32)
        S.dma('sp', fnbc, fnbc_d, 'd_c3', writes=['fnbc'])
        norm_mod(x[:], hT[:], NG, 6, 7, own_mi, 'x', 'h')
        ffn(x[:], hT[:], hid, NG, f2w1, f2w2, 8, own_mi, 'x', 'h', 'hid')
        if 'x3' in dbg:
            out_toks.append(S.dma('sp', dbg_out['x3'], x[:].rearrange("p k t -> p (k t)"), 'd_dbg1', reads=XK))

        S.enabled = True
        if LVL < 7:
            fnbc = AR.alloc([D], F32) if AR.off + 4096 <= AR.cap else None
            if fnbc is None:
                AR.reset()
                fnbc = AR.alloc([D], F32)
            S.barrier()
            S.dma('sp', fnbc, fnbc_d, 'd_c3', writes=['fnbc'])
        for t in range(NTILE):
            g = t // 4
            oi = t % 2
            banks = []
            for half in range(2):
                b = psum()
                banks.append(b)

                def f(e, half=half, b=b, t=t):
                    for i in range(4):
                        kc = half * 4 + i
                        ins = e.transpose(PS[b][:, i * 128:(i + 1) * 128], x[:, kc, t * 128:(t + 1) * 128], ident[:])
                    return ins
                S.op('pe', f, reads=[('x', kc, g) for kc in range(half * 4, half * 4 + 4)] + ['ident'], writes=[('ps', b)])
            for half in range(2):
                S.op('act', lambda e, half=half, b=banks[half], t=t: e.activation(
                    out=tmpf[0][:], in_=PS[b][:], func=AF.Square, accum_out=small[:, 2 * (t % 8) + half:2 * (t % 8) + half + 1]),
                    reads=[('ps', banks[half])], writes=[('tmpf', 0), ('small', t % 8, half)])
            c0 = 2 * (t % 8)
            rc = small[:, 16 + (t % 8):17 + (t % 8)]
            S.op('dve', lambda e, c0=c0, rc=rc: e.tensor_tensor(out=rc, in0=small[:, c0:c0 + 1], in1=small[:, c0 + 1:c0 + 2], op=ALU.add),
                 reads=[('small', t % 8, 0), ('small', t % 8, 1)], writes=[('small2', t % 8)])
            S.op('act', lambda e, rc=rc: e.activation(out=rc, in_=rc, func=AF.Sqrt, bias=1e-6, scale=1.0 / D), reads=[('small2', t % 8)], writes=[('small2', t % 8)])
            S.op('dve', lambda e, rc=rc: e.reciprocal(out=rc, in_=rc), reads=[('small2', t % 8)], writes=[('small2', t % 8)])
            for half in range(2):
                S.op('dve', lambda e, half=half, b=banks[half], rc=rc, oi=oi: e.scalar_tensor_tensor(
                    out=xstage[oi][:, half * 512:(half + 1) * 512], in0=PS[b][:], scalar=rc,
                    in1=fnbc[:, half * 512:(half + 1) * 512], op0=ALU.mult, op1=ALU.mult),
                    reads=[('ps', banks[half]), ('small2', t % 8), 'fnbc'], writes=[('xstage', oi)])
            out_toks.append(S.dma('sp', yout[t * 128:(t + 1) * 128, :], xstage[oi][:], 'd_o%d' % oi, reads=[('xstage', oi)]))
        S.wait_all('sp', out_toks)
        S.replay()
    return nc


def _host_inputs(inputs, core, consts):
    f = np.float32
    xp = inputs['x_prompt']
    xs = inputs['x_sample']
    b = core // 4
    r = core % 4
    xin = np.concatenate([xp[2 * core], xp[2 * core + 1], xs[b, r * 1024:(r + 1) * 1024]], axis=0)
    fsegs = [s_ for s_ in range(4) if s_ != r]
    pos = fsegs
    xfor = np.concatenate([xs[b, p * 1024:(p + 1) * 1024] for p in pos], axis=0)
    cond2 = np.stack([inputs['c_ctx'], inputs['c'][b]], axis=1)
    condT = cond2.reshape(8, 128, 2).transpose(1, 0, 2).reshape(128, 16)
    TA = np.zeros((4, 3, 8), f)
    TM = np.zeros((4, 8), f)
    for j in range(8):
        fwd = j < 4
        inc = [(pos[p] < r) if fwd else (pos[p] > r) for p in range(NFG)]
        for p in range(NFG):
            TM[p, j] = 0.0 if inc[p] else BIGNEG
            for q in range(NFG):
                if inc[q] and inc[p]:
                    if (fwd and pos[p] < pos[q]) or ((not fwd) and pos[q] < pos[p]):
                        TA[p, q, j] = 1.0
        for q in range(NFG):
            if inc[q]:
                TA[3, q, j] = 1.0
    rowtab = np.concatenate([TA.reshape(-1), TM.reshape(-1), inputs['state_m'][b, 0].reshape(-1)]).astype(f)[None, :]
    sC = inputs['state_C'][b, 0].reshape(8, 2, 128, 256)
    sn = inputs['state_n'][b, 0].reshape(8, 2, 128, 1)
    c0ext = np.concatenate([sC, sn], axis=-1)
    m = dict(consts)
    m.update(xin=np.ascontiguousarray(xin, f), xfor=np.ascontiguousarray(xfor, f), condT=np.ascontiguousarray(condT, f),
             rowtab=np.ascontiguousarray(rowtab, f), c0ext=np.ascontiguousarray(c0ext, f))
    return m


def _host_consts(inputs):
    f = np.float32
    vecs = np.zeros((128, NV), f)

    def blk(v):
        return v.reshape(-1, 128).T
    vecs[:, V_BMOD:V_BMOD + 72] = blk(inputs['b_mod'][0])
    vecs[:, V_NG:V_NG + 24] = blk(inputs['norm_g'][0].reshape(-1))
    b_in = inputs['b_in'][0]
    vecs[:, V_BIN:V_BIN + 32] = blk(b_in[0:4096])
    vecs[:, V_BIN + 32:V_BIN + 72] = blk(b_in[4112:])
    vecs[:, V_CW:V_CW + 24] = blk(inputs['conv_w'][0].reshape(-1))
    vecs[:, V_CB:V_CB + 8] = blk(inputs['conv_b'][0])
    vecs[:, V_MLN:V_MLN + 8] = blk(inputs['ml_norm'][0])
    ss, tt = np.meshgrid(np.arange(128), np.arange(128), indexing='ij')
    return dict(
        vecs=vecs, ident=np.eye(128, dtype=f),
        ntriF=-(ss <= tt).astype(f), ntriB=-(ss >= tt).astype(f),
        fnbc=np.ascontiguousarray(np.broadcast_to(inputs['final_norm'], (128, D)), f),
        gb12=np.ascontiguousarray(np.broadcast_to(np.tile(b_in[4096:4112], 12), (128, 192)), f),
        bkrow=np.ascontiguousarray(np.broadcast_to(b_in[K_OFF:K_OFF + 1024], (128, 1024)), f),
        bvrow=np.ascontiguousarray(np.broadcast_to(b_in[V_OFF:V_OFF + 1024], (128, 1024)), f),
        w_mod=inputs['w_mod'][0], ffn1_w1=inputs['ffn1_w1'][0], ffn1_w2=inputs['ffn1_w2'][0],
        ffn2_w1=inputs['ffn2_w1'][0], ffn2_w2=inputs['ffn2_w2'][0], w_in=inputs['w_in'][0],
        w_ml_out=inputs['w_ml_out'][0], w_sc_out=inputs['w_sc_out'][0], w_o=inputs['w_o'][0],
    )


_NC_CACHE = {}


def kernel(**inputs):
    inputs = {k: np.asarray(v) for k, v in inputs.items()}
    dbg = tuple(DEBUG.get('dump', ())) + (DEBUG.get('stop', float(os.environ.get('KLVL', '99'))),)
    if dbg not in _NC_CACHE:
        _NC_CACHE[dbg] = build_program(list(dbg[:-1]))
    nc = _NC_CACHE[dbg]
    consts = _host_consts(inputs)
    in_maps = [_host_inputs(inputs, c, consts) for c in range(8)]
    res = run_bass_kernel_spmd(nc, in_maps, core_ids=list(range(8)))
    R = res.results
    DEBUG['results'] = R
    y_prompt = np.zeros((16, 256, D), np.float32)
    y_sample = np.zeros((2, 4096, D), np.float32)
    nC = np.zeros((16, 1, 2, 4, 256, 256), np.float32)
    nn = np.zeros((16, 1, 2, 4, 256), np.float32)
    nm = np.zeros((16, 1, 2, 4), np.float32)
    for c in range(8):
        yo = R[c]['yout']
        y_prompt[2 * c] = yo[0:256]
        y_prompt[2 * c + 1] = yo[256:512]
        y_sample[c // 4, (c % 4) * 1024:(c % 4 + 1) * 1024] = yo[512:]
        nC[2 * c:2 * c + 2, 0] = R[c]['outC']
        nn[2 * c:2 * c + 2, 0] = R[c]['outN']
        nm[2 * c:2 * c + 2, 0] = R[c]['outM'].reshape(2, 2, 4)
    return (y_prompt, y_sample, nC, nn, nm)
```
